# Optimizing a Trainium2 kernel written in Bass

```python
import jax, jax.numpy as jnp
from jax import lax
import numpy as np

D_MODEL = 1024
BATCH = 4
SEQ = 4096
DEPTH = 2
DEC_BATCH = 32
DEC_SEQ = 64
PAST_LEN = 2048

CHUNK = 64
N_EVEN = (DEPTH + 1) // 2
N_ODD = DEPTH // 2
A_HEADS = 8
A_HEAD_DIM = 64
A_WIDTH = A_HEADS * A_HEAD_DIM
Q_BLOCK = 2 * CHUNK
B_WIDTH = D_MODEL // 2
B_BLOCKS = 8
B_BLOCK_DIM = B_WIDTH // B_BLOCKS
RG_CONV = 4
RG_C = 8.0
E_SPLITS = (A_WIDTH, 2 * A_WIDTH, 3 * A_WIDTH, 3 * A_WIDTH + A_HEADS, 3 * A_WIDTH + A_HEADS + B_WIDTH)
E_COLS = 3 * A_WIDTH + A_HEADS + 2 * B_WIDTH
C_WIDTH = D_MODEL
C_GROUPS = 8
C_GROUP_DIM = C_WIDTH // C_GROUPS
C_LEN = 2 * CHUNK
D_FF = 2816
FFN_CONV = 3
ALPHA = (2 * DEPTH) ** 0.25
BETA = (8 * DEPTH) ** -0.25
LN_EPS = 1e-5

kernel_name = 'fox_rglru_gmlp_convffn_stream_step'


def layer_norm(x, g, b):
    xf = x.astype(jnp.float32)
    mu = jnp.mean(xf, axis=-1, keepdims=True)
    var = jnp.mean(jnp.square(xf - mu), axis=-1, keepdims=True)
    return ((xf - mu) * lax.rsqrt(var + LN_EPS) * g + b).astype(x.dtype)


def causal_dwconv(x, state, w, b):
    width = w.shape[0]
    t = x.shape[1]
    xp = jnp.concatenate([state.astype(x.dtype), x], axis=1)
    y = b
    for j in range(width):
        y = y + w[j] * xp[:, j:j + t]
    return y, xp[:, -(width - 1):]


def fox_attend(q, k, v, cq, ck, q_pos, k_pos):
    s = jnp.einsum('bqhd,bkhd->bhqk', q, k).astype(jnp.float32) * (A_HEAD_DIM ** -0.5)
    s = s + (jnp.transpose(cq, (0, 2, 1))[..., :, None] - jnp.transpose(ck, (0, 2, 1))[..., None, :])
    mask = k_pos[None, :] <= q_pos[:, None]
    s = jnp.where(mask, s, -jnp.inf)
    p = jax.nn.softmax(s, axis=-1).astype(v.dtype)
    return jnp.einsum('bhqk,bkhd->bqhd', p, v)


def fox_prompt(q, k, v, c):
    bsz, t, h, d = q.shape
    pos = jnp.arange(t)

    def one_block(i):
        start = i * Q_BLOCK
        qb = lax.dynamic_slice_in_dim(q, start, Q_BLOCK, axis=1)
        cb = lax.dynamic_slice_in_dim(c, start, Q_BLOCK, axis=1)
        return fox_attend(qb, k, v, cb, c, start + jnp.arange(Q_BLOCK), pos)

    out = lax.map(one_block, jnp.arange(t // Q_BLOCK))
    return jnp.transpose(out, (1, 0, 2, 3, 4)).reshape(bsz, t, h, d)


def rg_lru(xb, conv_state, h0, conv_w, conv_b, wa, ba, wx, bx, lam):
    xc, conv_new = causal_dwconv(xb, conv_state, conv_w, conv_b)
    bsz, t, _ = xc.shape
    xg = xc.reshape(bsz, t, B_BLOCKS, B_BLOCK_DIM)
    gate_r = jax.nn.sigmoid(jnp.einsum('btnc,ncd->btnd', xg, wa).reshape(bsz, t, B_WIDTH) + ba)
    gate_i = jax.nn.sigmoid(jnp.einsum('btnc,ncd->btnd', xg, wx).reshape(bsz, t, B_WIDTH) + bx)
    log_a = -RG_C * gate_r.astype(jnp.float32) * jax.nn.softplus(-lam.astype(jnp.float32))
    a = jnp.exp(log_a)
    u = jnp.sqrt(-jnp.expm1(2.0 * log_a)) * (gate_i * xc).astype(jnp.float32)
    u = u.at[:, 0].add(a[:, 0] * h0.astype(jnp.float32))

    def combine(left, right):
        a1, b1 = left
        a2, b2 = right
        return a1 * a2, a2 * b1 + b2

    _, h = lax.associative_scan(combine, (a, u), axis=1)
    return h.astype(xb.dtype), conv_new, h[:, -1].astype(xb.dtype)


def even_mixer(x, past, w_in, b_f, rg_conv_w, rg_conv_b, rg_wa, rg_ba, rg_wx, rg_bx, rg_lam, w_out):
    bsz, t, _ = x.shape
    q, k, v, f, xb, gb = jnp.split(x @ w_in, list(E_SPLITS), axis=-1)
    q = q.reshape(bsz, t, A_HEADS, A_HEAD_DIM)
    k = k.reshape(bsz, t, A_HEADS, A_HEAD_DIM)
    v = v.reshape(bsz, t, A_HEADS, A_HEAD_DIM)
    logf = jax.nn.log_sigmoid((f + b_f).astype(jnp.float32))
    cum = jnp.cumsum(logf, axis=1)
    if past is None:
        attn = fox_prompt(q, k, v, cum)
        conv_s = jnp.zeros((bsz, RG_CONV - 1, B_WIDTH), x.dtype)
        h0 = jnp.zeros((bsz, B_WIDTH), jnp.float32)
    else:
        k_c, v_c, lf_c, conv_s, h0 = past
        plen = k_c.shape[1]
        lf_c = lf_c.astype(jnp.float32)
        rev = jnp.flip(jnp.cumsum(jnp.flip(lf_c, 1), axis=1), 1)
        ck = jnp.concatenate([lf_c - rev, cum], axis=1)
        k_all = jnp.concatenate([k_c.astype(k.dtype), k], axis=1)
        v_all = jnp.concatenate([v_c.astype(v.dtype), v], axis=1)
        attn = fox_attend(q, k_all, v_all, cum, ck, plen + jnp.arange(t), jnp.arange(plen + t))
    h, conv_new, h_last = rg_lru(xb, conv_s, h0, rg_conv_w, rg_conv_b, rg_wa, rg_ba, rg_wx, rg_bx, rg_lam)
    y = jnp.concatenate([attn.reshape(bsz, t, A_WIDTH), jax.nn.gelu(gb) * h], axis=-1) @ w_out
    return y, (k, v, logf, conv_new, h_last)


def odd_mixer(x, w_in, sgu_g, sgu_b, sgu_w, sgu_bias, w_out):
    bsz, t, _ = x.shape
    z = jax.nn.gelu(x @ w_in)
    u, v = jnp.split(z, 2, axis=-1)
    v = layer_norm(v, sgu_g, sgu_b)
    length = min(t, C_LEN)
    nc = t // length
    vg = v.reshape(bsz, nc, length, C_GROUPS, C_GROUP_DIM)
    ws = sgu_w[:, :length, :length] * jnp.tril(jnp.ones((length, length), sgu_w.dtype))
    mixed = jnp.einsum('gts,bnsgc->bntgc', ws, vg) + sgu_bias[:, :length].T[None, None, :, :, None]
    y = (u * mixed.reshape(bsz, t, C_WIDTH)) @ w_out
    return y, v


def conv_ffn(x, conv_state, w_up, conv_w, conv_b, w_down):
    hc, new_state = causal_dwconv(x @ w_up, conv_state, conv_w, conv_b)
    g, u = jnp.split(hc, 2, axis=-1)
    return (jax.nn.gelu(g) * u) @ w_down, new_state


def setup_inputs(seed: int = 0) -> dict:
    key = jax.random.key(seed)
    ks = jax.random.split(key, 40)
    nrm = lambda i, shape: jax.random.normal(ks[i], shape, jnp.float32)
    w_in_e = nrm(10, (N_EVEN, D_MODEL, E_COLS)) * D_MODEL ** -0.5
    w_in_e = w_in_e.at[..., 2 * A_WIDTH:3 * A_WIDTH].multiply(BETA)
    u_a = jax.random.uniform(ks[18], (N_EVEN, B_WIDTH), jnp.float32, minval=0.9, maxval=0.999)
    a_base = u_a ** (1.0 / RG_C)
    return {
        'x_prompt': nrm(0, (BATCH, SEQ, D_MODEL)),
        'x_sample': nrm(1, (DEC_BATCH, DEC_SEQ, D_MODEL)),
        'cache_k': nrm(2, (N_EVEN, DEC_BATCH, PAST_LEN, A_HEADS, A_HEAD_DIM)),
        'cache_v': nrm(3, (N_EVEN, DEC_BATCH, PAST_LEN, A_HEADS, A_HEAD_DIM)) * BETA,
        'cache_logf': jax.nn.log_sigmoid(2.0 + nrm(4, (N_EVEN, DEC_BATCH, PAST_LEN, A_HEADS))),
        'state_rglru_conv': nrm(5, (N_EVEN, DEC_BATCH, RG_CONV - 1, B_WIDTH)),
        'state_rglru_h': nrm(6, (N_EVEN, DEC_BATCH, B_WIDTH)) * 0.5,
        'state_ffn_conv': nrm(7, (DEPTH, DEC_BATCH, FFN_CONV - 1, 2 * D_FF)),
        'w_in_e': w_in_e,
        'b_f': 2.0 + 0.1 * nrm(11, (N_EVEN, A_HEADS)),
        'rg_conv_w': nrm(12, (N_EVEN, RG_CONV, B_WIDTH)) * RG_CONV ** -0.5,
        'rg_conv_b': 0.01 * nrm(13, (N_EVEN, B_WIDTH)),
        'rg_wa': nrm(14, (N_EVEN, B_BLOCKS, B_BLOCK_DIM, B_BLOCK_DIM)) * B_BLOCK_DIM ** -0.5,
        'rg_ba': 0.01 * nrm(15, (N_EVEN, B_WIDTH)),
        'rg_wx': nrm(16, (N_EVEN, B_BLOCKS, B_BLOCK_DIM, B_BLOCK_DIM)) * B_BLOCK_DIM ** -0.5,
        'rg_bx': 0.01 * nrm(17, (N_EVEN, B_WIDTH)),
        'rg_lam': jnp.log(a_base) - jnp.log1p(-a_base),
        'w_out_e': nrm(19, (N_EVEN, A_WIDTH + B_WIDTH, D_MODEL)) * (A_WIDTH + B_WIDTH) ** -0.5 * BETA,
        'w_in_o': nrm(20, (N_ODD, D_MODEL, 2 * C_WIDTH)) * D_MODEL ** -0.5,
        'sgu_g': 1.0 + 0.1 * nrm(21, (N_ODD, C_WIDTH)),
        'sgu_b': 0.01 * nrm(22, (N_ODD, C_WIDTH)),
        'sgu_w': nrm(23, (N_ODD, C_GROUPS, C_LEN, C_LEN)) * C_LEN ** -0.5,
        'sgu_bias': 1.0 + 0.1 * nrm(24, (N_ODD, C_GROUPS, C_LEN)),
        'w_out_o': nrm(25, (N_ODD, C_WIDTH, D_MODEL)) * C_WIDTH ** -0.5 * BETA,
        'ln_mix_g': 1.0 + 0.1 * nrm(26, (DEPTH, D_MODEL)),
        'ln_mix_b': 0.01 * nrm(27, (DEPTH, D_MODEL)),
        'ln_ffn_g': 1.0 + 0.1 * nrm(28, (DEPTH, D_MODEL)),
        'ln_ffn_b': 0.01 * nrm(29, (DEPTH, D_MODEL)),
        'ffn_w_up': nrm(30, (DEPTH, D_MODEL, 2 * D_FF)) * D_MODEL ** -0.5,
        'ffn_conv_w': nrm(31, (DEPTH, FFN_CONV, 2 * D_FF)) * FFN_CONV ** -0.5,
        'ffn_conv_b': 0.01 * nrm(32, (DEPTH, 2 * D_FF)),
        'ffn_w_down': nrm(33, (DEPTH, D_FF, D_MODEL)) * D_FF ** -0.5 * BETA,
    }


def reference(x_prompt, x_sample, cache_k, cache_v, cache_logf, state_rglru_conv, state_rglru_h,
              state_ffn_conv, w_in_e, b_f, rg_conv_w, rg_conv_b, rg_wa, rg_ba, rg_wx, rg_bx, rg_lam,
              w_out_e, w_in_o, sgu_g, sgu_b, sgu_w, sgu_bias, w_out_o, ln_mix_g, ln_mix_b,
              ln_ffn_g, ln_ffn_b, ffn_w_up, ffn_conv_w, ffn_conv_b, ffn_w_down):
    xp, xs = x_prompt, x_sample
    kp_l, vp_l, lfp_l, rcp_l, rhp_l = [], [], [], [], []
    ks_l, vs_l, lfs_l, rcs_l, rhs_l = [], [], [], [], []
    fcp_l, fcs_l, sgv_l = [], [], []
    for layer in range(DEPTH):
        if layer % 2 == 0:
            e = layer // 2
            prm = (w_in_e[e], b_f[e], rg_conv_w[e], rg_conv_b[e], rg_wa[e], rg_ba[e],
                   rg_wx[e], rg_bx[e], rg_lam[e], w_out_e[e])
            mp, (kp, vp, lfp, rcp, rhp) = even_mixer(xp, None, *prm)
            past = (cache_k[e], cache_v[e], cache_logf[e], state_rglru_conv[e], state_rglru_h[e])
            ms, (ks_, vs_, lfs, rcs, rhs) = even_mixer(xs, past, *prm)
            kp_l.append(kp); vp_l.append(vp); lfp_l.append(lfp); rcp_l.append(rcp); rhp_l.append(rhp)
            ks_l.append(ks_); vs_l.append(vs_); lfs_l.append(lfs); rcs_l.append(rcs); rhs_l.append(rhs)
        else:
            o = layer // 2
            prm = (w_in_o[o], sgu_g[o], sgu_b[o], sgu_w[o], sgu_bias[o], w_out_o[o])
            mp, _ = odd_mixer(xp, *prm)
            ms, sv = odd_mixer(xs, *prm)
            sgv_l.append(sv)
        xp = layer_norm(ALPHA * xp + mp, ln_mix_g[layer], ln_mix_b[layer])
        xs = layer_norm(ALPHA * xs + ms, ln_mix_g[layer], ln_mix_b[layer])
        fprm = (ffn_w_up[layer], ffn_conv_w[layer], ffn_conv_b[layer], ffn_w_down[layer])
        zero_fc = jnp.zeros((xp.shape[0], FFN_CONV - 1, 2 * D_FF), xp.dtype)
        fp, fcp = conv_ffn(xp, zero_fc, *fprm)
        fs, fcs = conv_ffn(xs, state_ffn_conv[layer], *fprm)
        fcp_l.append(fcp); fcs_l.append(fcs)
        xp = layer_norm(ALPHA * xp + fp, ln_ffn_g[layer], ln_ffn_b[layer])
        xs = layer_norm(ALPHA * xs + fs, ln_ffn_g[layer], ln_ffn_b[layer])
    return (xp, xs,
            jnp.stack(kp_l), jnp.stack(vp_l), jnp.stack(lfp_l),
            jnp.stack(ks_l), jnp.stack(vs_l), jnp.stack(lfs_l),
            jnp.stack(rcp_l), jnp.stack(rhp_l), jnp.stack(rcs_l), jnp.stack(rhs_l),
            jnp.stack(fcp_l), jnp.stack(fcs_l), jnp.stack(sgv_l))
```

```python
import numpy as np
from contextlib import ExitStack
import concourse.bass as bass
import concourse.mybir as mybir
from concourse.bass_utils import run_bass_kernel_spmd

F32 = mybir.dt.float32
BF16 = mybir.dt.bfloat16
AF = mybir.ActivationFunctionType
ALU = mybir.AluOpType

D = 1024
NCH = 8
DFF = 2816
NHC = 44
ALPHA = 4.0 ** 0.25
EPS = 1e-5
PENV = 30000.0
NLIGHT = 1792
NOWN = 2560
NWSLOT = 4
SAFE_SAME_ENGINE = True
DEBUG_NO_STORES = False
DEBUG_NO_GELU = False
DEBUG_SKIPKT = False
DEBUG_ACTPROBE = False
DEBUG_SKIPKO = False


class Sem:
    def __init__(self, name, dma=False):
        self.name = name
        self.dma = dma
        self.count = 0
        self.h = None


class Reg:
    __slots__ = ("w", "rd")

    def __init__(self):
        self.w = None
        self.rd = {}


class Prog:
    ENGS = ("pe", "act", "dve", "pool", "sp")

    def __init__(self):
        self.sem = {e: Sem("s_" + e) for e in self.ENGS}
        self.ops = {e: [] for e in self.ENGS}
        self.seen = {e: {} for e in self.ENGS}
        self.dsems = []

    def dsem(self, name):
        s = Sem(name, dma=True)
        self.dsems.append(s)
        return s

    def _waits(self, en, reads, writes):
        mysem = self.sem[en]
        need = {}

        def add(tok, skip_same):
            if tok is None:
                return
            s, v = tok
            if s is mysem and (en == "pe" or not SAFE_SAME_ENGINE):
                return
            if s.dma:
                v = s.count
            if need.get(s, 0) < v:
                need[s] = v

        for r in reads:
            add(r.w, False)
        for w in writes:
            add(w.w, True)
            for s, v in w.rd.items():
                add((s, v), True)
        out = []
        seen = self.seen[en]
        for s, v in need.items():
            if seen.get(s, 0) < v:
                seen[s] = v
                out.append((s, v))
        return out

    def _commit(self, tok, reads, writes):
        s, v = tok
        for r in reads:
            if r.rd.get(s, 0) < v:
                r.rd[s] = v
        for w in writes:
            w.w = tok
            w.rd = {}

    def op(self, en, fn, reads=(), writes=()):
        waits = self._waits(en, reads, writes)
        s = self.sem[en]
        s.count += 1
        self.ops[en].append((waits, fn, s, 1))
        self._commit((s, s.count), reads, writes)

    def dma(self, qn, fn, reads, writes, sem):
        waits = self._waits(qn, reads, writes)
        sem.count += 16
        self.ops[qn].append((waits, fn, sem, 16))
        self._commit((sem, sem.count), reads, writes)

    def final_wait(self, qn):
        waits = [(s, s.count) for s in self.dsems if s.count > 0]
        waits += [(self.sem[e], self.sem[e].count) for e in self.ENGS if e != qn and self.sem[e].count > 0]
        self.ops[qn].append((waits, None, None, 0))

    def emit(self, en, eh):
        for waits, fn, s, amt in self.ops[en]:
            for ws, v in waits:
                eh.wait_ge(ws.h, v)
            if fn is not None:
                ins = fn(eh)
                ins.then_inc(s.h, amt)


class Buf:
    def __init__(self, t, name=""):
        self.t = t
        self.name = name
        self.regs = {}
        self.sem = None

    def r(self, key=None):
        g = self.regs.get(key)
        if g is None:
            g = self.regs[key] = Reg()
        return g


class Slot(Buf):
    def __init__(self, f, idx):
        super().__init__(f, "slot%d" % idx)
        self.f = f
        self.b = f.bitcast(BF16)
        self.idx = idx


class Pool_:
    def __init__(self, items, name):
        self.free = list(items)
        self.name = name
        self.total = len(items)
        self.low = len(items)

    def alloc(self):
        if not self.free:
            raise RuntimeError("pool %s exhausted" % self.name)
        x = self.free.pop(0)
        self.low = min(self.low, len(self.free))
        return x

    def release(self, *xs):
        for x in xs:
            if isinstance(x, (list, tuple)):
                self.release(*x)
            else:
                assert x not in self.free
                self.free.append(x)


class _Stop(Exception):
    pass


def build_program(n_own_tiles=5, n_light_tiles=4, arena_slots=None, stop_stage=None):
    nc = bass.Bass("TRN2", target_bir_lowering=False)
    P = Prog()

    def din(name, shape):
        return nc.dram_tensor(name, list(shape), F32, kind="ExternalInput").ap()

    def dout(name, shape):
        return nc.dram_tensor(name, list(shape), F32, kind="ExternalOutput").ap()

    d_xl = din("xl", [NCH, 128, NLIGHT])
    d_xo = din("xo", [NCH, 128, NOWN])
    d_ckT = din("ckT", [4, 4, 128, 2048])
    d_cvr = din("cvr", [4, 4, 128, 16, 128])
    d_clf = din("clf", [4, 8, 2048])
    d_rcs = din("rcs", [128, 4, 4, 3])
    d_rh0 = din("rh0", [128, 4, 4])
    d_sfc = din("sfc", [2, 128, NHC, 4, 2])
    d_win0 = din("win0", [5, 128, 8, 512])
    d_wf = din("wf", [128, 8, 8])
    d_woa = din("woa", [2, 64, 4, 1024])
    d_wor = din("wor", [128, 4, 1024])
    d_wup = din("wup", [2, 11, 128, 8, 2, 256])
    d_wdn = din("wdn", [2, 8, 128, 22, 128])
    d_wio = din("wio", [4, 128, 8, 512])
    d_woo = din("woo", [2, 128, 8, 512])
    d_rgw = din("rgw", [2, 128, 4, 128])
    d_wst = din("wst", [128, 8, 128])
    d_wst64 = din("wst64", [128, 8, 64])
    d_sgb = din("sgb", [1, 8, 128])
    d_lnp = din("lnp", [128, 2, 2, 2, 8])
    d_fcw = din("fcw", [128, 2, NHC, 4])
    d_rgp = din("rgp", [128, 4, 8])
    d_bf = din("bfv", [8, 1])
    d_sgg = din("sgg", [1, 1024])
    d_sgbb = din("sgbb", [1, 1024])
    d_tri = din("tri", [128, 128])
    d_i8 = din("i8", [8, 8])
    d_flag = din("flag", [128, 1])
    d_pen = din("pen", [128, 1])

    o_y = dout("yT", [NCH, 128, NOWN])
    o_k = dout("koT", [4, 128, NOWN])
    o_v = dout("vo", [NOWN, 512])
    o_lf = dout("lfo", [8, NOWN])
    o_rc = dout("rco", [128, 4, 3])
    o_rh = dout("rho", [128, 4])
    o_rcs = dout("rcso", [128, 4, 4, 3])
    o_rhs = dout("rhso", [128, 4, 4])
    o_fcp = dout("fcpo", [2, 128, NHC, 2])
    o_fcs = dout("fcso", [2, 128, NHC, 4, 2])
    o_sgv = dout("sgvo", [256, 1024])

    es = ExitStack()

    def sb(name, shape, dt):
        return Buf(es.enter_context(nc.sbuf_tensor(name, list(shape), dt)), name)

    KT = sb("KT", [128, 4, 2048], BF16)
    KT2 = sb("KT2", [128, 4, 2048], BF16)
    VA = sb("VA", [128, 32, 8, 66], BF16)
    CK = sb("CK", [128, 32, 8], F32)
    VAS = [sb("VAS%d" % i, [128, 16, 2, 66], BF16) for i in range(1)]
    VNEW = sb("VNEW", [64, 4, 8, 66], BF16)
    XBH = sb("XBH", [128, 4, 3 + 512], F32)
    HCAR = sb("HCAR", [128, 4], F32)
    CUMC = sb("CUMC", [8, 1], F32)
    HC = sb("HC", [128, 2, NHC, 2], F32)
    HCS = sb("HCS", [128, 2, NHC, 4, 2], F32)
    SFC = sb("SFC", [128, 2, NHC, 4, 2], F32)
    HSL = sb("HSL", [128, 4, 4], F32)
    RH0 = sb("RH0", [128, 4, 4], F32)
    WF = sb("WF", [128, 8, 8], BF16)
    RGW = sb("RGW", [128, 2, 4, 128], BF16)
    WST = sb("WST", [128, 8, 128], BF16)
    WST64 = sb("WST64", [128, 8, 64], BF16)
    SGB1 = sb("SGB1", [1, 8, 128], F32)
    BHI = sb("BHI", [1, 8, 128], BF16)
    BLO = sb("BLO", [1, 8, 128], BF16)
    BTMP = sb("BTMP", [1, 8, 128], F32)
    SGG = sb("SGG", [128, 1024], F32)
    SGBB = sb("SGBB", [128, 1024], F32)
    LNP = sb("LNP", [128, 2, 2, 2, 8], F32)
    FCW = sb("FCW", [128, 2, NHC, 4], F32)
    RGP = sb("RGP", [128, 4, 8], F32)
    RGC = sb("RGC", [128, 4, 4], F32)
    NBF = sb("NBF", [8, 1], F32)
    TRI = sb("TRI", [128, 128], BF16)
    I8 = sb("I8", [8, 8], F32)
    FLAG = sb("FLAG", [128, 1], F32)
    PEN = sb("PEN", [128, 1], F32)
    ONESF = sb("ONESF", [128, 512], F32)
    ONESB = sb("ONESB", [128, 128], BF16)
    ONE1B = sb("ONE1B", [1, 128], BF16)
    CST = sb("CST", [128, 4], F32)
    WSL = [sb("WSL%d" % i, [128, 4096], BF16) for i in range(NWSLOT)]
    for i, w in enumerate(WSL):
        w.sem = P.dsem("wsl%d" % i)

    rem = nc.sbuf_bytes_remaining
    ns = arena_slots if arena_slots is not None else (rem - 1024) // 2048
    ARENA = es.enter_context(nc.sbuf_tensor("ARENA", [128, ns, 512], F32))
    print("arena slots", ns, "sbuf remaining", rem)
    SL = Pool_([Slot(ARENA[:, i, :], i) for i in range(ns)], "arena")
    for s in SL.free:
        s.sem = None
    PSB = []
    for i in range(8):
        t = es.enter_context(nc.psum_tensor("PS%d" % i, [128, 512], F32))
        PSB.append(Buf(t, "ps%d" % i))
    PS = Pool_(PSB, "psum")

    misc_sem = P.dsem("misc_ld")
    misc_sw = P.dsem("misc_sw")
    out_sem = [P.dsem("out%d" % i) for i in range(4)]
    _oc = [0]

    def osem():
        _oc[0] += 1
        return out_sem[_oc[0] % len(out_sem)]

    def slot_sem(s, sw=False):
        if sw:
            if getattr(s, "sem_sw", None) is None:
                s.sem_sw = P.dsem("sw%d" % s.idx)
            return s.sem_sw
        if s.sem is None:
            s.sem = P.dsem("sl%d" % s.idx)
        return s.sem

    def act(out, in_, func, reads, writes, bias=None, scale=None):
        if DEBUG_NO_GELU and func == AF.Gelu_apprx_tanh:
            func = AF.Copy
        kw = {}
        if bias is not None:
            kw["bias"] = bias
        if scale is not None:
            kw["scale"] = scale
        P.op("act", lambda e: e.activation(out=out, in_=in_, func=func, **kw), reads, writes)

    def tt(out, in0, in1, op, reads, writes, en="dve"):
        P.op(en, lambda e: e.tensor_tensor(out=out, in0=in0, in1=in1, op=op), reads, writes)

    def ts(out, in0, s1, s2, op0, op1, reads, writes, en="dve"):
        if s2 is None:
            P.op(en, lambda e: e.tensor_scalar(out=out, in0=in0, scalar1=s1, scalar2=None, op0=op0), reads, writes)
        else:
            P.op(en, lambda e: e.tensor_scalar(out=out, in0=in0, scalar1=s1, scalar2=s2, op0=op0, op1=op1),
                 reads, writes)

    def stt(out, in0, scalar, in1, op0, op1, reads, writes):
        P.op("dve", lambda e: e.scalar_tensor_tensor(out=out, in0=in0, scalar=scalar, in1=in1, op0=op0, op1=op1),
             reads, writes)

    def cp(out, in_, reads, writes, en="dve"):
        P.op(en, lambda e: e.tensor_copy(out=out, in_=in_), reads, writes)

    def mm(out, lhsT, rhs, reads, writes, start=True, stop=True):
        P.op("pe", lambda e: e.matmul(out, lhsT, rhs, start=start, stop=stop), reads, writes)

    def mmg(out, pairs, reads, writes):
        n = len(pairs)

        def fn(e):
            ins = None
            for i, (l, r) in enumerate(pairs):
                ins = e.matmul(out, l, r, start=(i == 0), stop=(i == n - 1))
            return ins

        P.op("pe", fn, reads, writes)

    def ld(q, out, in_, writes, sem, reads=()):
        P.dma(q, lambda e: e.dma_start(out=out, in_=in_), list(reads), list(writes), sem)

    def st(out, in_, reads, sem):
        if DEBUG_NO_STORES:
            return
        P.dma("sp", lambda e: e.dma_start(out=out, in_=in_), list(reads), [], sem)

    def ldm(buf, src, q="sp"):
        ld(q, buf.t[:], src, [buf.r()], misc_sw if q == "pool" else misc_sem)

    ldm(LNP, d_lnp)
    ldm(FCW, d_fcw)
    ldm(RGP, d_rgp)
    ldm(NBF, d_bf)
    ldm(I8, d_i8)
    ldm(FLAG, d_flag)
    ldm(PEN, d_pen)
    for l_ in range(2):
        ld("sp", SFC.t[:, l_, :, :, :], d_sfc[l_], [SFC.r()], misc_sem)
    ldm(RH0, d_rh0)
    ldm(WST, d_wst, q="pool")
    ldm(WST64, d_wst64, q="pool")
    ldm(SGB1, d_sgb)
    ldm(SGG, d_sgg.partition_broadcast(128))
    ldm(SGBB, d_sgbb.partition_broadcast(128))
    ldm(WF, d_wf, q="pool")
    for w_ in range(2):
        ld("pool", RGW.t[:, w_, :, :], d_rgw[w_], [RGW.r()], misc_sw)
    ldm(TRI, d_tri, q="pool")

    P.op("dve", lambda e: e.memset(ONESF.t[:], 1.0), [], [ONESF.r()])
    P.op("dve", lambda e: e.memset(ONESB.t[:], 1.0 / 1024.0), [], [ONESB.r()])
    P.op("dve", lambda e: e.memset(ONE1B.t[:], 1.0), [], [ONE1B.r()])
    P.op("dve", lambda e: e.memset(CST.t[:, 0:1], 1.0), [], [CST.r()])
    P.op("dve", lambda e: e.memset(CST.t[:, 1:2], EPS), [], [CST.r()])
    P.op("dve", lambda e: e.memset(CST.t[:, 2:3], 0.0), [], [CST.r()])
    P.op("dve", lambda e: e.memset(VA.t[:, :, :, 64:66], 1.0), [], [VA.r("ones")])
    for i in range(1):
        P.op("dve", lambda e, i=i: e.memset(VAS[i].t[:, :, :, 64:66], 1.0), [], [VAS[i].r("ones")])
    P.op("dve", lambda e: e.memset(VNEW.t[:, :, :, 64:66], 1.0), [], [VNEW.r("ones")])
    P.op("dve", lambda e: e.memset(XBH.t[:, :, 0:3], 0.0), [], [XBH.r()])
    P.op("dve", lambda e: e.memset(HCAR.t[:], 0.0), [], [HCAR.r()])
    P.op("dve", lambda e: e.memset(CUMC.t[:], 0.0), [], [CUMC.r()])
    P.op("dve", lambda e: e.memset(HC.t[:], 0.0), [], [HC.r(0), HC.r(1)])
    ONE = CST.t[:, 0:1]
    EPSC = CST.t[:, 1:2]
    ts(NBF.t[:], NBF.t[:], -1.0, None, ALU.mult, None, [NBF.r()], [NBF.r()])
    for g in range(8):
        tt(WST.t[:, g, :], WST.t[:, g, :], TRI.t[:, :], ALU.mult, [WST.r(), TRI.r()], [WST.r()])
        tt(WST64.t[0:64, g, :], WST64.t[0:64, g, :], TRI.t[0:64, 0:64], ALU.mult, [WST64.r(), TRI.r()],
           [WST64.r()])
        tt(WST64.t[64:128, g, :], WST64.t[64:128, g, :], TRI.t[64:128, 64:128], ALU.mult, [WST64.r(), TRI.r()],
           [WST64.r()])
    cp(BHI.t[:], SGB1.t[:], [SGB1.r()], [BHI.r()])
    tt(BTMP.t[:], SGB1.t[:], BHI.t[:], ALU.subtract, [SGB1.r(), BHI.r()], [BTMP.r()])
    cp(BLO.t[:], BTMP.t[:], [BTMP.r()], [BLO.r()])
    act(RGC.t[:, :, 2], RGP.t[:, :, 7], AF.Exp, [RGP.r()], [RGC.r()], scale=-1.0)
    act(RGC.t[:, :, 3], RGC.t[:, :, 2], AF.Ln, [RGC.r(), CST.r()], [RGC.r()], bias=ONE)
    ts(RGC.t[:, :, 0], RGC.t[:, :, 3], -8.0, None, ALU.mult, None, [RGC.r()], [RGC.r()])
    ts(RGC.t[:, :, 1], RGC.t[:, :, 3], -16.0, None, ALU.mult, None, [RGC.r()], [RGC.r()])
    ts(RGC.t[:, :, 2], RGP.t[:, :, 5], -1.0, None, ALU.mult, None, [RGP.r(), RGC.r()], [RGC.r()])
    ts(RGC.t[:, :, 3], RGP.t[:, :, 6], -1.0, None, ALU.mult, None, [RGP.r(), RGC.r()], [RGC.r()])

    blocks = []

    def v_k512(t):
        return t[:, 0:4096].rearrange("p (k n) -> p k n", k=8)

    def addblk(name, src, view):
        blocks.append((name, src, view))

    for li in range(n_light_tiles):
        for nm, bi in (("K", 1), ("V", 2), ("XB", 3)):
            addblk("L%d_%s" % (li, nm), d_win0[bi], v_k512)
    for ti in range(n_own_tiles):
        for nm, bi in (("K", 1), ("V", 2), ("Q", 0), ("GB", 4), ("XB", 3)):
            addblk("O%d_%s" % (ti, nm), d_win0[bi], v_k512)
        for i in range(2):
            addblk("O%d_WOA%d" % (ti, i), d_woa[i],
                   lambda t: t[0:64, 0:4096].rearrange("p (h n) -> p h n", h=4))
        addblk("O%d_WOR" % ti, d_wor, lambda t: t[:, 0:4096].rearrange("p (c n) -> p c n", c=4))
        for l in range(2):
            if l == 1:
                for i in range(4):
                    addblk("O%d_WIO%d" % (ti, i), d_wio[i], v_k512)
                for i in range(2):
                    addblk("O%d_WOO%d" % (ti, i), d_woo[i], v_k512)
            for g in range(11):
                addblk("O%d_UP%d_%d" % (ti, l, g), d_wup[l, g],
                       lambda t: t[:, 0:4096].rearrange("p (k u n) -> p k u n", k=8, u=2))
            for o in range(8):
                addblk("O%d_DN%d_%d" % (ti, l, o), d_wdn[l, o],
                       lambda t: t[:, 0:2816].rearrange("p (c n) -> p c n", c=22))

    class WS:
        nload = 0
        done = set()
        cur = {}

    def ws_pump():
        while WS.nload < len(blocks) and (WS.nload < NWSLOT or (WS.nload - NWSLOT) in WS.done):
            i = WS.nload
            name, src, view = blocks[i]
            slot = WSL[i % NWSLOT]
            ld("pool", view(slot.t), src, [slot.r()], slot.sem)
            WS.nload += 1

    def ws_get(name):
        ws_pump()
        for i in range(len(blocks)):
            if blocks[i][0] == name:
                break
        else:
            raise KeyError(name)
        assert i < WS.nload, "weight block %s not prefetched (ring too small)" % name
        slot = WSL[i % NWSLOT]
        WS.cur[name] = i
        return blocks[i][2](slot.t), slot.r()

    def ws_done(name):
        WS.done.add(WS.cur.pop(name))
        ws_pump()

    def xbf_ap(xbf, kc, a=0, b=512):
        return xbf[kc // 2].b[:, (kc % 2) * 512 + a:(kc % 2) * 512 + b]

    def load_xbf(src, c0, W):
        xbf = [SL.alloc() for _ in range(4)]
        for s in range(4):
            dst = xbf[s].b[:, 0:1024].rearrange("p (c t) -> p c t", c=2)[:, :, 0:W]
            ld("pool", dst, src[2 * s:2 * s + 2, :, c0:c0 + W].rearrange("c p t -> p c t"), [xbf[s].r()],
               slot_sem(xbf[s], True))
        return xbf

    def xr(xbf):
        return [s.r() for s in xbf]

    def forget_gate(xbf, W, segs, out_c0, lf_out=True):
        ps = PS.alloc()
        mmg(ps.t[0:8, 0:W], [(WF.t[:, kc, :], xbf_ap(xbf, kc, 0, W)) for kc in range(8)], xr(xbf) + [WF.r()],
            [ps.r()])
        t1 = SL.alloc()
        act(t1.f[0:8, 0:W], ps.t[0:8, 0:W], AF.Exp, [ps.r(), NBF.r()], [t1.r()], bias=NBF.t[:, 0:1], scale=-1.0)
        PS.release(ps)
        t2 = SL.alloc()
        act(t2.f[0:8, 0:W], t1.f[0:8, 0:W], AF.Ln, [t1.r(), CST.r()], [t2.r()], bias=CST.t[0:8, 0:1])
        lf = t1
        ts(lf.f[0:8, 0:W], t2.f[0:8, 0:W], -1.0, None, ALU.mult, None, [t2.r()], [lf.r()])
        if lf_out:
            st(o_lf[:, out_c0:out_c0 + W], lf.f[0:8, 0:W], [lf.r()], slot_sem(lf))
        cum = t2
        for kind, a, b in segs:
            if kind == "samp":
                for bb in range(4):
                    P.op("dve", lambda e, a=a, bb=bb: e.tensor_tensor_scan(
                        out=cum.f[0:8, a + bb * 64:a + bb * 64 + 64], data0=ONESF.t[0:8, 0:64],
                        data1=lf.f[0:8, a + bb * 64:a + bb * 64 + 64], initial=0.0, op0=ALU.mult, op1=ALU.add),
                        [lf.r(), ONESF.r()], [cum.r()])
            else:
                P.op("dve", lambda e, a=a, b=b: e.tensor_tensor_scan(
                    out=cum.f[0:8, a:b], data0=ONESF.t[0:8, 0:b - a], data1=lf.f[0:8, a:b],
                    initial=CUMC.t[0:8, 0:1], op0=ALU.mult, op1=ALU.add),
                    [lf.r(), ONESF.r(), CUMC.r()], [cum.r()])
                cp(CUMC.t[0:8, 0:1], cum.f[0:8, b - 1:b], [cum.r()], [CUMC.r()])
        return lf, cum

    def ck_from_cum(cum, a, b, jt0, add_pen):
        n = (b - a) // 128
        ps = PS.alloc()
        for i in range(n):
            mm(ps.t[:, i * 8:(i + 1) * 8], cum.f[0:8, a + i * 128:a + (i + 1) * 128], I8.t[:, :],
               [cum.r(), I8.r()], [ps.r()])
        dst = CK.t[:, jt0:jt0 + n, :]
        src = ps.t[:, 0:n * 8].rearrange("p (j h) -> p j h", h=8)
        regs = [CK.r(jt0 + i) for i in range(n)]
        if add_pen:
            ts(dst, src, PEN.t[:, 0:1], None, ALU.add, None, [ps.r(), PEN.r()], regs)
        else:
            cp(dst, src, [ps.r()], regs)
        PS.release(ps)

    def rglru_seg(kind, a, b, src_of, gb, hg, xbs=None):
        Wd = b - a
        samp = kind == "samp"

        def v2(ap):
            return ap.rearrange("p (b t) -> p b t", b=4) if samp else ap

        for grp_ in range(2):
            cs_ = (2 * grp_, 2 * grp_ + 1)
            xc, xcb, gr, gi, a2 = {}, {}, {}, {}, {}
            for c in cs_:
                hist, hreg = src_of(c)

                def tap(j):
                    return hist[:, :, j:j + 64] if samp else hist[:, j:j + Wd]

                t = SL.alloc()
                xc[c] = t
                o = v2(t.f[:, 0:Wd])
                act(o, tap(3), AF.Identity, [hreg, RGP.r()], [t.r()], bias=RGP.t[:, c, 4:5], scale=RGP.t[:, c, 3:4])
                for j in range(3):
                    stt(o, tap(j), RGP.t[:, c, j:j + 1], o, ALU.mult, ALU.add, [hreg, RGP.r(), t.r()], [t.r()])
                tb = SL.alloc()
                xcb[c] = tb
                act(tb.b[:, 0:Wd], t.f[:, 0:Wd], AF.Copy, [t.r()], [tb.r()])
            for c in cs_:
                for which, lst, bcol in ((0, gr, 5), (1, gi, 6)):
                    ps = PS.alloc()
                    mm(ps.t[:, 0:Wd], RGW.t[:, which, c, :], xcb[c].b[:, 0:Wd], [RGW.r(), xcb[c].r()], [ps.r()])
                    g = SL.alloc()
                    lst[c] = g
                    act(g.f[:, 0:Wd], ps.t[:, 0:Wd], AF.Exp, [ps.r(), RGC.r()], [g.r()], bias=RGC.t[:, c, 2 + which:3 + which],
                        scale=-1.0)
                    PS.release(ps)
                    ts(g.f[:, 0:Wd], g.f[:, 0:Wd], 1.0, None, ALU.add, None, [g.r()], [g.r()])
                    P.op("dve", lambda e, o_=g.f[:, 0:Wd]: e.reciprocal(out=o_, in_=o_), [g.r()], [g.r()])
                SL.release(xcb[c])
            if kind == 'halo' and grp_ == 0:
                stage(22)
            for c in cs_:
                t = SL.alloc()
                a2[c] = t
                act(t.f[:, 0:Wd], gr[c].f[:, 0:Wd], AF.Exp, [gr[c].r(), RGC.r()], [t.r()], scale=RGC.t[:, c, 1:2])
                act(gr[c].f[:, 0:Wd], gr[c].f[:, 0:Wd], AF.Exp, [gr[c].r(), RGC.r()], [gr[c].r()], scale=RGC.t[:, c, 0:1])
            for c in cs_:
                act(a2[c].f[:, 0:Wd], a2[c].f[:, 0:Wd], AF.Ln, [a2[c].r(), CST.r()], [a2[c].r()], bias=ONE, scale=-1.0)
                act(a2[c].f[:, 0:Wd], a2[c].f[:, 0:Wd], AF.Exp, [a2[c].r()], [a2[c].r()], scale=0.5)
            if kind == 'halo' and grp_ == 0:
                stage(23)
            for c in cs_:
                tt(gi[c].f[:, 0:Wd], gi[c].f[:, 0:Wd], xc[c].f[:, 0:Wd], ALU.mult, [gi[c].r(), xc[c].r()], [gi[c].r()])
                tt(gi[c].f[:, 0:Wd], gi[c].f[:, 0:Wd], a2[c].f[:, 0:Wd], ALU.mult, [gi[c].r(), a2[c].r()], [gi[c].r()])
                h = xc[c]
                if samp:
                    for bb in range(4):
                        P.op("dve", lambda e, o_=h.f[:, bb * 64:bb * 64 + 64], d0=gr[c].f[:, bb * 64:bb * 64 + 64],
                             d1=gi[c].f[:, bb * 64:bb * 64 + 64], i_=RH0.t[:, c, bb:bb + 1]: e.tensor_tensor_scan(
                            out=o_, data0=d0, data1=d1, initial=i_,
                            op0=ALU.mult, op1=ALU.add), [gr[c].r(), gi[c].r(), RH0.r()], [h.r()])
                    cp(HSL.t[:, c, :], h.f[:, 0:256].rearrange("p (b t) -> p b t", b=4)[:, :, 63], [h.r()], [HSL.r()])
                else:
                    P.op("dve", lambda e, o_=h.f[:, 0:Wd], d0=gr[c].f[:, 0:Wd], d1=gi[c].f[:, 0:Wd],
                         i_=HCAR.t[:, c:c + 1]: e.tensor_tensor_scan(
                        out=o_, data0=d0, data1=d1, initial=i_, op0=ALU.mult, op1=ALU.add),
                        [gr[c].r(), gi[c].r(), HCAR.r()], [h.r()])
                    cp(HCAR.t[:, c:c + 1], h.f[:, Wd - 1:Wd], [h.r()], [HCAR.r()])
                if hg is not None:
                    tt(hg[c // 2].b[:, (c % 2) * 512 + a:(c % 2) * 512 + b], h.f[:, 0:Wd], gb[c].f[:, a:b], ALU.mult,
                       [h.r(), gb[c].r()], [hg[c // 2].r()])
                SL.release(xc[c], gr[c], gi[c], a2[c])
            if kind == 'halo' and grp_ == 0:
                stage(24)


    def xbh_src(a):
        def f(c):
            return XBH.t[:, c, a:a + 3 + 512], XBH.r()
        return f

    def attend(q_ap_of, Wq, qblocks, keys, att_dst, att_reg):
        acc = PS.alloc()
        n = len(keys)
        LOOK = 2
        sts = {}

        def qk(i):
            k = keys[i]
            sps = PS.alloc()
            cs = k["cstart"]
            qap, qreg = q_ap_of(cs, Wq)
            mm(sps.t[0:k["nk"], cs:Wq], k["lhsT_k"], qap, [k["k_reg"], qreg], [sps.r()])
            sts[i] = sps

        for i in range(min(LOOK, n)):
            qk(i)
        for i in range(n):
            if i + LOOK < n:
                qk(i + LOOK)
            k = keys[i]
            nk, cs = k["nk"], k["cstart"]
            sps = sts.pop(i)
            pt = SL.alloc()
            for (lo, hi, bias_fn) in qblocks:
                lo2 = max(lo, cs)
                if lo2 >= hi:
                    continue
                bap, breg = bias_fn(k)
                act(pt.b[0:nk, lo2:hi], sps.t[0:nk, lo2:hi], AF.Exp, [sps.r(), breg], [pt.r()], bias=bap, scale=0.125)
            PS.release(sps)
            if k["tri"]:
                tt(pt.b[0:nk, cs:cs + nk], pt.b[0:nk, cs:cs + nk], TRI.t[0:nk, 0:nk], ALU.mult, [pt.r(), TRI.r()],
                   [pt.r()])
            mm(acc.t[0:65, cs:Wq], k["lhsT_v"], pt.b[0:nk, cs:Wq], [k["v_reg"], pt.r()] + k.get("v_extra", []),
               [acc.r()], start=(i == 0), stop=(i == n - 1))
            SL.release(pt)
        rd = SL.alloc()
        P.op("dve", lambda e: e.reciprocal(out=rd.f[64:65, 0:Wq], in_=acc.t[64:65, 0:Wq]), [acc.r()], [rd.r()])
        bc = PS.alloc()
        mm(bc.t[0:64, 0:Wq], ONESF.t[64:65, 0:64], rd.f[64:65, 0:Wq], [ONESF.r(), rd.r()], [bc.r()])
        rb = SL.alloc()
        act(rb.f[0:64, 0:Wq], bc.t[0:64, 0:Wq], AF.Copy, [bc.r()], [rb.r()])
        PS.release(bc)
        tt(att_dst, acc.t[0:64, 0:Wq], rb.f[0:64, 0:Wq], ALU.mult, [acc.r(), rb.r()], [att_reg])
        PS.release(acc)
        SL.release(rd, rb)

    def layer_norm(xres, l, which, W=512):
        psm = PS.alloc()
        psq = PS.alloc()
        for c in range(8):
            t = SL.alloc()
            act(t.b[:, 0:W], xres[c].f[:, 0:W], AF.Copy, [xres[c].r()], [t.r()])
            act(t.b[:, 512:512 + W], xres[c].f[:, 0:W], AF.Square, [xres[c].r()], [t.r()])
            mm(psm.t[:, 0:W], ONESB.t[:, :], t.b[:, 0:W], [ONESB.r(), t.r()], [psm.r()], start=(c == 0), stop=(c == 7))
            mm(psq.t[:, 0:W], ONESB.t[:, :], t.b[:, 512:512 + W], [ONESB.r(), t.r()], [psq.r()], start=(c == 0),
               stop=(c == 7))
            SL.release(t)
        mean = SL.alloc()
        msq = SL.alloc()
        act(mean.f[:, 0:W], psm.t[:, 0:W], AF.Copy, [psm.r()], [mean.r()])
        act(msq.f[:, 0:W], psm.t[:, 0:W], AF.Square, [psm.r()], [msq.r()])
        PS.release(psm)
        tt(msq.f[:, 0:W], psq.t[:, 0:W], msq.f[:, 0:W], ALU.subtract, [psq.r(), msq.r()], [msq.r()])
        PS.release(psq)
        ts(msq.f[:, 0:W], msq.f[:, 0:W], 0.0, EPS, ALU.max, ALU.add, [msq.r()], [msq.r()])
        act(msq.f[:, 0:W], msq.f[:, 0:W], AF.Ln, [msq.r()], [msq.r()])
        act(msq.f[:, 0:W], msq.f[:, 0:W], AF.Exp, [msq.r()], [msq.r()], scale=-0.5)
        xbf = [SL.alloc() for _ in range(4)]
        for c in range(8):
            t = SL.alloc()
            tt(t.f[:, 0:W], xres[c].f[:, 0:W], mean.f[:, 0:W], ALU.subtract, [xres[c].r(), mean.r()], [t.r()])
            tt(t.f[:, 0:W], t.f[:, 0:W], msq.f[:, 0:W], ALU.mult, [t.r(), msq.r()], [t.r()])
            g = LNP.t[:, l, which, 0, c:c + 1]
            bb = LNP.t[:, l, which, 1, c:c + 1]
            act(xres[c].f[:, 0:W], t.f[:, 0:W], AF.Identity, [t.r(), LNP.r()], [xres[c].r()], bias=bb, scale=g)
            act(xbf_ap(xbf, c, 0, W), t.f[:, 0:W], AF.Identity, [t.r(), LNP.r()], [xbf[c // 2].r()], bias=bb, scale=g)
            SL.release(t)
        SL.release(mean, msq)
        return xbf

    def conv_seg(ps, T, l, ch, kind, a, b, carry_in, save_carry):
        wv = lambda j: FCW.t[:, l, ch, j:j + 1]
        act(T.f[:, a:b], ps.t[:, a:b], AF.Identity, [ps.r(), FCW.r()], [T.r()], bias=wv(3), scale=wv(2))
        if kind == "samp":
            p3 = ps.t[:, a:b].rearrange("p (b t) -> p b t", b=4)
            t3 = T.f[:, a:b].rearrange("p (b t) -> p b t", b=4)
            stt(t3[:, :, 1:64], p3[:, :, 0:63], wv(1), t3[:, :, 1:64], ALU.mult, ALU.add, [ps.r(), FCW.r(), T.r()], [T.r()])
            stt(t3[:, :, 2:64], p3[:, :, 0:62], wv(0), t3[:, :, 2:64], ALU.mult, ALU.add, [ps.r(), FCW.r(), T.r()], [T.r()])
            s3 = SFC.t[:, l, ch, :, :]
            stt(t3[:, :, 0:2], s3[:, :, 0:2], wv(0), t3[:, :, 0:2], ALU.mult, ALU.add, [SFC.r(), FCW.r(), T.r()], [T.r()])
            stt(t3[:, :, 0:1], s3[:, :, 1:2], wv(1), t3[:, :, 0:1], ALU.mult, ALU.add, [SFC.r(), FCW.r(), T.r()], [T.r()])
            cp(HCS.t[:, l, ch, :, :], p3[:, :, 62:64], [ps.r()], [HCS.r(l)])
        else:
            stt(T.f[:, a + 1:b], ps.t[:, a:b - 1], wv(1), T.f[:, a + 1:b], ALU.mult, ALU.add, [ps.r(), FCW.r(), T.r()], [T.r()])
            stt(T.f[:, a + 2:b], ps.t[:, a:b - 2], wv(0), T.f[:, a + 2:b], ALU.mult, ALU.add, [ps.r(), FCW.r(), T.r()], [T.r()])
            if carry_in:
                hc = HC.t[:, l, ch, :]
                stt(T.f[:, a:a + 2], hc[:, 0:2], wv(0), T.f[:, a:a + 2], ALU.mult, ALU.add, [HC.r(l), FCW.r(), T.r()], [T.r()])
                stt(T.f[:, a:a + 1], hc[:, 1:2], wv(1), T.f[:, a:a + 1], ALU.mult, ALU.add, [HC.r(l), FCW.r(), T.r()], [T.r()])
            if save_carry:
                cp(HC.t[:, l, ch, :], ps.t[:, b - 2:b], [ps.r()], [HC.r(l)])

    def ffn(tname, xres, xbf, l, segs):
        M = [SL.alloc() for _ in range(11)]
        for g in range(11):
            wv, wreg = ws_get("%s_UP%d_%d" % (tname, l, g))
            for cc in range(2):
                c = g * 2 + cc
                tg = SL.alloc()
                tu = SL.alloc()
                for gu, T in ((0, tg), (1, tu)):
                    ps = PS.alloc()
                    mmg(ps.t[:, 0:512], [(wv[:, kc, gu, cc * 128:(cc + 1) * 128], xbf_ap(xbf, kc)) for kc in range(8)],
                        xr(xbf) + [wreg], [ps.r()])
                    for (kind, a, b, cin, csave) in segs:
                        conv_seg(ps, T, l, gu * 22 + c, kind, a, b, cin, csave)
                    PS.release(ps)
                act(tg.f[:, 0:512], tg.f[:, 0:512], AF.Gelu_apprx_tanh, [tg.r()], [tg.r()])
                tt(M[c // 2].b[:, (c % 2) * 512:(c % 2) * 512 + 512], tg.f[:, 0:512], tu.f[:, 0:512], ALU.mult,
                   [tg.r(), tu.r()], [M[c // 2].r()])
                SL.release(tg, tu)
            ws_done("%s_UP%d_%d" % (tname, l, g))
        SL.release(xbf)
        for o in range(8):
            wv, wreg = ws_get("%s_DN%d_%d" % (tname, l, o))
            ps = PS.alloc()
            mmg(ps.t[:, 0:512], [(wv[:, c, :], M[c // 2].b[:, (c % 2) * 512:(c % 2) * 512 + 512]) for c in range(22)],
                [m.r() for m in M] + [wreg], [ps.r()])
            ws_done("%s_DN%d_%d" % (tname, l, o))
            stt(xres[o].f[:, 0:512], xres[o].f[:, 0:512], ALPHA, ps.t[:, 0:512], ALU.mult, ALU.add,
                [xres[o].r(), ps.r()], [xres[o].r()])
            PS.release(ps)
        SL.release(M)

    light = [(0, 512), (512, 512), (1024, 512), (1536, 256)][:n_light_tiles]
    nxt_xbf = load_xbf(d_xl, 0, 512) if light else None
    for li, (k0, W) in enumerate(light):
        tn = "L%d" % li
        xbf = nxt_xbf
        if li + 1 < len(light):
            nxt_xbf = load_xbf(d_xl, light[li + 1][0], light[li + 1][1])
        jt0 = k0 // 128
        nst = W // 128
        wv, wreg = ws_get(tn + "_K")
        for p in range(4):
            ps = PS.alloc()
            mmg(ps.t[:, 0:W], [(wv[:, kc, p * 128:(p + 1) * 128], xbf_ap(xbf, kc, 0, W)) for kc in range(8)],
                xr(xbf) + [wreg], [ps.r()])
            act(KT.t[:, p, k0:k0 + W], ps.t[:, 0:W], AF.Copy, [ps.r()], [KT.r(jt0 + i) for i in range(nst)])
            PS.release(ps)
        ws_done(tn + "_K")
        wv, wreg = ws_get(tn + "_V")
        for s_ in range(nst):
            ps = PS.alloc()
            mmg(ps.t[:, 0:512], [(xbf_ap(xbf, kc, s_ * 128, s_ * 128 + 128), wv[:, kc, :]) for kc in range(8)],
                xr(xbf) + [wreg], [ps.r()])
            act(VA.t[:, jt0 + s_, :, 0:64], ps.t[:, 0:512].rearrange("p (h d) -> p h d", h=8), AF.Copy, [ps.r()],
                [VA.r(jt0 + s_)])
            PS.release(ps)
        ws_done(tn + "_V")
        lf, cum = forget_gate(xbf, W, [("light", 0, W)], 0, lf_out=False)
        ck_from_cum(cum, 0, W, jt0, add_pen=True)
        SL.release(lf, cum)
        wv, wreg = ws_get(tn + "_XB")
        for c in range(4):
            ps = PS.alloc()
            mmg(ps.t[:, 0:W], [(wv[:, kc, c * 128:(c + 1) * 128], xbf_ap(xbf, kc, 0, W)) for kc in range(8)],
                xr(xbf) + [wreg], [ps.r()])
            act(XBH.t[:, c, 3:3 + W], ps.t[:, 0:W], AF.Copy, [ps.r()], [XBH.r()])
            PS.release(ps)
        ws_done(tn + "_XB")
        SL.release(xbf)
        rglru_seg("light", 0, W, xbh_src(0), None, None)
        cp(XBH.t[:, :, 0:3], XBH.t[:, :, W:W + 3], [XBH.r()], [XBH.r()])

    _cur_tile = [0]

    def stage(k):
        if stop_stage is None:
            return
        if isinstance(stop_stage, tuple):
            if (_cur_tile[0], k) == stop_stage:
                raise _Stop()
        elif k == stop_stage:
            raise _Stop()

    nxt_xbf = load_xbf(d_xo, 0, 512) if n_own_tiles else None
    try:
      for ti in range(n_own_tiles):
          tn = "O%d" % ti
          _cur_tile[0] = ti
          stage(100)
          if DEBUG_ACTPROBE and ti >= 1:
              act(CST.t[:, 3:4], CST.t[:, 2:3], AF.Copy, [CST.r()], [CST.r()])
          c0 = ti * 512
          first = ti == 0
          last = ti == 4
          xbf = nxt_xbf
          if first:
              psegs = [("halo", 0, 256, 1792)]
          else:
              psegs = [("own", 0, 512, 2048 + (ti - 1) * 512)]

          wv, wreg = ws_get(tn + "_K")
          knew = SL.alloc() if first else None
          for p in range(4):
              ps = PS.alloc()
              mmg(ps.t[:, 0:512], [(wv[:, kc, p * 128:(p + 1) * 128], xbf_ap(xbf, kc)) for kc in range(8)],
                  xr(xbf) + [wreg], [ps.r()])
              for (kind, a, b, k0) in psegs:
                  if DEBUG_SKIPKT and ti >= 1:
                      continue
                  if kind == "own":
                      for hf in range(2):
                          cp(KT2.t[:, p, k0 - 2048 + hf * 256:k0 - 2048 + hf * 256 + 256],
                             ps.t[:, a + hf * 256:a + hf * 256 + 256],
                             [ps.r()], [KT2.r(k0 // 128 + hf * 2), KT2.r(k0 // 128 + hf * 2 + 1)])
                  else:
                      act(KT.t[:, p, k0:k0 + (b - a)], ps.t[:, a:b], AF.Copy, [ps.r()],
                          [KT.r(k0 // 128 + i) for i in range((b - a) // 128)])
              if first:
                  act(knew.b[:, p * 256:(p + 1) * 256], ps.t[:, 256:512], AF.Copy, [ps.r()], [knew.r()])
              ko = SL.alloc()
              if not (DEBUG_SKIPKO and ti >= 1):
                  cp(ko.f[:, 0:512], ps.t[:, 0:512], [ps.r()], [ko.r()])
              PS.release(ps)
              st(o_k[p, :, c0:c0 + 512], ko.f[:, 0:512], [ko.r()], slot_sem(ko))
              SL.release(ko)
          ws_done(tn + "_K")
          stage(101)

          wv, wreg = ws_get(tn + "_V")
          for s_ in range(4):
              ps = PS.alloc()
              mmg(ps.t[:, 0:512], [(xbf_ap(xbf, kc, s_ * 128, s_ * 128 + 128), wv[:, kc, :]) for kc in range(8)],
                  xr(xbf) + [wreg], [ps.r()])
              if not (first and s_ >= 2):
                  k0 = psegs[0][3] + s_ * 128
                  if first:
                      act(VA.t[:, k0 // 128, :, 0:64], ps.t[:, 0:512].rearrange("p (h d) -> p h d", h=8), AF.Copy,
                          [ps.r()], [VA.r(k0 // 128)])
                  else:
                      cp(VA.t[:, k0 // 128, :, 0:64], ps.t[:, 0:512].rearrange("p (h d) -> p h d", h=8),
                         [ps.r()], [VA.r(k0 // 128)])
              vo = SL.alloc()
              cp(vo.f[:, 0:512], ps.t[:, 0:512], [ps.r()], [vo.r()])
              PS.release(ps)
              st(o_v[c0 + s_ * 128:c0 + s_ * 128 + 128, :], vo.f[:, 0:512], [vo.r()], slot_sem(vo))
              SL.release(vo)
          if first:
              for bb in range(4):
                  ps = PS.alloc()
                  mmg(ps.t[0:64, 0:512], [(xbf_ap(xbf, kc, 256 + bb * 64, 320 + bb * 64), wv[:, kc, :]) for kc in range(8)],
                      xr(xbf) + [wreg], [ps.r()])
                  act(VNEW.t[:, bb, :, 0:64], ps.t[0:64, 0:512].rearrange("p (h d) -> p h d", h=8), AF.Copy,
                      [ps.r()], [VNEW.r(bb)])
                  PS.release(ps)
          ws_done(tn + "_V")
          stage(102)

          if first:
              fsegs = [("halo", 0, 256), ("samp", 256, 512)]
          else:
              fsegs = [("own", 0, 512)]
          lf, cum = forget_gate(xbf, 512, fsegs, c0)
          SL.release(lf)
          cref = SL.alloc()
          ckn = None
          if first:
              ck_from_cum(cum, 0, 256, 14, add_pen=False)
              qstarts = [0]
          else:
              ck_from_cum(cum, 0, 512, psegs[0][3] // 128, add_pen=False)
              qstarts = [0, 256]
          ps = PS.alloc()
          for qi, q0 in enumerate(qstarts):
              tmp = SL.alloc()
              ts(tmp.f[0:8, 0:128], ONESF.t[0:8, 0:128], cum.f[0:8, q0:q0 + 1], None, ALU.mult, None,
                 [ONESF.r(), cum.r()], [tmp.r()])
              mm(ps.t[:, qi * 8:(qi + 1) * 8], tmp.f[0:8, 0:128], I8.t[:, :], [tmp.r(), I8.r()], [ps.r()])
              SL.release(tmp)
          cp(cref.f[:, 0:8 * len(qstarts)], ps.t[:, 0:8 * len(qstarts)], [ps.r()], [cref.r()])
          PS.release(ps)
          if first:
              ckn = SL.alloc()
              ps = PS.alloc()
              for bb in range(4):
                  mm(ps.t[0:64, bb * 8:(bb + 1) * 8], cum.f[0:8, 256 + bb * 64:320 + bb * 64], I8.t[:, :],
                     [cum.r(), I8.r()], [ps.r()])
              ts(ckn.f[0:64, 0:32], ps.t[0:64, 0:32], -1.0, None, ALU.mult, None, [ps.r()], [ckn.r()])
              PS.release(ps)
          SL.release(cum)

          stage(103)
          wv, wreg = ws_get(tn + "_Q")
          qt = [SL.alloc() for _ in range(2)]
          for p in range(4):
              ps = PS.alloc()
              mmg(ps.t[:, 0:512], [(wv[:, kc, p * 128:(p + 1) * 128], xbf_ap(xbf, kc)) for kc in range(8)],
                  xr(xbf) + [wreg], [ps.r()])
              if first:
                  act(qt[p // 2].b[:, (p % 2) * 512:(p % 2) * 512 + 512], ps.t[:, 0:512], AF.Copy, [ps.r()],
                      [qt[p // 2].r()])
              else:
                  cp(qt[p // 2].b[:, (p % 2) * 512:(p % 2) * 512 + 512], ps.t[:, 0:512], [ps.r()], [qt[p // 2].r()])
              PS.release(ps)
          ws_done(tn + "_Q")

          stage(104)
          wv, wreg = ws_get(tn + "_GB")
          gb = [SL.alloc() for _ in range(4)]
          for c in range(4):
              ps = PS.alloc()
              mmg(ps.t[:, 0:512], [(wv[:, kc, c * 128:(c + 1) * 128], xbf_ap(xbf, kc)) for kc in range(8)],
                  xr(xbf) + [wreg], [ps.r()])
              act(gb[c].f[:, 0:512], ps.t[:, 0:512], AF.Gelu_apprx_tanh, [ps.r()], [gb[c].r()])
              PS.release(ps)
          ws_done(tn + "_GB")

          stage(105)
          wv, wreg = ws_get(tn + "_XB")
          xbs = None
          if first:
              xbs = [SL.alloc() for _ in range(4)]
              for c in range(4):
                  ld("sp", xbs[c].f[:, 0:268].rearrange("p (b t) -> p b t", b=4)[:, :, 0:3], d_rcs[:, c, :, :],
                     [xbs[c].r()], slot_sem(xbs[c]))
          for c in range(4):
              ps = PS.alloc()
              mmg(ps.t[:, 0:512], [(wv[:, kc, c * 128:(c + 1) * 128], xbf_ap(xbf, kc)) for kc in range(8)],
                  xr(xbf) + [wreg], [ps.r()])
              if first:
                  act(XBH.t[:, c, 3:3 + 256], ps.t[:, 0:256], AF.Copy, [ps.r()], [XBH.r()])
                  act(xbs[c].f[:, 0:268].rearrange("p (b t) -> p b t", b=4)[:, :, 3:67],
                      ps.t[:, 256:512].rearrange("p (b t) -> p b t", b=4), AF.Copy, [ps.r()], [xbs[c].r()])
              else:
                  cp(XBH.t[:, c, 3:3 + 512], ps.t[:, 0:512], [ps.r()], [XBH.r()])
              PS.release(ps)
          ws_done(tn + "_XB")
          SL.release(xbf)

          stage(1)
          hg = [SL.alloc() for _ in range(2)]
          if first:
              rglru_seg("halo", 0, 256, xbh_src(0), gb, hg)
              cp(XBH.t[:, :, 0:3], XBH.t[:, :, 256:259], [XBH.r()], [XBH.r()])
              ts(HCAR.t[:, :], HCAR.t[:, :], FLAG.t[:, 0:1], None, ALU.mult, None, [HCAR.r(), FLAG.r()], [HCAR.r()])

              def xbs_src(c):
                  return xbs[c].f[:, 0:268].rearrange("p (b t) -> p b t", b=4), xbs[c].r()
              stage(13)
              rglru_seg("samp", 256, 512, xbs_src, gb, hg)
              stage(16)
              for c in range(4):
                  st(o_rcs[:, c, :, :], xbs[c].f[:, 0:268].rearrange("p (b t) -> p b t", b=4)[:, :, 64:67],
                     [xbs[c].r()], slot_sem(xbs[c]))
              st(o_rhs[:, :, :], HSL.t[:, :, :], [HSL.r()], osem())
              SL.release(xbs)
          else:
              rglru_seg("own", 0, 512, xbh_src(0), gb, hg)
              if last:
                  st(o_rc[:, :, :], XBH.t[:, :, 512:515], [XBH.r()], osem())
                  st(o_rh[:, :], HCAR.t[:, :], [HCAR.r()], osem())
              else:
                  cp(XBH.t[:, :, 0:3], XBH.t[:, :, 512:515], [XBH.r()], [XBH.r()])
          SL.release(gb)

          stage(2)
          att = [SL.alloc() for _ in range(4)]
          for (kind, a, b, k0) in psegs:
              Wq = b - a
              jd0 = k0 // 128
              nj = jd0 + Wq // 128
              nqb = Wq // 256
              for h in range(8):
                  p, r0 = h // 2, (h % 2) * 64
                  bias = SL.alloc()
                  for qi in (nqb - 1,):
                      ts(bias.f[:, qi * 32:qi * 32 + nj], CK.t[:, 0:nj, h], -1.0, cref.f[:, qi * 8 + h:qi * 8 + h + 1],
                         ALU.mult, ALU.add, [CK.r(j) for j in range(nj)] + [cref.r()], [bias.r()])
                  keys = []
                  for j in range(nj):
                      dd = j - jd0
                      ktt = KT if j < 16 else KT2
                      jc = j if j < 16 else j - 16
                      keys.append(dict(lhsT_k=ktt.t[r0:r0 + 64, p, jc * 128:(jc + 1) * 128], k_reg=ktt.r(j),
                                       lhsT_v=VA.t[:, j, h, 0:65], v_reg=VA.r(j), v_extra=[VA.r("ones")], nk=128,
                                       cstart=max(0, dd * 128), tri=dd >= 0, j=j))
                  qbl = []
                  for qi in (nqb - 1,):
                      def bf(k, qi=qi, bias=bias):
                          return bias.f[:, qi * 32 + k["j"]:qi * 32 + k["j"] + 1], bias.r()
                      qbl.append((0, Wq, bf))

                  def qap(lo, hi, p=p, r0=r0, a=a):
                      return qt[p // 2].b[r0:r0 + 64, (p % 2) * 512 + a + lo:(p % 2) * 512 + a + hi], qt[p // 2].r()
                  attend(qap, Wq, qbl, keys, att[h // 2].b[0:64, (h % 2) * 512 + a:(h % 2) * 512 + b], att[h // 2].r())
                  SL.release(bias)
          stage(3)
          if first:
              ts(CK.t[:, 14:16, :], CK.t[:, 14:16, :], PEN.t[:, 0:1], None, ALU.add, None, [CK.r(14), CK.r(15), PEN.r()],
                 [CK.r(14), CK.r(15)])
              for bb in range(4):
                  lfr = [SL.alloc() for _ in range(4)]
                  for i in range(4):
                      ld("sp", lfr[i].f[0:8, 0:512], d_clf[bb, :, i * 512:(i + 1) * 512], [lfr[i].r()], slot_sem(lfr[i]))
                  cks = SL.alloc()
                  ps = PS.alloc()
                  for i in range(4):
                      rr = SL.alloc()
                      init = 0.0 if i == 0 else prev.f[0:8, 511:512]
                      rds = [lfr[i].r(), ONESF.r()] + ([] if i == 0 else [prev.r()])
                      P.op("dve", lambda e, o_=rr.f[0:8, 0:512], d1=lfr[i].f[0:8, 0:512], init=init: e.tensor_tensor_scan(
                          out=o_, data0=ONESF.t[0:8, 0:512], data1=d1, initial=init,
                          op0=ALU.mult, op1=ALU.add), rds, [rr.r()])
                      tt(lfr[i].f[0:8, 0:512], rr.f[0:8, 0:512], lfr[i].f[0:8, 0:512], ALU.subtract, [rr.r(), lfr[i].r()],
                         [lfr[i].r()])
                      for jj in range(4):
                          j = i * 4 + jj
                          mm(ps.t[:, j * 8:(j + 1) * 8], lfr[i].f[0:8, jj * 128:(jj + 1) * 128], I8.t[:, :],
                             [lfr[i].r(), I8.r()], [ps.r()])
                      if i > 0:
                          SL.release(prev)
                      prev = rr
                  SL.release(prev)
                  cp(cks.f[:, 0:128], ps.t[:, 0:128], [ps.r()], [cks.r()])
                  PS.release(ps)
                  SL.release(lfr)
                  for p in range(4):
                      kts = [SL.alloc() for _ in range(2)]
                      for i in range(2):
                          ld("pool", kts[i].b[:, 0:1024], d_ckT[bb, p, :, i * 1024:(i + 1) * 1024], [kts[i].r()],
                             slot_sem(kts[i], True))
                      vst = [SL.alloc() for _ in range(2)]
                      vas = VAS[0]
                      for i in range(2):
                          ld("pool", vst[i].b[:, 0:1024].rearrange("p (j n) -> p j n", j=8),
                             d_cvr[bb, p, :, i * 8:(i + 1) * 8, :], [vst[i].r()], slot_sem(vst[i], True))
                          cp(vas.t[:, i * 8:(i + 1) * 8, :, 0:64],
                             vst[i].b[:, 0:1024].rearrange("p (j h d) -> p j h d", j=8, h=2), [vst[i].r()], [vas.r()],
                             en="dve")
                      SL.release(vst)
                      for hh_ in range(2):
                          h = 2 * p + hh_
                          r0 = hh_ * 64
                          keys = []
                          for j in range(16):
                              keys.append(dict(lhsT_k=kts[j // 8].b[r0:r0 + 64, (j % 8) * 128:(j % 8 + 1) * 128],
                                               k_reg=kts[j // 8].r(), lhsT_v=vas.t[:, j, hh_, 0:65], v_reg=vas.r(),
                                               v_extra=[vas.r("ones")], nk=128, cstart=0, tri=False, j=j))
                          keys.append(dict(lhsT_k=knew.b[r0:r0 + 64, p * 256 + bb * 64:p * 256 + bb * 64 + 64],
                                           k_reg=knew.r(), lhsT_v=VNEW.t[:, bb, h, 0:65], v_reg=VNEW.r(bb),
                                           v_extra=[VNEW.r("ones")], nk=64, cstart=0, tri=True, j=16))

                          def bf(k, h=h, bb=bb, cks=cks):
                              if k["j"] < 16:
                                  return cks.f[:, k["j"] * 8 + h:k["j"] * 8 + h + 1], cks.r()
                              return ckn.f[0:64, bb * 8 + h:bb * 8 + h + 1], ckn.r()

                          def qap(lo, hi, p=p, r0=r0, bb=bb):
                              o = (p % 2) * 512 + 256 + bb * 64
                              return qt[p // 2].b[r0:r0 + 64, o + lo:o + hi], qt[p // 2].r()
                          o = (h % 2) * 512 + 256 + bb * 64
                          attend(qap, 64, [(0, 64, bf)], keys, att[h // 2].b[0:64, o:o + 64], att[h // 2].r())
                      SL.release(kts)
                  SL.release(cks)
              SL.release(knew, ckn)
          SL.release(qt, cref)

          stage(4)
          xres = [SL.alloc() for _ in range(8)]
          for c in range(8):
              ld("sp", xres[c].f[:, 0:512], d_xo[c, :, c0:c0 + 512], [xres[c].r()], slot_sem(xres[c]))
          wa0, ra0 = ws_get(tn + "_WOA0")
          wa1, ra1 = ws_get(tn + "_WOA1")
          wr_, rr_ = ws_get(tn + "_WOR")
          for o in range(8):
              ps = PS.alloc()
              pairs = []
              for h in range(8):
                  wa = wa0 if h < 4 else wa1
                  pairs.append((wa[:, h % 4, o * 128:(o + 1) * 128], att[h // 2].b[0:64, (h % 2) * 512:(h % 2) * 512 + 512]))
              for c in range(4):
                  pairs.append((wr_[:, c, o * 128:(o + 1) * 128], hg[c // 2].b[:, (c % 2) * 512:(c % 2) * 512 + 512]))
              mmg(ps.t[:, 0:512], pairs, [ra0, ra1, rr_] + [x.r() for x in att] + [x.r() for x in hg], [ps.r()])
              stt(xres[o].f[:, 0:512], xres[o].f[:, 0:512], ALPHA, ps.t[:, 0:512], ALU.mult, ALU.add,
                  [xres[o].r(), ps.r()], [xres[o].r()])
              PS.release(ps)
          ws_done(tn + "_WOA0")
          ws_done(tn + "_WOA1")
          ws_done(tn + "_WOR")
          SL.release(att, hg)

          if first:
              csegs = [("halo", 0, 256, False, True), ("samp", 256, 512, False, False)]
          else:
              csegs = [("own", 0, 512, True, True)]

          stage(5)
          xbf = layer_norm(xres, 0, 0)
          stage(6)
          ffn(tn, xres, xbf, 0, csegs)
          stage(7)
          if first:
              ts(HC.t[:, 0, :, :], HC.t[:, 0, :, :], FLAG.t[:, 0:1], None, ALU.mult, None, [HC.r(0), FLAG.r()], [HC.r(0)])
          xbf = layer_norm(xres, 0, 1)

          stage(8)
          U = [SL.alloc() for _ in range(4)]
          for half in range(2):
              wv, wreg = ws_get("%s_WIO%d" % (tn, half))
              for cc in range(4):
                  c = half * 4 + cc
                  ps = PS.alloc()
                  mmg(ps.t[:, 0:512], [(wv[:, kc, cc * 128:(cc + 1) * 128], xbf_ap(xbf, kc)) for kc in range(8)],
                      xr(xbf) + [wreg], [ps.r()])
                  act(U[c // 2].b[:, (c % 2) * 512:(c % 2) * 512 + 512], ps.t[:, 0:512], AF.Gelu_apprx_tanh, [ps.r()],
                      [U[c // 2].r()])
                  PS.release(ps)
              ws_done("%s_WIO%d" % (tn, half))
          VT = [[SL.alloc(), SL.alloc()] for _ in range(4)]
          for vh in range(2):
              wv, wreg = ws_get("%s_WIO%d" % (tn, 2 + vh))
              for s_ in range(4):
                  ps = PS.alloc()
                  mmg(ps.t[:, 0:512], [(xbf_ap(xbf, kc, s_ * 128, s_ * 128 + 128), wv[:, kc, :]) for kc in range(8)],
                      xr(xbf) + [wreg], [ps.r()])
                  act(VT[s_][vh].f[:, 0:512], ps.t[:, 0:512], AF.Gelu_apprx_tanh, [ps.r()], [VT[s_][vh].r()])
                  PS.release(ps)
              ws_done("%s_WIO%d" % (tn, 2 + vh))
          SL.release(xbf)
          VNB = []
          for s_ in range(4):
              stt_ = SL.alloc()
              for vh in range(2):
                  P.op("dve", lambda e, o_=stt_.f[:, vh * 6:vh * 6 + 6], i_=VT[s_][vh].f[:, 0:512]: e.bn_stats(out=o_, in_=i_),
                       [VT[s_][vh].r()], [stt_.r()])
              P.op("dve", lambda e, stt_=stt_: e.bn_aggr(out=stt_.f[:, 16:18], in_=stt_.f[:, 0:12]), [stt_.r()], [stt_.r()])
              ts(stt_.f[:, 17:18], stt_.f[:, 17:18], 0.0, EPS, ALU.max, ALU.add, [stt_.r()], [stt_.r()])
              act(stt_.f[:, 17:18], stt_.f[:, 17:18], AF.Ln, [stt_.r()], [stt_.r()])
              act(stt_.f[:, 17:18], stt_.f[:, 17:18], AF.Exp, [stt_.r()], [stt_.r()], scale=-0.5)
              vb = SL.alloc()
              VNB.append(vb)
              for vh in range(2):
                  v = VT[s_][vh]
                  ts(v.f[:, 0:512], v.f[:, 0:512], stt_.f[:, 16:17], stt_.f[:, 17:18], ALU.subtract, ALU.mult,
                     [v.r(), stt_.r()], [v.r()])
                  tt(v.f[:, 0:512], v.f[:, 0:512], SGG.t[:, vh * 512:(vh + 1) * 512], ALU.mult, [v.r(), SGG.r()], [v.r()])
                  tt(v.f[:, 0:512], v.f[:, 0:512], SGBB.t[:, vh * 512:(vh + 1) * 512], ALU.add, [v.r(), SGBB.r()], [v.r()])
                  act(vb.b[:, vh * 512:(vh + 1) * 512], v.f[:, 0:512], AF.Copy, [v.r()], [vb.r()])
                  if first and s_ >= 2:
                      st(o_sgv[(s_ - 2) * 128:(s_ - 1) * 128, vh * 512:(vh + 1) * 512], v.f[:, 0:512], [v.r()], slot_sem(v))
              SL.release(stt_, VT[s_])
          GT = [SL.alloc() for _ in range(4)]
          for g in range(8):
              ps = PS.alloc()
              for s_ in range(4):
                  if first and s_ >= 2:
                      for b2 in range(2):
                          bb = (s_ - 2) * 2 + b2
                          r0 = b2 * 64
                          o = 256 + bb * 64
                          mm(ps.t[:, o:o + 64], VNB[s_].b[r0:r0 + 64, g * 128:(g + 1) * 128], WST64.t[r0:r0 + 64, g, :],
                             [VNB[s_].r(), WST64.r()], [ps.r()], start=True, stop=False)
                          mm(ps.t[:, o:o + 64], ONE1B.t[0:1, :], BHI.t[0:1, g, 0:64], [ONE1B.r(), BHI.r()], [ps.r()],
                             start=False, stop=False)
                          mm(ps.t[:, o:o + 64], ONE1B.t[0:1, :], BLO.t[0:1, g, 0:64], [ONE1B.r(), BLO.r()], [ps.r()],
                             start=False, stop=True)
                  else:
                      o = s_ * 128
                      mm(ps.t[:, o:o + 128], VNB[s_].b[:, g * 128:(g + 1) * 128], WST.t[:, g, :], [VNB[s_].r(), WST.r()],
                         [ps.r()], start=True, stop=False)
                      mm(ps.t[:, o:o + 128], ONE1B.t[0:1, :], BHI.t[0:1, g, :], [ONE1B.r(), BHI.r()], [ps.r()],
                         start=False, stop=False)
                      mm(ps.t[:, o:o + 128], ONE1B.t[0:1, :], BLO.t[0:1, g, :], [ONE1B.r(), BLO.r()], [ps.r()],
                         start=False, stop=True)
              tt(GT[g // 2].b[:, (g % 2) * 512:(g % 2) * 512 + 512], ps.t[:, 0:512],
                 U[g // 2].b[:, (g % 2) * 512:(g % 2) * 512 + 512], ALU.mult, [ps.r(), U[g // 2].r()], [GT[g // 2].r()])
              PS.release(ps)
          SL.release(VNB, U)
          for half in range(2):
              wv, wreg = ws_get("%s_WOO%d" % (tn, half))
              for oo in range(4):
                  o = half * 4 + oo
                  ps = PS.alloc()
                  mmg(ps.t[:, 0:512], [(wv[:, kc, oo * 128:(oo + 1) * 128], GT[kc // 2].b[:, (kc % 2) * 512:(kc % 2) * 512 + 512])
                                       for kc in range(8)], [x.r() for x in GT] + [wreg], [ps.r()])
                  stt(xres[o].f[:, 0:512], xres[o].f[:, 0:512], ALPHA, ps.t[:, 0:512], ALU.mult, ALU.add,
                      [xres[o].r(), ps.r()], [xres[o].r()])
                  PS.release(ps)
              ws_done("%s_WOO%d" % (tn, half))
          SL.release(GT)

          xbf = layer_norm(xres, 1, 0)
          ffn(tn, xres, xbf, 1, csegs)
          if ti + 1 < n_own_tiles:
              nxt_xbf = load_xbf(d_xo, c0 + 512, 512)
          if first:
              ts(HC.t[:, 1, :, :], HC.t[:, 1, :, :], FLAG.t[:, 0:1], None, ALU.mult, None, [HC.r(1), FLAG.r()], [HC.r(1)])
              for l in range(2):
                  st(o_fcs[l], HCS.t[:, l, :, :, :], [HCS.r(l)], osem())
          if last:
              for l in range(2):
                  st(o_fcp[l], HC.t[:, l, :, :], [HC.r(l)], osem())
          xbf = layer_norm(xres, 1, 1)
          SL.release(xbf)
          for c in range(8):
              st(o_y[c, :, c0:c0 + 512], xres[c].f[:, 0:512], [xres[c].r()], slot_sem(xres[c]))
          SL.release(xres)
    except _Stop:
        pass

    P.final_wait("sp")

    allsems = [P.sem[e] for e in P.ENGS] + P.dsems
    for s in allsems:
        s.h = es.enter_context(nc.semaphore(s.name))
    block = es.enter_context(nc.Block())

    @block.tensor
    def _(e):
        P.emit("pe", e)

    @block.scalar
    def _(e):
        P.emit("act", e)

    @block.vector
    def _(e):
        P.emit("dve", e)

    @block.gpsimd
    def _(e):
        P.emit("pool", e)

    @block.sync
    def _(e):
        P.emit("sp", e)

    info = dict(arena=ns, arena_low=SL.low, psum_low=PS.low, nsem=len(allsems),
                nops={e: len(P.ops[e]) for e in P.ENGS})
    es.close()
    return nc, info


def _f(x):
    return np.ascontiguousarray(np.asarray(x, dtype=np.float32))


def _shared_inputs(w_in_e, b_f, rg_conv_w, rg_conv_b, rg_wa, rg_ba, rg_wx, rg_bx, rg_lam, w_out_e, w_in_o, sgu_g,
                   sgu_b, sgu_w, sgu_bias, w_out_o, ln_mix_g, ln_mix_b, ln_ffn_g, ln_ffn_b, ffn_w_up, ffn_conv_w,
                   ffn_conv_b, ffn_w_down):
    m = {}
    W = _f(w_in_e)[0]
    cols = [(0, 512), (512, 1024), (1024, 1536), (1544, 2056), (2056, 2568)]
    m["win0"] = _f(np.stack([W[:, a:b].reshape(8, 128, 512).transpose(1, 0, 2) for a, b in cols]))
    m["wf"] = _f(W[:, 1536:1544].reshape(8, 128, 8).transpose(1, 0, 2))
    Wo = _f(w_out_e)[0]
    m["woa"] = _f(Wo[0:512].reshape(2, 4, 64, 1024).transpose(0, 2, 1, 3))
    m["wor"] = _f(Wo[512:1024].reshape(4, 128, 1024).transpose(1, 0, 2))
    Wu = _f(ffn_w_up)
    m["wup"] = _f(Wu.reshape(2, 8, 128, 2, 11, 256).transpose(0, 4, 2, 1, 3, 5))
    Wd = _f(ffn_w_down)
    m["wdn"] = _f(Wd.reshape(2, 22, 128, 8, 128).transpose(0, 3, 2, 1, 4))
    Wi = _f(w_in_o)[0]
    m["wio"] = _f(Wi.reshape(8, 128, 4, 512).transpose(2, 1, 0, 3))
    Woo = _f(w_out_o)[0]
    m["woo"] = _f(Woo.reshape(8, 128, 2, 512).transpose(2, 1, 0, 3))
    rgw = np.zeros((2, 128, 4, 128), np.float32)
    for wi, w in enumerate((_f(rg_wa)[0], _f(rg_wx)[0])):
        for blk in range(8):
            c, hb = blk // 2, blk % 2
            rgw[wi, hb * 64:(hb + 1) * 64, c, hb * 64:(hb + 1) * 64] = w[blk]
    m["rgw"] = rgw
    sw = _f(sgu_w)[0]
    wst = _f(sw.transpose(2, 0, 1))
    m["wst"] = wst
    m["wst64"] = _f(np.concatenate([wst[0:64, :, 0:64], wst[0:64, :, 0:64]], axis=0))
    m["sgb"] = _f(sgu_bias).reshape(1, 8, 128)
    lnp = np.zeros((128, 2, 2, 2, 8), np.float32)
    for l in range(2):
        for wi, (g, b) in enumerate(((ln_mix_g, ln_mix_b), (ln_ffn_g, ln_ffn_b))):
            lnp[:, l, wi, 0, :] = _f(g)[l].reshape(8, 128).T
            lnp[:, l, wi, 1, :] = _f(b)[l].reshape(8, 128).T
    m["lnp"] = lnp
    fcw = np.zeros((128, 2, NHC, 4), np.float32)
    for l in range(2):
        for j in range(3):
            fcw[:, l, :, j] = _f(ffn_conv_w)[l, j].reshape(NHC, 128).T
        fcw[:, l, :, 3] = _f(ffn_conv_b)[l].reshape(NHC, 128).T
    m["fcw"] = fcw
    rgp = np.zeros((128, 4, 8), np.float32)
    for j in range(4):
        rgp[:, :, j] = _f(rg_conv_w)[0, j].reshape(4, 128).T
    for j, v in enumerate((rg_conv_b, rg_ba, rg_bx, rg_lam)):
        rgp[:, :, 4 + j] = _f(v)[0].reshape(4, 128).T
    m["rgp"] = rgp
    m["bfv"] = _f(b_f)[0].reshape(8, 1)
    m["sgg"] = _f(sgu_g).reshape(1, 1024)
    m["sgbb"] = _f(sgu_b).reshape(1, 1024)
    m["tri"] = _f(np.triu(np.ones((128, 128), np.float32)))
    m["i8"] = _f(np.eye(8, dtype=np.float32))
    return m


_CACHE = {}


def kernel(x_prompt, x_sample, cache_k, cache_v, cache_logf, state_rglru_conv, state_rglru_h, state_ffn_conv,
           **weights):
    if "nc" not in _CACHE:
        _CACHE["nc"] = build_program()
    nc, info = _CACHE["nc"]
    shared = _shared_inputs(**weights)
    xp = _f(x_prompt)
    xs = _f(x_sample)
    ck = _f(cache_k)[0]
    cv = _f(cache_v)[0]
    clf = _f(cache_logf)[0]
    rcs = _f(state_rglru_conv)[0]
    rh0 = _f(state_rglru_h)[0]
    sfc = _f(state_ffn_conv)
    in_maps = []
    for c in range(8):
        b, hh = c // 2, c % 2
        win = np.concatenate([np.zeros((2048, D), np.float32), xp[b]], axis=0)[hh * 2048:hh * 2048 + 4096]
        samp = xs[4 * c:4 * c + 4].reshape(256, D)
        own = np.concatenate([win[1792:2048], samp, win[2048:4096]], axis=0)
        m = dict(shared)
        m["xl"] = _f(win[0:1792].T.reshape(8, 128, 1792))
        m["xo"] = _f(own.T.reshape(8, 128, NOWN))
        kb = ck[4 * c:4 * c + 4, ::-1]
        m["ckT"] = _f(kb.reshape(4, 2048, 4, 128).transpose(0, 2, 3, 1))
        vb = cv[4 * c:4 * c + 4, ::-1]
        m["cvr"] = _f(vb.reshape(4, 16, 128, 4, 128).transpose(0, 3, 2, 1, 4))
        m["clf"] = _f(clf[4 * c:4 * c + 4, ::-1].transpose(0, 2, 1))
        m["rcs"] = _f(rcs[4 * c:4 * c + 4].reshape(4, 3, 4, 128).transpose(3, 2, 0, 1))
        m["rh0"] = _f(rh0[4 * c:4 * c + 4].reshape(4, 4, 128).transpose(2, 1, 0))
        m["sfc"] = _f(sfc[:, 4 * c:4 * c + 4].reshape(2, 4, 2, NHC, 128).transpose(0, 4, 3, 1, 2))
        m["flag"] = np.full((128, 1), float(hh), np.float32)
        m["pen"] = np.full((128, 1), 0.0 if hh else PENV, np.float32)
        in_maps.append(m)
    res = run_bass_kernel_spmd(nc, in_maps, core_ids=list(range(8))).results

    y_p = np.zeros((4, 4096, D), np.float32)
    y_s = np.zeros((32, 64, D), np.float32)
    k_p = np.zeros((1, 4, 4096, 8, 64), np.float32)
    v_p = np.zeros((1, 4, 4096, 8, 64), np.float32)
    lf_p = np.zeros((1, 4, 4096, 8), np.float32)
    k_s = np.zeros((1, 32, 64, 8, 64), np.float32)
    v_s = np.zeros((1, 32, 64, 8, 64), np.float32)
    lf_s = np.zeros((1, 32, 64, 8), np.float32)
    rc_p = np.zeros((1, 4, 3, 512), np.float32)
    rh_p = np.zeros((1, 4, 512), np.float32)
    rc_s = np.zeros((1, 32, 3, 512), np.float32)
    rh_s = np.zeros((1, 32, 512), np.float32)
    fc_p = np.zeros((2, 4, 2, 2 * DFF), np.float32)
    fc_s = np.zeros((2, 32, 2, 2 * DFF), np.float32)
    sg_s = np.zeros((1, 32, 64, D), np.float32)
    for c in range(8):
        r = res[c]
        b, hh = c // 2, c % 2
        sl = slice(hh * 2048, hh * 2048 + 2048)
        yT = r["yT"].reshape(D, NOWN)
        y_p[b, sl] = yT[:, 512:].T
        y_s[4 * c:4 * c + 4] = yT[:, 256:512].T.reshape(4, 64, D)
        kT = r["koT"].reshape(512, NOWN)
        k_p[0, b, sl] = kT[:, 512:].T.reshape(2048, 8, 64)
        k_s[0, 4 * c:4 * c + 4] = kT[:, 256:512].T.reshape(4, 64, 8, 64)
        vo = r["vo"]
        v_p[0, b, sl] = vo[512:].reshape(2048, 8, 64)
        v_s[0, 4 * c:4 * c + 4] = vo[256:512].reshape(4, 64, 8, 64)
        lf = r["lfo"]
        lf_p[0, b, sl] = lf[:, 512:].T
        lf_s[0, 4 * c:4 * c + 4] = lf[:, 256:512].T.reshape(4, 64, 8)
        if hh == 1:
            rc_p[0, b] = r["rco"].transpose(2, 1, 0).reshape(3, 512)
            rh_p[0, b] = r["rho"].T.reshape(512)
            fc_p[:, b] = r["fcpo"].transpose(0, 3, 2, 1).reshape(2, 2, 2 * DFF)
        rc_s[0, 4 * c:4 * c + 4] = r["rcso"].transpose(2, 3, 1, 0).reshape(4, 3, 512)
        rh_s[0, 4 * c:4 * c + 4] = r["rhso"].transpose(2, 1, 0).reshape(4, 512)
        fc_s[:, 4 * c:4 * c + 4] = r["fcso"].transpose(0, 3, 4, 2, 1).reshape(2, 4, 2, 2 * DFF)
        sg_s[0, 4 * c:4 * c + 4] = r["sgvo"].reshape(4, 64, D)
    return (y_p, y_s, k_p, v_p, lf_p, k_s, v_s, lf_s, rc_p, rh_p, rc_s, rh_s, fc_p, fc_s, sg_s)


if __name__ == "__main__":
    import time
    t0 = time.time()
    nc, info = build_program()
    print("built in", time.time() - t0, info)
```

```python
import numpy as np
from contextlib import ExitStack
import concourse.bass as bass
import concourse.mybir as mybir
from concourse.bass_utils import run_bass_kernel_spmd

F32 = mybir.dt.float32
BF16 = mybir.dt.bfloat16
AF = mybir.ActivationFunctionType
ALU = mybir.AluOpType

D = 1024
NCH = 8
DFF = 2816
NHC = 44
ALPHA = 4.0 ** 0.25
EPS = 1e-5
PENV = 30000.0
NLIGHT = 1792
NOWN = 2560
NWSLOT = 4
SAFE_SAME_ENGINE = True
DEBUG_NO_STORES = False
DEBUG_NO_GELU = False
DEBUG_SKIPKT = False
DEBUG_ACTPROBE = False
DEBUG_SKIPKO = False


class Sem:
    def __init__(self, name, dma=False):
        self.name = name
        self.dma = dma
        self.count = 0
        self.h = None


class Reg:
    __slots__ = ("w", "rd")

    def __init__(self):
        self.w = None
        self.rd = {}


class Prog:
    ENGS = ("pe", "act", "dve", "pool", "sp")

    def __init__(self):
        self.sem = {e: Sem("s_" + e) for e in self.ENGS}
        self.ops = {e: [] for e in self.ENGS}
        self.seen = {e: {} for e in self.ENGS}
        self.dsems = []

    def dsem(self, name):
        s = Sem(name, dma=True)
        self.dsems.append(s)
        return s

    def _waits(self, en, reads, writes):
        mysem = self.sem[en]
        need = {}

        def add(tok, skip_same):
            if tok is None:
                return
            s, v = tok
            if s is mysem and (en == "pe" or not SAFE_SAME_ENGINE):
                return
            if s.dma:
                v = s.count
            if need.get(s, 0) < v:
                need[s] = v

        for r in reads:
            add(r.w, False)
        for w in writes:
            add(w.w, True)
            for s, v in w.rd.items():
                add((s, v), True)
        out = []
        seen = self.seen[en]
        for s, v in need.items():
            if seen.get(s, 0) < v:
                seen[s] = v
                out.append((s, v))
        return out

    def _commit(self, tok, reads, writes):
        s, v = tok
        for r in reads:
            if r.rd.get(s, 0) < v:
                r.rd[s] = v
        for w in writes:
            w.w = tok
            w.rd = {}

    def op(self, en, fn, reads=(), writes=()):
        waits = self._waits(en, reads, writes)
        s = self.sem[en]
        s.count += 1
        self.ops[en].append((waits, fn, s, 1))
        self._commit((s, s.count), reads, writes)

    def dma(self, qn, fn, reads, writes, sem):
        waits = self._waits(qn, reads, writes)
        sem.count += 16
        self.ops[qn].append((waits, fn, sem, 16))
        self._commit((sem, sem.count), reads, writes)

    def final_wait(self, qn):
        waits = [(s, s.count) for s in self.dsems if s.count > 0]
        waits += [(self.sem[e], self.sem[e].count) for e in self.ENGS if e != qn and self.sem[e].count > 0]
        self.ops[qn].append((waits, None, None, 0))

    def emit(self, en, eh):
        for waits, fn, s, amt in self.ops[en]:
            for ws, v in waits:
                eh.wait_ge(ws.h, v)
            if fn is not None:
                ins = fn(eh)
                ins.then_inc(s.h, amt)


class Buf:
    def __init__(self, t, name=""):
        self.t = t
        self.name = name
        self.regs = {}
        self.sem = None

    def r(self, key=None):
        g = self.regs.get(key)
        if g is None:
            g = self.regs[key] = Reg()
        return g


class Slot(Buf):
    def __init__(self, f, idx):
        super().__init__(f, "slot%d" % idx)
        self.f = f
        self.b = f.bitcast(BF16)
        self.idx = idx


class Pool_:
    def __init__(self, items, name):
        self.free = list(items)
        self.name = name
        self.total = len(items)
        self.low = len(items)

    def alloc(self):
        if not self.free:
            raise RuntimeError("pool %s exhausted" % self.name)
        x = self.free.pop(0)
        self.low = min(self.low, len(self.free))
        return x

    def release(self, *xs):
        for x in xs:
            if isinstance(x, (list, tuple)):
                self.release(*x)
            else:
                assert x not in self.free
                self.free.append(x)


class _Stop(Exception):
    pass


def build_program(n_own_tiles=5, n_light_tiles=4, arena_slots=None, stop_stage=None):
    nc = bass.Bass("TRN2", target_bir_lowering=False)
    P = Prog()

    def din(name, shape):
        return nc.dram_tensor(name, list(shape), F32, kind="ExternalInput").ap()

    def dout(name, shape):
        return nc.dram_tensor(name, list(shape), F32, kind="ExternalOutput").ap()

    d_xl = din("xl", [NCH, 128, NLIGHT])
    d_xo = din("xo", [NCH, 128, NOWN])
    d_ckT = din("ckT", [4, 4, 128, 2048])
    d_cvr = din("cvr", [4, 4, 128, 16, 128])
    d_clf = din("clf", [4, 8, 2048])
    d_rcs = din("rcs", [128, 4, 4, 3])
    d_rh0 = din("rh0", [128, 4, 4])
    d_sfc = din("sfc", [2, 128, NHC, 4, 2])
    d_win0 = din("win0", [5, 128, 8, 512])
    d_wf = din("wf", [128, 8, 8])
    d_woa = din("woa", [2, 64, 4, 1024])
    d_wor = din("wor", [128, 4, 1024])
    d_wup = din("wup", [2, 11, 128, 8, 2, 256])
    d_wdn = din("wdn", [2, 8, 128, 22, 128])
    d_wio = din("wio", [4, 128, 8, 512])
    d_woo = din("woo", [2, 128, 8, 512])
    d_rgw = din("rgw", [2, 128, 4, 128])
    d_wst = din("wst", [128, 8, 128])
    d_wst64 = din("wst64", [128, 8, 64])
    d_sgb = din("sgb", [1, 8, 128])
    d_lnp = din("lnp", [128, 2, 2, 2, 8])
    d_fcw = din("fcw", [128, 2, NHC, 4])
    d_rgp = din("rgp", [128, 4, 8])
    d_bf = din("bfv", [8, 1])
    d_sgg = din("sgg", [1, 1024])
    d_sgbb = din("sgbb", [1, 1024])
    d_tri = din("tri", [128, 128])
    d_i8 = din("i8", [8, 8])
    d_flag = din("flag", [128, 1])
    d_pen = din("pen", [128, 1])

    o_y = dout("yT", [NCH, 128, NOWN])
    o_k = dout("koT", [4, 128, NOWN])
    o_v = dout("vo", [NOWN, 512])
    o_lf = dout("lfo", [8, NOWN])
    o_rc = dout("rco", [128, 4, 3])
    o_rh = dout("rho", [128, 4])
    o_rcs = dout("rcso", [128, 4, 4, 3])
    o_rhs = dout("rhso", [128, 4, 4])
    o_fcp = dout("fcpo", [2, 128, NHC, 2])
    o_fcs = dout("fcso", [2, 128, NHC, 4, 2])
    o_sgv = dout("sgvo", [256, 1024])

    es = ExitStack()

    def sb(name, shape, dt):
        return Buf(es.enter_context(nc.sbuf_tensor(name, list(shape), dt)), name)

    KT = sb("KT", [128, 4, 2048], BF16)
    KT2 = sb("KT2", [128, 4, 2048], BF16)
    VA = sb("VA", [128, 32, 8, 66], BF16)
    CK = sb("CK", [128, 32, 8], F32)
    VAS = [sb("VAS%d" % i, [128, 16, 2, 66], BF16) for i in range(1)]
    VNEW = sb("VNEW", [64, 4, 8, 66], BF16)
    XBH = sb("XBH", [128, 4, 3 + 512], F32)
    HCAR = sb("HCAR", [128, 4], F32)
    CUMC = sb("CUMC", [8, 1], F32)
    HC = sb("HC", [128, 2, NHC, 2], F32)
    HCS = sb("HCS", [128, 2, NHC, 4, 2], F32)
    SFC = sb("SFC", [128, 2, NHC, 4, 2], F32)
    HSL = sb("HSL", [128, 4, 4], F32)
    RH0 = sb("RH0", [128, 4, 4], F32)
    WF = sb("WF", [128, 8, 8], BF16)
    RGW = sb("RGW", [128, 2, 4, 128], BF16)
    WST = sb("WST", [128, 8, 128], BF16)
    WST64 = sb("WST64", [128, 8, 64], BF16)
    SGB1 = sb("SGB1", [1, 8, 128], F32)
    BHI = sb("BHI", [1, 8, 128], BF16)
    BLO = sb("BLO", [1, 8, 128], BF16)
    BTMP = sb("BTMP", [1, 8, 128], F32)
    SGG = sb("SGG", [128, 1024], F32)
    SGBB = sb("SGBB", [128, 1024], F32)
    LNP = sb("LNP", [128, 2, 2, 2, 8], F32)
    FCW = sb("FCW", [128, 2, NHC, 4], F32)
    RGP = sb("RGP", [128, 4, 8], F32)
    RGC = sb("RGC", [128, 4, 4], F32)
    NBF = sb("NBF", [8, 1], F32)
    TRI = sb("TRI", [128, 128], BF16)
    I8 = sb("I8", [8, 8], F32)
    FLAG = sb("FLAG", [128, 1], F32)
    PEN = sb("PEN", [128, 1], F32)
    ONESF = sb("ONESF", [128, 512], F32)
    ONESB = sb("ONESB", [128, 128], BF16)
    ONE1B = sb("ONE1B", [1, 128], BF16)
    CST = sb("CST", [128, 4], F32)
    WSL = [sb("WSL%d" % i, [128, 4096], BF16) for i in range(NWSLOT)]
    for i, w in enumerate(WSL):
        w.sem = P.dsem("wsl%d" % i)

    rem = nc.sbuf_bytes_remaining
    ns = arena_slots if arena_slots is not None else (rem - 1024) // 2048
    ARENA = es.enter_context(nc.sbuf_tensor("ARENA", [128, ns, 512], F32))
    print("arena slots", ns, "sbuf remaining", rem)
    SL = Pool_([Slot(ARENA[:, i, :], i) for i in range(ns)], "arena")
    for s in SL.free:
        s.sem = None
    PSB = []
    for i in range(8):
        t = es.enter_context(nc.psum_tensor("PS%d" % i, [128, 512], F32))
        PSB.append(Buf(t, "ps%d" % i))
    PS = Pool_(PSB, "psum")

    misc_sem = P.dsem("misc_ld")
    misc_sw = P.dsem("misc_sw")
    out_sem = [P.dsem("out%d" % i) for i in range(4)]
    _oc = [0]

    def osem():
        _oc[0] += 1
        return out_sem[_oc[0] % len(out_sem)]

    def slot_sem(s, sw=False):
        if sw:
            if getattr(s, "sem_sw", None) is None:
                s.sem_sw = P.dsem("sw%d" % s.idx)
            return s.sem_sw
        if s.sem is None:
            s.sem = P.dsem("sl%d" % s.idx)
        return s.sem

    def act(out, in_, func, reads, writes, bias=None, scale=None):
        if DEBUG_NO_GELU and func == AF.Gelu_apprx_tanh:
            func = AF.Copy
        kw = {}
        if bias is not None:
            kw["bias"] = bias
        if scale is not None:
            kw["scale"] = scale
        P.op("act", lambda e: e.activation(out=out, in_=in_, func=func, **kw), reads, writes)

    def tt(out, in0, in1, op, reads, writes, en="dve"):
        P.op(en, lambda e: e.tensor_tensor(out=out, in0=in0, in1=in1, op=op), reads, writes)

    def ts(out, in0, s1, s2, op0, op1, reads, writes, en="dve"):
        if s2 is None:
            P.op(en, lambda e: e.tensor_scalar(out=out, in0=in0, scalar1=s1, scalar2=None, op0=op0), reads, writes)
        else:
            P.op(en, lambda e: e.tensor_scalar(out=out, in0=in0, scalar1=s1, scalar2=s2, op0=op0, op1=op1),
                 reads, writes)

    def stt(out, in0, scalar, in1, op0, op1, reads, writes):
        P.op("dve", lambda e: e.scalar_tensor_tensor(out=out, in0=in0, scalar=scalar, in1=in1, op0=op0, op1=op1),
             reads, writes)

    def cp(out, in_, reads, writes, en="dve"):
        P.op(en, lambda e: e.tensor_copy(out=out, in_=in_), reads, writes)

    def mm(out, lhsT, rhs, reads, writes, start=True, stop=True):
        P.op("pe", lambda e: e.matmul(out, lhsT, rhs, start=start, stop=stop), reads, writes)

    def mmg(out, pairs, reads, writes):
        n = len(pairs)

        def fn(e):
            ins = None
            for i, (l, r) in enumerate(pairs):
                ins = e.matmul(out, l, r, start=(i == 0), stop=(i == n - 1))
            return ins

        P.op("pe", fn, reads, writes)

    def ld(q, out, in_, writes, sem, reads=()):
        P.dma(q, lambda e: e.dma_start(out=out, in_=in_), list(reads), list(writes), sem)

    def st(out, in_, reads, sem):
        if DEBUG_NO_STORES:
            return
        P.dma("sp", lambda e: e.dma_start(out=out, in_=in_), list(reads), [], sem)

    def ldm(buf, src, q="sp"):
        ld(q, buf.t[:], src, [buf.r()], misc_sw if q == "pool" else misc_sem)

    ldm(LNP, d_lnp)
    ldm(FCW, d_fcw)
    ldm(RGP, d_rgp)
    ldm(NBF, d_bf)
    ldm(I8, d_i8)
    ldm(FLAG, d_flag)
    ldm(PEN, d_pen)
    for l_ in range(2):
        ld("sp", SFC.t[:, l_, :, :, :], d_sfc[l_], [SFC.r()], misc_sem)
    ldm(RH0, d_rh0)
    ldm(WST, d_wst, q="pool")
    ldm(WST64, d_wst64, q="pool")
    ldm(SGB1, d_sgb)
    ldm(SGG, d_sgg.partition_broadcast(128))
    ldm(SGBB, d_sgbb.partition_broadcast(128))
    ldm(WF, d_wf, q="pool")
    for w_ in range(2):
        ld("pool", RGW.t[:, w_, :, :], d_rgw[w_], [RGW.r()], misc_sw)
    ldm(TRI, d_tri, q="pool")

    P.op("dve", lambda e: e.memset(ONESF.t[:], 1.0), [], [ONESF.r()])
    P.op("dve", lambda e: e.memset(ONESB.t[:], 1.0 / 1024.0), [], [ONESB.r()])
    P.op("dve", lambda e: e.memset(ONE1B.t[:], 1.0), [], [ONE1B.r()])
    P.op("dve", lambda e: e.memset(CST.t[:, 0:1], 1.0), [], [CST.r()])
    P.op("dve", lambda e: e.memset(CST.t[:, 1:2], EPS), [], [CST.r()])
    P.op("dve", lambda e: e.memset(CST.t[:, 2:3], 0.0), [], [CST.r()])
    P.op("dve", lambda e: e.memset(VA.t[:, :, :, 64:66], 1.0), [], [VA.r("ones")])
    for i in range(1):
        P.op("dve", lambda e, i=i: e.memset(VAS[i].t[:, :, :, 64:66], 1.0), [], [VAS[i].r("ones")])
    P.op("dve", lambda e: e.memset(VNEW.t[:, :, :, 64:66], 1.0), [], [VNEW.r("ones")])
    P.op("dve", lambda e: e.memset(XBH.t[:, :, 0:3], 0.0), [], [XBH.r()])
    P.op("dve", lambda e: e.memset(HCAR.t[:], 0.0), [], [HCAR.r()])
    P.op("dve", lambda e: e.memset(CUMC.t[:], 0.0), [], [CUMC.r()])
    P.op("dve", lambda e: e.memset(HC.t[:], 0.0), [], [HC.r(0), HC.r(1)])
    ONE = CST.t[:, 0:1]
    EPSC = CST.t[:, 1:2]
    ts(NBF.t[:], NBF.t[:], -1.0, None, ALU.mult, None, [NBF.r()], [NBF.r()])
    for g in range(8):
        tt(WST.t[:, g, :], WST.t[:, g, :], TRI.t[:, :], ALU.mult, [WST.r(), TRI.r()], [WST.r()])
        tt(WST64.t[0:64, g, :], WST64.t[0:64, g, :], TRI.t[0:64, 0:64], ALU.mult, [WST64.r(), TRI.r()],
           [WST64.r()])
        tt(WST64.t[64:128, g, :], WST64.t[64:128, g, :], TRI.t[64:128, 64:128], ALU.mult, [WST64.r(), TRI.r()],
           [WST64.r()])
    cp(BHI.t[:], SGB1.t[:], [SGB1.r()], [BHI.r()])
    tt(BTMP.t[:], SGB1.t[:], BHI.t[:], ALU.subtract, [SGB1.r(), BHI.r()], [BTMP.r()])
    cp(BLO.t[:], BTMP.t[:], [BTMP.r()], [BLO.r()])
    act(RGC.t[:, :, 2], RGP.t[:, :, 7], AF.Exp, [RGP.r()], [RGC.r()], scale=-1.0)
    act(RGC.t[:, :, 3], RGC.t[:, :, 2], AF.Ln, [RGC.r(), CST.r()], [RGC.r()], bias=ONE)
    ts(RGC.t[:, :, 0], RGC.t[:, :, 3], -8.0, None, ALU.mult, None, [RGC.r()], [RGC.r()])
    ts(RGC.t[:, :, 1], RGC.t[:, :, 3], -16.0, None, ALU.mult, None, [RGC.r()], [RGC.r()])
    ts(RGC.t[:, :, 2], RGP.t[:, :, 5], -1.0, None, ALU.mult, None, [RGP.r(), RGC.r()], [RGC.r()])
    ts(RGC.t[:, :, 3], RGP.t[:, :, 6], -1.0, None, ALU.mult, None, [RGP.r(), RGC.r()], [RGC.r()])

    blocks = []

    def v_k512(t):
        return t[:, 0:4096].rearrange("p (k n) -> p k n", k=8)

    def addblk(name, src, view):
        blocks.append((name, src, view))

    for li in range(n_light_tiles):
        for nm, bi in (("K", 1), ("V", 2), ("XB", 3)):
            addblk("L%d_%s" % (li, nm), d_win0[bi], v_k512)
    for ti in range(n_own_tiles):
        for nm, bi in (("K", 1), ("V", 2), ("Q", 0), ("GB", 4), ("XB", 3)):
            addblk("O%d_%s" % (ti, nm), d_win0[bi], v_k512)
        for i in range(2):
            addblk("O%d_WOA%d" % (ti, i), d_woa[i],
                   lambda t: t[0:64, 0:4096].rearrange("p (h n) -> p h n", h=4))
        addblk("O%d_WOR" % ti, d_wor, lambda t: t[:, 0:4096].rearrange("p (c n) -> p c n", c=4))
        for l in range(2):
            if l == 1:
                for i in range(4):
                    addblk("O%d_WIO%d" % (ti, i), d_wio[i], v_k512)
                for i in range(2):
                    addblk("O%d_WOO%d" % (ti, i), d_woo[i], v_k512)
            for g in range(11):
                addblk("O%d_UP%d_%d" % (ti, l, g), d_wup[l, g],
                       lambda t: t[:, 0:4096].rearrange("p (k u n) -> p k u n", k=8, u=2))
            for o in range(8):
                addblk("O%d_DN%d_%d" % (ti, l, o), d_wdn[l, o],
                       lambda t: t[:, 0:2816].rearrange("p (c n) -> p c n", c=22))

    class WS:
        nload = 0
        done = set()
        cur = {}

    def ws_pump():
        while WS.nload < len(blocks) and (WS.nload < NWSLOT or (WS.nload - NWSLOT) in WS.done):
            i = WS.nload
            name, src, view = blocks[i]
            slot = WSL[i % NWSLOT]
            ld("pool", view(slot.t), src, [slot.r()], slot.sem)
            WS.nload += 1

    def ws_get(name):
        ws_pump()
        for i in range(len(blocks)):
            if blocks[i][0] == name:
                break
        else:
            raise KeyError(name)
        assert i < WS.nload, "weight block %s not prefetched (ring too small)" % name
        slot = WSL[i % NWSLOT]
        WS.cur[name] = i
        return blocks[i][2](slot.t), slot.r()

    def ws_done(name):
        WS.done.add(WS.cur.pop(name))
        ws_pump()

    def xbf_ap(xbf, kc, a=0, b=512):
        return xbf[kc // 2].b[:, (kc % 2) * 512 + a:(kc % 2) * 512 + b]

    def load_xbf(src, c0, W):
        xbf = [SL.alloc() for _ in range(4)]
        for s in range(4):
            dst = xbf[s].b[:, 0:1024].rearrange("p (c t) -> p c t", c=2)[:, :, 0:W]
            ld("pool", dst, src[2 * s:2 * s + 2, :, c0:c0 + W].rearrange("c p t -> p c t"), [xbf[s].r()],
               slot_sem(xbf[s], True))
        return xbf

    def xr(xbf):
        return [s.r() for s in xbf]

    def forget_gate(xbf, W, segs, out_c0, lf_out=True):
        ps = PS.alloc()
        mmg(ps.t[0:8, 0:W], [(WF.t[:, kc, :], xbf_ap(xbf, kc, 0, W)) for kc in range(8)], xr(xbf) + [WF.r()],
            [ps.r()])
        t1 = SL.alloc()
        act(t1.f[0:8, 0:W], ps.t[0:8, 0:W], AF.Exp, [ps.r(), NBF.r()], [t1.r()], bias=NBF.t[:, 0:1], scale=-1.0)
        PS.release(ps)
        t2 = SL.alloc()
        act(t2.f[0:8, 0:W], t1.f[0:8, 0:W], AF.Ln, [t1.r(), CST.r()], [t2.r()], bias=CST.t[0:8, 0:1])
        lf = t1
        ts(lf.f[0:8, 0:W], t2.f[0:8, 0:W], -1.0, None, ALU.mult, None, [t2.r()], [lf.r()])
        if lf_out:
            st(o_lf[:, out_c0:out_c0 + W], lf.f[0:8, 0:W], [lf.r()], slot_sem(lf))
        cum = t2
        for kind, a, b in segs:
            if kind == "samp":
                for bb in range(4):
                    P.op("dve", lambda e, a=a, bb=bb: e.tensor_tensor_scan(
                        out=cum.f[0:8, a + bb * 64:a + bb * 64 + 64], data0=ONESF.t[0:8, 0:64],
                        data1=lf.f[0:8, a + bb * 64:a + bb * 64 + 64], initial=0.0, op0=ALU.mult, op1=ALU.add),
                        [lf.r(), ONESF.r()], [cum.r()])
            else:
                P.op("dve", lambda e, a=a, b=b: e.tensor_tensor_scan(
                    out=cum.f[0:8, a:b], data0=ONESF.t[0:8, 0:b - a], data1=lf.f[0:8, a:b],
                    initial=CUMC.t[0:8, 0:1], op0=ALU.mult, op1=ALU.add),
                    [lf.r(), ONESF.r(), CUMC.r()], [cum.r()])
                cp(CUMC.t[0:8, 0:1], cum.f[0:8, b - 1:b], [cum.r()], [CUMC.r()])
        return lf, cum

    def ck_from_cum(cum, a, b, jt0, add_pen):
        n = (b - a) // 128
        ps = PS.alloc()
        for i in range(n):
            mm(ps.t[:, i * 8:(i + 1) * 8], cum.f[0:8, a + i * 128:a + (i + 1) * 128], I8.t[:, :],
               [cum.r(), I8.r()], [ps.r()])
        dst = CK.t[:, jt0:jt0 + n, :]
        src = ps.t[:, 0:n * 8].rearrange("p (j h) -> p j h", h=8)
        regs = [CK.r(jt0 + i) for i in range(n)]
        if add_pen:
            ts(dst, src, PEN.t[:, 0:1], None, ALU.add, None, [ps.r(), PEN.r()], regs)
        else:
            cp(dst, src, [ps.r()], regs)
        PS.release(ps)

    def rglru_seg(kind, a, b, src_of, gb, hg, xbs=None):
        Wd = b - a
        samp = kind == "samp"

        def v2(ap):
            return ap.rearrange("p (b t) -> p b t", b=4) if samp else ap

        for grp_ in range(2):
            cs_ = (2 * grp_, 2 * grp_ + 1)
            xc, xcb, gr, gi, a2 = {}, {}, {}, {}, {}
            for c in cs_:
                hist, hreg = src_of(c)

                def tap(j):
                    return hist[:, :, j:j + 64] if samp else hist[:, j:j + Wd]

                t = SL.alloc()
                xc[c] = t
                o = v2(t.f[:, 0:Wd])
                act(o, tap(3), AF.Identity, [hreg, RGP.r()], [t.r()], bias=RGP.t[:, c, 4:5], scale=RGP.t[:, c, 3:4])
                for j in range(3):
                    stt(o, tap(j), RGP.t[:, c, j:j + 1], o, ALU.mult, ALU.add, [hreg, RGP.r(), t.r()], [t.r()])
                tb = SL.alloc()
                xcb[c] = tb
                act(tb.b[:, 0:Wd], t.f[:, 0:Wd], AF.Copy, [t.r()], [tb.r()])
            for c in cs_:
                for which, lst, bcol in ((0, gr, 5), (1, gi, 6)):
                    ps = PS.alloc()
                    mm(ps.t[:, 0:Wd], RGW.t[:, which, c, :], xcb[c].b[:, 0:Wd], [RGW.r(), xcb[c].r()], [ps.r()])
                    g = SL.alloc()
                    lst[c] = g
                    act(g.f[:, 0:Wd], ps.t[:, 0:Wd], AF.Exp, [ps.r(), RGC.r()], [g.r()], bias=RGC.t[:, c, 2 + which:3 + which],
                        scale=-1.0)
                    PS.release(ps)
                    ts(g.f[:, 0:Wd], g.f[:, 0:Wd], 1.0, None, ALU.add, None, [g.r()], [g.r()])
                    P.op("dve", lambda e, o_=g.f[:, 0:Wd]: e.reciprocal(out=o_, in_=o_), [g.r()], [g.r()])
                SL.release(xcb[c])
            if kind == 'halo' and grp_ == 0:
                stage(22)
            for c in cs_:
                t = SL.alloc()
                a2[c] = t
                act(t.f[:, 0:Wd], gr[c].f[:, 0:Wd], AF.Exp, [gr[c].r(), RGC.r()], [t.r()], scale=RGC.t[:, c, 1:2])
                act(gr[c].f[:, 0:Wd], gr[c].f[:, 0:Wd], AF.Exp, [gr[c].r(), RGC.r()], [gr[c].r()], scale=RGC.t[:, c, 0:1])
            for c in cs_:
                act(a2[c].f[:, 0:Wd], a2[c].f[:, 0:Wd], AF.Ln, [a2[c].r(), CST.r()], [a2[c].r()], bias=ONE, scale=-1.0)
                act(a2[c].f[:, 0:Wd], a2[c].f[:, 0:Wd], AF.Exp, [a2[c].r()], [a2[c].r()], scale=0.5)
            if kind == 'halo' and grp_ == 0:
                stage(23)
            for c in cs_:
                tt(gi[c].f[:, 0:Wd], gi[c].f[:, 0:Wd], xc[c].f[:, 0:Wd], ALU.mult, [gi[c].r(), xc[c].r()], [gi[c].r()])
                tt(gi[c].f[:, 0:Wd], gi[c].f[:, 0:Wd], a2[c].f[:, 0:Wd], ALU.mult, [gi[c].r(), a2[c].r()], [gi[c].r()])
                h = xc[c]
                if samp:
                    for bb in range(4):
                        P.op("dve", lambda e, o_=h.f[:, bb * 64:bb * 64 + 64], d0=gr[c].f[:, bb * 64:bb * 64 + 64],
                             d1=gi[c].f[:, bb * 64:bb * 64 + 64], i_=RH0.t[:, c, bb:bb + 1]: e.tensor_tensor_scan(
                            out=o_, data0=d0, data1=d1, initial=i_,
                            op0=ALU.mult, op1=ALU.add), [gr[c].r(), gi[c].r(), RH0.r()], [h.r()])
                    cp(HSL.t[:, c, :], h.f[:, 0:256].rearrange("p (b t) -> p b t", b=4)[:, :, 63], [h.r()], [HSL.r()])
                else:
                    P.op("dve", lambda e, o_=h.f[:, 0:Wd], d0=gr[c].f[:, 0:Wd], d1=gi[c].f[:, 0:Wd],
                         i_=HCAR.t[:, c:c + 1]: e.tensor_tensor_scan(
                        out=o_, data0=d0, data1=d1, initial=i_, op0=ALU.mult, op1=ALU.add),
                        [gr[c].r(), gi[c].r(), HCAR.r()], [h.r()])
                    cp(HCAR.t[:, c:c + 1], h.f[:, Wd - 1:Wd], [h.r()], [HCAR.r()])
                if hg is not None:
                    tt(hg[c // 2].b[:, (c % 2) * 512 + a:(c % 2) * 512 + b], h.f[:, 0:Wd], gb[c].f[:, a:b], ALU.mult,
                       [h.r(), gb[c].r()], [hg[c // 2].r()])
                SL.release(xc[c], gr[c], gi[c], a2[c])
            if kind == 'halo' and grp_ == 0:
                stage(24)


    def xbh_src(a):
        def f(c):
            return XBH.t[:, c, a:a + 3 + 512], XBH.r()
        return f

    def attend(q_ap_of, Wq, qblocks, keys, att_dst, att_reg):
        acc = PS.alloc()
        n = len(keys)
        LOOK = 4
        sts = {}

        def qk(i):
            k = keys[i]
            sps = PS.alloc()
            cs = k["cstart"]
            qap, qreg = q_ap_of(cs, Wq)
            mm(sps.t[0:k["nk"], cs:Wq], k["lhsT_k"], qap, [k["k_reg"], qreg], [sps.r()])
            sts[i] = sps

        for i in range(min(LOOK, n)):
            qk(i)
        for i in range(n):
            if i + LOOK < n:
                qk(i + LOOK)
            k = keys[i]
            nk, cs = k["nk"], k["cstart"]
            sps = sts.pop(i)
            pt = SL.alloc()
            for (lo, hi, bias_fn) in qblocks:
                lo2 = max(lo, cs)
                if lo2 >= hi:
                    continue
                bap, breg = bias_fn(k)
                act(pt.b[0:nk, lo2:hi], sps.t[0:nk, lo2:hi], AF.Exp, [sps.r(), breg], [pt.r()], bias=bap, scale=0.125)
            PS.release(sps)
            if k["tri"]:
                tt(pt.b[0:nk, cs:cs + nk], pt.b[0:nk, cs:cs + nk], TRI.t[0:nk, 0:nk], ALU.mult, [pt.r(), TRI.r()],
                   [pt.r()])
            mm(acc.t[0:65, cs:Wq], k["lhsT_v"], pt.b[0:nk, cs:Wq], [k["v_reg"], pt.r()] + k.get("v_extra", []),
               [acc.r()], start=(i == 0), stop=(i == n - 1))
            SL.release(pt)
        rd = SL.alloc()
        P.op("dve", lambda e: e.reciprocal(out=rd.f[64:65, 0:Wq], in_=acc.t[64:65, 0:Wq]), [acc.r()], [rd.r()])
        bc = PS.alloc()
        mm(bc.t[0:64, 0:Wq], ONESF.t[64:65, 0:64], rd.f[64:65, 0:Wq], [ONESF.r(), rd.r()], [bc.r()])
        rb = SL.alloc()
        act(rb.f[0:64, 0:Wq], bc.t[0:64, 0:Wq], AF.Copy, [bc.r()], [rb.r()])
        PS.release(bc)
        tt(att_dst, acc.t[0:64, 0:Wq], rb.f[0:64, 0:Wq], ALU.mult, [acc.r(), rb.r()], [att_reg])
        PS.release(acc)
        SL.release(rd, rb)

    def layer_norm(xres, l, which, W=512):
        psm = PS.alloc()
        psq = PS.alloc()
        for c in range(8):
            t = SL.alloc()
            act(t.b[:, 0:W], xres[c].f[:, 0:W], AF.Copy, [xres[c].r()], [t.r()])
            act(t.b[:, 512:512 + W], xres[c].f[:, 0:W], AF.Square, [xres[c].r()], [t.r()])
            mm(psm.t[:, 0:W], ONESB.t[:, :], t.b[:, 0:W], [ONESB.r(), t.r()], [psm.r()], start=(c == 0), stop=(c == 7))
            mm(psq.t[:, 0:W], ONESB.t[:, :], t.b[:, 512:512 + W], [ONESB.r(), t.r()], [psq.r()], start=(c == 0),
               stop=(c == 7))
            SL.release(t)
        mean = SL.alloc()
        msq = SL.alloc()
        act(mean.f[:, 0:W], psm.t[:, 0:W], AF.Copy, [psm.r()], [mean.r()])
        act(msq.f[:, 0:W], psm.t[:, 0:W], AF.Square, [psm.r()], [msq.r()])
        PS.release(psm)
        tt(msq.f[:, 0:W], psq.t[:, 0:W], msq.f[:, 0:W], ALU.subtract, [psq.r(), msq.r()], [msq.r()])
        PS.release(psq)
        ts(msq.f[:, 0:W], msq.f[:, 0:W], 0.0, EPS, ALU.max, ALU.add, [msq.r()], [msq.r()])
        act(msq.f[:, 0:W], msq.f[:, 0:W], AF.Ln, [msq.r()], [msq.r()])
        act(msq.f[:, 0:W], msq.f[:, 0:W], AF.Exp, [msq.r()], [msq.r()], scale=-0.5)
        xbf = [SL.alloc() for _ in range(4)]
        for c in range(8):
            t = SL.alloc()
            tt(t.f[:, 0:W], xres[c].f[:, 0:W], mean.f[:, 0:W], ALU.subtract, [xres[c].r(), mean.r()], [t.r()])
            tt(t.f[:, 0:W], t.f[:, 0:W], msq.f[:, 0:W], ALU.mult, [t.r(), msq.r()], [t.r()])
            g = LNP.t[:, l, which, 0, c:c + 1]
            bb = LNP.t[:, l, which, 1, c:c + 1]
            act(xres[c].f[:, 0:W], t.f[:, 0:W], AF.Identity, [t.r(), LNP.r()], [xres[c].r()], bias=bb, scale=g)
            act(xbf_ap(xbf, c, 0, W), t.f[:, 0:W], AF.Identity, [t.r(), LNP.r()], [xbf[c // 2].r()], bias=bb, scale=g)
            SL.release(t)
        SL.release(mean, msq)
        return xbf

    def conv_seg(ps, T, l, ch, kind, a, b, carry_in, save_carry):
        wv = lambda j: FCW.t[:, l, ch, j:j + 1]
        act(T.f[:, a:b], ps.t[:, a:b], AF.Identity, [ps.r(), FCW.r()], [T.r()], bias=wv(3), scale=wv(2))
        if kind == "samp":
            p3 = ps.t[:, a:b].rearrange("p (b t) -> p b t", b=4)
            t3 = T.f[:, a:b].rearrange("p (b t) -> p b t", b=4)
            stt(t3[:, :, 1:64], p3[:, :, 0:63], wv(1), t3[:, :, 1:64], ALU.mult, ALU.add, [ps.r(), FCW.r(), T.r()], [T.r()])
            stt(t3[:, :, 2:64], p3[:, :, 0:62], wv(0), t3[:, :, 2:64], ALU.mult, ALU.add, [ps.r(), FCW.r(), T.r()], [T.r()])
            s3 = SFC.t[:, l, ch, :, :]
            stt(t3[:, :, 0:2], s3[:, :, 0:2], wv(0), t3[:, :, 0:2], ALU.mult, ALU.add, [SFC.r(), FCW.r(), T.r()], [T.r()])
            stt(t3[:, :, 0:1], s3[:, :, 1:2], wv(1), t3[:, :, 0:1], ALU.mult, ALU.add, [SFC.r(), FCW.r(), T.r()], [T.r()])
            cp(HCS.t[:, l, ch, :, :], p3[:, :, 62:64], [ps.r()], [HCS.r(l)])
        else:
            stt(T.f[:, a + 1:b], ps.t[:, a:b - 1], wv(1), T.f[:, a + 1:b], ALU.mult, ALU.add, [ps.r(), FCW.r(), T.r()], [T.r()])
            stt(T.f[:, a + 2:b], ps.t[:, a:b - 2], wv(0), T.f[:, a + 2:b], ALU.mult, ALU.add, [ps.r(), FCW.r(), T.r()], [T.r()])
            if carry_in:
                hc = HC.t[:, l, ch, :]
                stt(T.f[:, a:a + 2], hc[:, 0:2], wv(0), T.f[:, a:a + 2], ALU.mult, ALU.add, [HC.r(l), FCW.r(), T.r()], [T.r()])
                stt(T.f[:, a:a + 1], hc[:, 1:2], wv(1), T.f[:, a:a + 1], ALU.mult, ALU.add, [HC.r(l), FCW.r(), T.r()], [T.r()])
            if save_carry:
                cp(HC.t[:, l, ch, :], ps.t[:, b - 2:b], [ps.r()], [HC.r(l)])

    def ffn(tname, xres, xbf, l, segs):
        M = [SL.alloc() for _ in range(11)]
        for g in range(11):
            wv, wreg = ws_get("%s_UP%d_%d" % (tname, l, g))
            for cc in range(2):
                c = g * 2 + cc
                tg = SL.alloc()
                tu = SL.alloc()
                for gu, T in ((0, tg), (1, tu)):
                    ps = PS.alloc()
                    mmg(ps.t[:, 0:512], [(wv[:, kc, gu, cc * 128:(cc + 1) * 128], xbf_ap(xbf, kc)) for kc in range(8)],
                        xr(xbf) + [wreg], [ps.r()])
                    for (kind, a, b, cin, csave) in segs:
                        conv_seg(ps, T, l, gu * 22 + c, kind, a, b, cin, csave)
                    PS.release(ps)
                act(tg.f[:, 0:512], tg.f[:, 0:512], AF.Gelu_apprx_tanh, [tg.r()], [tg.r()])
                tt(M[c // 2].b[:, (c % 2) * 512:(c % 2) * 512 + 512], tg.f[:, 0:512], tu.f[:, 0:512], ALU.mult,
                   [tg.r(), tu.r()], [M[c // 2].r()])
                SL.release(tg, tu)
            ws_done("%s_UP%d_%d" % (tname, l, g))
        SL.release(xbf)
        for o in range(8):
            wv, wreg = ws_get("%s_DN%d_%d" % (tname, l, o))
            ps = PS.alloc()
            mmg(ps.t[:, 0:512], [(wv[:, c, :], M[c // 2].b[:, (c % 2) * 512:(c % 2) * 512 + 512]) for c in range(22)],
                [m.r() for m in M] + [wreg], [ps.r()])
            ws_done("%s_DN%d_%d" % (tname, l, o))
            stt(xres[o].f[:, 0:512], xres[o].f[:, 0:512], ALPHA, ps.t[:, 0:512], ALU.mult, ALU.add,
                [xres[o].r(), ps.r()], [xres[o].r()])
            PS.release(ps)
        SL.release(M)

    light = [(0, 512), (512, 512), (1024, 512), (1536, 256)][:n_light_tiles]
    nxt_xbf = load_xbf(d_xl, 0, 512) if light else None
    for li, (k0, W) in enumerate(light):
        tn = "L%d" % li
        xbf = nxt_xbf
        if li + 1 < len(light):
            nxt_xbf = load_xbf(d_xl, light[li + 1][0], light[li + 1][1])
        jt0 = k0 // 128
        nst = W // 128
        wv, wreg = ws_get(tn + "_K")
        for p in range(4):
            ps = PS.alloc()
            mmg(ps.t[:, 0:W], [(wv[:, kc, p * 128:(p + 1) * 128], xbf_ap(xbf, kc, 0, W)) for kc in range(8)],
                xr(xbf) + [wreg], [ps.r()])
            act(KT.t[:, p, k0:k0 + W], ps.t[:, 0:W], AF.Copy, [ps.r()], [KT.r(jt0 + i) for i in range(nst)])
            PS.release(ps)
        ws_done(tn + "_K")
        wv, wreg = ws_get(tn + "_V")
        for s_ in range(nst):
            ps = PS.alloc()
            mmg(ps.t[:, 0:512], [(xbf_ap(xbf, kc, s_ * 128, s_ * 128 + 128), wv[:, kc, :]) for kc in range(8)],
                xr(xbf) + [wreg], [ps.r()])
            act(VA.t[:, jt0 + s_, :, 0:64], ps.t[:, 0:512].rearrange("p (h d) -> p h d", h=8), AF.Copy, [ps.r()],
                [VA.r(jt0 + s_)])
            PS.release(ps)
        ws_done(tn + "_V")
        lf, cum = forget_gate(xbf, W, [("light", 0, W)], 0, lf_out=False)
        ck_from_cum(cum, 0, W, jt0, add_pen=True)
        SL.release(lf, cum)
        wv, wreg = ws_get(tn + "_XB")
        for c in range(4):
            ps = PS.alloc()
            mmg(ps.t[:, 0:W], [(wv[:, kc, c * 128:(c + 1) * 128], xbf_ap(xbf, kc, 0, W)) for kc in range(8)],
                xr(xbf) + [wreg], [ps.r()])
            act(XBH.t[:, c, 3:3 + W], ps.t[:, 0:W], AF.Copy, [ps.r()], [XBH.r()])
            PS.release(ps)
        ws_done(tn + "_XB")
        SL.release(xbf)
        rglru_seg("light", 0, W, xbh_src(0), None, None)
        cp(XBH.t[:, :, 0:3], XBH.t[:, :, W:W + 3], [XBH.r()], [XBH.r()])

    _cur_tile = [0]

    def stage(k):
        if stop_stage is None:
            return
        if isinstance(stop_stage, tuple):
            if (_cur_tile[0], k) == stop_stage:
                raise _Stop()
        elif k == stop_stage:
            raise _Stop()

    nxt_xbf = load_xbf(d_xo, 0, 512) if n_own_tiles else None
    try:
      for ti in range(n_own_tiles):
          tn = "O%d" % ti
          _cur_tile[0] = ti
          stage(100)
          if DEBUG_ACTPROBE and ti >= 1:
              act(CST.t[:, 3:4], CST.t[:, 2:3], AF.Copy, [CST.r()], [CST.r()])
          c0 = ti * 512
          first = ti == 0
          last = ti == 4
          xbf = nxt_xbf
          if first:
              psegs = [("halo", 0, 256, 1792)]
          else:
              psegs = [("own", 0, 512, 2048 + (ti - 1) * 512)]

          wv, wreg = ws_get(tn + "_K")
          knew = SL.alloc() if first else None
          for p in range(4):
              ps = PS.alloc()
              mmg(ps.t[:, 0:512], [(wv[:, kc, p * 128:(p + 1) * 128], xbf_ap(xbf, kc)) for kc in range(8)],
                  xr(xbf) + [wreg], [ps.r()])
              for (kind, a, b, k0) in psegs:
                  if DEBUG_SKIPKT and ti >= 1:
                      continue
                  if kind == "own":
                      for hf in range(2):
                          cp(KT2.t[:, p, k0 - 2048 + hf * 256:k0 - 2048 + hf * 256 + 256],
                             ps.t[:, a + hf * 256:a + hf * 256 + 256],
                             [ps.r()], [KT2.r(k0 // 128 + hf * 2), KT2.r(k0 // 128 + hf * 2 + 1)])
                  else:
                      act(KT.t[:, p, k0:k0 + (b - a)], ps.t[:, a:b], AF.Copy, [ps.r()],
                          [KT.r(k0 // 128 + i) for i in range((b - a) // 128)])
              if first:
                  act(knew.b[:, p * 256:(p + 1) * 256], ps.t[:, 256:512], AF.Copy, [ps.r()], [knew.r()])
              ko = SL.alloc()
              if not (DEBUG_SKIPKO and ti >= 1):
                  cp(ko.f[:, 0:512], ps.t[:, 0:512], [ps.r()], [ko.r()])
              PS.release(ps)
              st(o_k[p, :, c0:c0 + 512], ko.f[:, 0:512], [ko.r()], slot_sem(ko))
              SL.release(ko)
          ws_done(tn + "_K")
          stage(101)

          wv, wreg = ws_get(tn + "_V")
          for s_ in range(4):
              ps = PS.alloc()
              mmg(ps.t[:, 0:512], [(xbf_ap(xbf, kc, s_ * 128, s_ * 128 + 128), wv[:, kc, :]) for kc in range(8)],
                  xr(xbf) + [wreg], [ps.r()])
              if not (first and s_ >= 2):
                  k0 = psegs[0][3] + s_ * 128
                  if first:
                      act(VA.t[:, k0 // 128, :, 0:64], ps.t[:, 0:512].rearrange("p (h d) -> p h d", h=8), AF.Copy,
                          [ps.r()], [VA.r(k0 // 128)])
                  else:
                      cp(VA.t[:, k0 // 128, :, 0:64], ps.t[:, 0:512].rearrange("p (h d) -> p h d", h=8),
                         [ps.r()], [VA.r(k0 // 128)])
              vo = SL.alloc()
              cp(vo.f[:, 0:512], ps.t[:, 0:512], [ps.r()], [vo.r()])
              PS.release(ps)
              st(o_v[c0 + s_ * 128:c0 + s_ * 128 + 128, :], vo.f[:, 0:512], [vo.r()], slot_sem(vo))
              SL.release(vo)
          if first:
              for bb in range(4):
                  ps = PS.alloc()
                  mmg(ps.t[0:64, 0:512], [(xbf_ap(xbf, kc, 256 + bb * 64, 320 + bb * 64), wv[:, kc, :]) for kc in range(8)],
                      xr(xbf) + [wreg], [ps.r()])
                  act(VNEW.t[:, bb, :, 0:64], ps.t[0:64, 0:512].rearrange("p (h d) -> p h d", h=8), AF.Copy,
                      [ps.r()], [VNEW.r(bb)])
                  PS.release(ps)
          ws_done(tn + "_V")
          stage(102)

          if first:
              fsegs = [("halo", 0, 256), ("samp", 256, 512)]
          else:
              fsegs = [("own", 0, 512)]
          lf, cum = forget_gate(xbf, 512, fsegs, c0)
          SL.release(lf)
          cref = SL.alloc()
          ckn = None
          if first:
              ck_from_cum(cum, 0, 256, 14, add_pen=False)
              qstarts = [0]
          else:
              ck_from_cum(cum, 0, 512, psegs[0][3] // 128, add_pen=False)
              qstarts = [0, 256]
          ps = PS.alloc()
          for qi, q0 in enumerate(qstarts):
              tmp = SL.alloc()
              ts(tmp.f[0:8, 0:128], ONESF.t[0:8, 0:128], cum.f[0:8, q0:q0 + 1], None, ALU.mult, None,
                 [ONESF.r(), cum.r()], [tmp.r()])
              mm(ps.t[:, qi * 8:(qi + 1) * 8], tmp.f[0:8, 0:128], I8.t[:, :], [tmp.r(), I8.r()], [ps.r()])
              SL.release(tmp)
          cp(cref.f[:, 0:8 * len(qstarts)], ps.t[:, 0:8 * len(qstarts)], [ps.r()], [cref.r()])
          PS.release(ps)
          if first:
              ckn = SL.alloc()
              ps = PS.alloc()
              for bb in range(4):
                  mm(ps.t[0:64, bb * 8:(bb + 1) * 8], cum.f[0:8, 256 + bb * 64:320 + bb * 64], I8.t[:, :],
                     [cum.r(), I8.r()], [ps.r()])
              ts(ckn.f[0:64, 0:32], ps.t[0:64, 0:32], -1.0, None, ALU.mult, None, [ps.r()], [ckn.r()])
              PS.release(ps)
          SL.release(cum)

          stage(103)
          wv, wreg = ws_get(tn + "_Q")
          qt = [SL.alloc() for _ in range(2)]
          for p in range(4):
              ps = PS.alloc()
              mmg(ps.t[:, 0:512], [(wv[:, kc, p * 128:(p + 1) * 128], xbf_ap(xbf, kc)) for kc in range(8)],
                  xr(xbf) + [wreg], [ps.r()])
              if first:
                  act(qt[p // 2].b[:, (p % 2) * 512:(p % 2) * 512 + 512], ps.t[:, 0:512], AF.Copy, [ps.r()],
                      [qt[p // 2].r()])
              else:
                  cp(qt[p // 2].b[:, (p % 2) * 512:(p % 2) * 512 + 512], ps.t[:, 0:512], [ps.r()], [qt[p // 2].r()])
              PS.release(ps)
          ws_done(tn + "_Q")

          stage(104)
          wv, wreg = ws_get(tn + "_GB")
          gb = [SL.alloc() for _ in range(4)]
          for c in range(4):
              ps = PS.alloc()
              mmg(ps.t[:, 0:512], [(wv[:, kc, c * 128:(c + 1) * 128], xbf_ap(xbf, kc)) for kc in range(8)],
                  xr(xbf) + [wreg], [ps.r()])
              act(gb[c].f[:, 0:512], ps.t[:, 0:512], AF.Gelu_apprx_tanh, [ps.r()], [gb[c].r()])
              PS.release(ps)
          ws_done(tn + "_GB")

          stage(105)
          wv, wreg = ws_get(tn + "_XB")
          xbs = None
          if first:
              xbs = [SL.alloc() for _ in range(4)]
              for c in range(4):
                  ld("sp", xbs[c].f[:, 0:268].rearrange("p (b t) -> p b t", b=4)[:, :, 0:3], d_rcs[:, c, :, :],
                     [xbs[c].r()], slot_sem(xbs[c]))
          for c in range(4):
              ps = PS.alloc()
              mmg(ps.t[:, 0:512], [(wv[:, kc, c * 128:(c + 1) * 128], xbf_ap(xbf, kc)) for kc in range(8)],
                  xr(xbf) + [wreg], [ps.r()])
              if first:
                  act(XBH.t[:, c, 3:3 + 256], ps.t[:, 0:256], AF.Copy, [ps.r()], [XBH.r()])
                  act(xbs[c].f[:, 0:268].rearrange("p (b t) -> p b t", b=4)[:, :, 3:67],
                      ps.t[:, 256:512].rearrange("p (b t) -> p b t", b=4), AF.Copy, [ps.r()], [xbs[c].r()])
              else:
                  cp(XBH.t[:, c, 3:3 + 512], ps.t[:, 0:512], [ps.r()], [XBH.r()])
              PS.release(ps)
          ws_done(tn + "_XB")
          SL.release(xbf)

          stage(1)
          hg = [SL.alloc() for _ in range(2)]
          if first:
              rglru_seg("halo", 0, 256, xbh_src(0), gb, hg)
              cp(XBH.t[:, :, 0:3], XBH.t[:, :, 256:259], [XBH.r()], [XBH.r()])
              ts(HCAR.t[:, :], HCAR.t[:, :], FLAG.t[:, 0:1], None, ALU.mult, None, [HCAR.r(), FLAG.r()], [HCAR.r()])

              def xbs_src(c):
                  return xbs[c].f[:, 0:268].rearrange("p (b t) -> p b t", b=4), xbs[c].r()
              stage(13)
              rglru_seg("samp", 256, 512, xbs_src, gb, hg)
              stage(16)
              for c in range(4):
                  st(o_rcs[:, c, :, :], xbs[c].f[:, 0:268].rearrange("p (b t) -> p b t", b=4)[:, :, 64:67],
                     [xbs[c].r()], slot_sem(xbs[c]))
              st(o_rhs[:, :, :], HSL.t[:, :, :], [HSL.r()], osem())
              SL.release(xbs)
          else:
              rglru_seg("own", 0, 512, xbh_src(0), gb, hg)
              if last:
                  st(o_rc[:, :, :], XBH.t[:, :, 512:515], [XBH.r()], osem())
                  st(o_rh[:, :], HCAR.t[:, :], [HCAR.r()], osem())
              else:
                  cp(XBH.t[:, :, 0:3], XBH.t[:, :, 512:515], [XBH.r()], [XBH.r()])
          SL.release(gb)

          stage(2)
          att = [SL.alloc() for _ in range(4)]
          for (kind, a, b, k0) in psegs:
              Wq = b - a
              jd0 = k0 // 128
              nj = jd0 + Wq // 128
              nqb = Wq // 256
              for h in range(8):
                  p, r0 = h // 2, (h % 2) * 64
                  bias = SL.alloc()
                  for qi in (nqb - 1,):
                      ts(bias.f[:, qi * 32:qi * 32 + nj], CK.t[:, 0:nj, h], -1.0, cref.f[:, qi * 8 + h:qi * 8 + h + 1],
                         ALU.mult, ALU.add, [CK.r(j) for j in range(nj)] + [cref.r()], [bias.r()])
                  keys = []
                  for j in range(nj):
                      dd = j - jd0
                      ktt = KT if j < 16 else KT2
                      jc = j if j < 16 else j - 16
                      keys.append(dict(lhsT_k=ktt.t[r0:r0 + 64, p, jc * 128:(jc + 1) * 128], k_reg=ktt.r(j),
                                       lhsT_v=VA.t[:, j, h, 0:65], v_reg=VA.r(j), v_extra=[VA.r("ones")], nk=128,
                                       cstart=max(0, dd * 128), tri=dd >= 0, j=j))
                  qbl = []
                  for qi in (nqb - 1,):
                      def bf(k, qi=qi, bias=bias):
                          return bias.f[:, qi * 32 + k["j"]:qi * 32 + k["j"] + 1], bias.r()
                      qbl.append((0, Wq, bf))

                  def qap(lo, hi, p=p, r0=r0, a=a):
                      return qt[p // 2].b[r0:r0 + 64, (p % 2) * 512 + a + lo:(p % 2) * 512 + a + hi], qt[p // 2].r()
                  attend(qap, Wq, qbl, keys, att[h // 2].b[0:64, (h % 2) * 512 + a:(h % 2) * 512 + b], att[h // 2].r())
                  SL.release(bias)
          stage(3)
          if first:
              ts(CK.t[:, 14:16, :], CK.t[:, 14:16, :], PEN.t[:, 0:1], None, ALU.add, None, [CK.r(14), CK.r(15), PEN.r()],
                 [CK.r(14), CK.r(15)])
              for bb in range(4):
                  lfr = [SL.alloc() for _ in range(4)]
                  for i in range(4):
                      ld("sp", lfr[i].f[0:8, 0:512], d_clf[bb, :, i * 512:(i + 1) * 512], [lfr[i].r()], slot_sem(lfr[i]))
                  cks = SL.alloc()
                  ps = PS.alloc()
                  for i in range(4):
                      rr = SL.alloc()
                      init = 0.0 if i == 0 else prev.f[0:8, 511:512]
                      rds = [lfr[i].r(), ONESF.r()] + ([] if i == 0 else [prev.r()])
                      P.op("dve", lambda e, o_=rr.f[0:8, 0:512], d1=lfr[i].f[0:8, 0:512], init=init: e.tensor_tensor_scan(
                          out=o_, data0=ONESF.t[0:8, 0:512], data1=d1, initial=init,
                          op0=ALU.mult, op1=ALU.add), rds, [rr.r()])
                      tt(lfr[i].f[0:8, 0:512], rr.f[0:8, 0:512], lfr[i].f[0:8, 0:512], ALU.subtract, [rr.r(), lfr[i].r()],
                         [lfr[i].r()])
                      for jj in range(4):
                          j = i * 4 + jj
                          mm(ps.t[:, j * 8:(j + 1) * 8], lfr[i].f[0:8, jj * 128:(jj + 1) * 128], I8.t[:, :],
                             [lfr[i].r(), I8.r()], [ps.r()])
                      if i > 0:
                          SL.release(prev)
                      prev = rr
                  SL.release(prev)
                  cp(cks.f[:, 0:128], ps.t[:, 0:128], [ps.r()], [cks.r()])
                  PS.release(ps)
                  SL.release(lfr)
                  for p in range(4):
                      kts = [SL.alloc() for _ in range(2)]
                      for i in range(2):
                          ld("pool", kts[i].b[:, 0:1024], d_ckT[bb, p, :, i * 1024:(i + 1) * 1024], [kts[i].r()],
                             slot_sem(kts[i], True))
                      vst = [SL.alloc() for _ in range(2)]
                      vas = VAS[0]
                      for i in range(2):
                          ld("pool", vst[i].b[:, 0:1024].rearrange("p (j n) -> p j n", j=8),
                             d_cvr[bb, p, :, i * 8:(i + 1) * 8, :], [vst[i].r()], slot_sem(vst[i], True))
                          cp(vas.t[:, i * 8:(i + 1) * 8, :, 0:64],
                             vst[i].b[:, 0:1024].rearrange("p (j h d) -> p j h d", j=8, h=2), [vst[i].r()], [vas.r()],
                             en="dve")
                      SL.release(vst)
                      for hh_ in range(2):
                          h = 2 * p + hh_
                          r0 = hh_ * 64
                          keys = []
                          for j in range(16):
                              keys.append(dict(lhsT_k=kts[j // 8].b[r0:r0 + 64, (j % 8) * 128:(j % 8 + 1) * 128],
                                               k_reg=kts[j // 8].r(), lhsT_v=vas.t[:, j, hh_, 0:65], v_reg=vas.r(),
                                               v_extra=[vas.r("ones")], nk=128, cstart=0, tri=False, j=j))
                          keys.append(dict(lhsT_k=knew.b[r0:r0 + 64, p * 256 + bb * 64:p * 256 + bb * 64 + 64],
                                           k_reg=knew.r(), lhsT_v=VNEW.t[:, bb, h, 0:65], v_reg=VNEW.r(bb),
                                           v_extra=[VNEW.r("ones")], nk=64, cstart=0, tri=True, j=16))

                          def bf(k, h=h, bb=bb, cks=cks):
                              if k["j"] < 16:
                                  return cks.f[:, k["j"] * 8 + h:k["j"] * 8 + h + 1], cks.r()
                              return ckn.f[0:64, bb * 8 + h:bb * 8 + h + 1], ckn.r()

                          def qap(lo, hi, p=p, r0=r0, bb=bb):
                              o = (p % 2) * 512 + 256 + bb * 64
                              return qt[p // 2].b[r0:r0 + 64, o + lo:o + hi], qt[p // 2].r()
                          o = (h % 2) * 512 + 256 + bb * 64
                          attend(qap, 64, [(0, 64, bf)], keys, att[h // 2].b[0:64, o:o + 64], att[h // 2].r())
                      SL.release(kts)
                  SL.release(cks)
              SL.release(knew, ckn)
          SL.release(qt, cref)

          stage(4)
          xres = [SL.alloc() for _ in range(8)]
          for c in range(8):
              ld("sp", xres[c].f[:, 0:512], d_xo[c, :, c0:c0 + 512], [xres[c].r()], slot_sem(xres[c]))
          wa0, ra0 = ws_get(tn + "_WOA0")
          wa1, ra1 = ws_get(tn + "_WOA1")
          wr_, rr_ = ws_get(tn + "_WOR")
          for o in range(8):
              ps = PS.alloc()
              pairs = []
              for h in range(8):
                  wa = wa0 if h < 4 else wa1
                  pairs.append((wa[:, h % 4, o * 128:(o + 1) * 128], att[h // 2].b[0:64, (h % 2) * 512:(h % 2) * 512 + 512]))
              for c in range(4):
                  pairs.append((wr_[:, c, o * 128:(o + 1) * 128], hg[c // 2].b[:, (c % 2) * 512:(c % 2) * 512 + 512]))
              mmg(ps.t[:, 0:512], pairs, [ra0, ra1, rr_] + [x.r() for x in att] + [x.r() for x in hg], [ps.r()])
              stt(xres[o].f[:, 0:512], xres[o].f[:, 0:512], ALPHA, ps.t[:, 0:512], ALU.mult, ALU.add,
                  [xres[o].r(), ps.r()], [xres[o].r()])
              PS.release(ps)
          ws_done(tn + "_WOA0")
          ws_done(tn + "_WOA1")
          ws_done(tn + "_WOR")
          SL.release(att, hg)

          if first:
              csegs = [("halo", 0, 256, False, True), ("samp", 256, 512, False, False)]
          else:
              csegs = [("own", 0, 512, True, True)]

          stage(5)
          xbf = layer_norm(xres, 0, 0)
          stage(6)
          ffn(tn, xres, xbf, 0, csegs)
          stage(7)
          if first:
              ts(HC.t[:, 0, :, :], HC.t[:, 0, :, :], FLAG.t[:, 0:1], None, ALU.mult, None, [HC.r(0), FLAG.r()], [HC.r(0)])
          xbf = layer_norm(xres, 0, 1)

          stage(8)
          U = [SL.alloc() for _ in range(4)]
          for half in range(2):
              wv, wreg = ws_get("%s_WIO%d" % (tn, half))
              for cc in range(4):
                  c = half * 4 + cc
                  ps = PS.alloc()
                  mmg(ps.t[:, 0:512], [(wv[:, kc, cc * 128:(cc + 1) * 128], xbf_ap(xbf, kc)) for kc in range(8)],
                      xr(xbf) + [wreg], [ps.r()])
                  act(U[c // 2].b[:, (c % 2) * 512:(c % 2) * 512 + 512], ps.t[:, 0:512], AF.Gelu_apprx_tanh, [ps.r()],
                      [U[c // 2].r()])
                  PS.release(ps)
              ws_done("%s_WIO%d" % (tn, half))
          VT = [[SL.alloc(), SL.alloc()] for _ in range(4)]
          for vh in range(2):
              wv, wreg = ws_get("%s_WIO%d" % (tn, 2 + vh))
              for s_ in range(4):
                  ps = PS.alloc()
                  mmg(ps.t[:, 0:512], [(xbf_ap(xbf, kc, s_ * 128, s_ * 128 + 128), wv[:, kc, :]) for kc in range(8)],
                      xr(xbf) + [wreg], [ps.r()])
                  act(VT[s_][vh].f[:, 0:512], ps.t[:, 0:512], AF.Gelu_apprx_tanh, [ps.r()], [VT[s_][vh].r()])
                  PS.release(ps)
              ws_done("%s_WIO%d" % (tn, 2 + vh))
          SL.release(xbf)
          VNB = []
          for s_ in range(4):
              stt_ = SL.alloc()
              for vh in range(2):
                  P.op("dve", lambda e, o_=stt_.f[:, vh * 6:vh * 6 + 6], i_=VT[s_][vh].f[:, 0:512]: e.bn_stats(out=o_, in_=i_),
                       [VT[s_][vh].r()], [stt_.r()])
              P.op("dve", lambda e, stt_=stt_: e.bn_aggr(out=stt_.f[:, 16:18], in_=stt_.f[:, 0:12]), [stt_.r()], [stt_.r()])
              ts(stt_.f[:, 17:18], stt_.f[:, 17:18], 0.0, EPS, ALU.max, ALU.add, [stt_.r()], [stt_.r()])
              act(stt_.f[:, 17:18], stt_.f[:, 17:18], AF.Ln, [stt_.r()], [stt_.r()])
              act(stt_.f[:, 17:18], stt_.f[:, 17:18], AF.Exp, [stt_.r()], [stt_.r()], scale=-0.5)
              vb = SL.alloc()
              VNB.append(vb)
              for vh in range(2):
                  v = VT[s_][vh]
                  ts(v.f[:, 0:512], v.f[:, 0:512], stt_.f[:, 16:17], stt_.f[:, 17:18], ALU.subtract, ALU.mult,
                     [v.r(), stt_.r()], [v.r()])
                  tt(v.f[:, 0:512], v.f[:, 0:512], SGG.t[:, vh * 512:(vh + 1) * 512], ALU.mult, [v.r(), SGG.r()], [v.r()])
                  tt(v.f[:, 0:512], v.f[:, 0:512], SGBB.t[:, vh * 512:(vh + 1) * 512], ALU.add, [v.r(), SGBB.r()], [v.r()])
                  act(vb.b[:, vh * 512:(vh + 1) * 512], v.f[:, 0:512], AF.Copy, [v.r()], [vb.r()])
                  if first and s_ >= 2:
                      st(o_sgv[(s_ - 2) * 128:(s_ - 1) * 128, vh * 512:(vh + 1) * 512], v.f[:, 0:512], [v.r()], slot_sem(v))
              SL.release(stt_, VT[s_])
          GT = [SL.alloc() for _ in range(4)]
          for g in range(8):
              ps = PS.alloc()
              for s_ in range(4):
                  if first and s_ >= 2:
                      for b2 in range(2):
                          bb = (s_ - 2) * 2 + b2
                          r0 = b2 * 64
                          o = 256 + bb * 64
                          mm(ps.t[:, o:o + 64], VNB[s_].b[r0:r0 + 64, g * 128:(g + 1) * 128], WST64.t[r0:r0 + 64, g, :],
                             [VNB[s_].r(), WST64.r()], [ps.r()], start=True, stop=False)
                          mm(ps.t[:, o:o + 64], ONE1B.t[0:1, :], BHI.t[0:1, g, 0:64], [ONE1B.r(), BHI.r()], [ps.r()],
                             start=False, stop=False)
                          mm(ps.t[:, o:o + 64], ONE1B.t[0:1, :], BLO.t[0:1, g, 0:64], [ONE1B.r(), BLO.r()], [ps.r()],
                             start=False, stop=True)
                  else:
                      o = s_ * 128
                      mm(ps.t[:, o:o + 128], VNB[s_].b[:, g * 128:(g + 1) * 128], WST.t[:, g, :], [VNB[s_].r(), WST.r()],
                         [ps.r()], start=True, stop=False)
                      mm(ps.t[:, o:o + 128], ONE1B.t[0:1, :], BHI.t[0:1, g, :], [ONE1B.r(), BHI.r()], [ps.r()],
                         start=False, stop=False)
                      mm(ps.t[:, o:o + 128], ONE1B.t[0:1, :], BLO.t[0:1, g, :], [ONE1B.r(), BLO.r()], [ps.r()],
                         start=False, stop=True)
              tt(GT[g // 2].b[:, (g % 2) * 512:(g % 2) * 512 + 512], ps.t[:, 0:512],
                 U[g // 2].b[:, (g % 2) * 512:(g % 2) * 512 + 512], ALU.mult, [ps.r(), U[g // 2].r()], [GT[g // 2].r()])
              PS.release(ps)
          SL.release(VNB, U)
          for half in range(2):
              wv, wreg = ws_get("%s_WOO%d" % (tn, half))
              for oo in range(4):
                  o = half * 4 + oo
                  ps = PS.alloc()
                  mmg(ps.t[:, 0:512], [(wv[:, kc, oo * 128:(oo + 1) * 128], GT[kc // 2].b[:, (kc % 2) * 512:(kc % 2) * 512 + 512])
                                       for kc in range(8)], [x.r() for x in GT] + [wreg], [ps.r()])
                  stt(xres[o].f[:, 0:512], xres[o].f[:, 0:512], ALPHA, ps.t[:, 0:512], ALU.mult, ALU.add,
                      [xres[o].r(), ps.r()], [xres[o].r()])
                  PS.release(ps)
              ws_done("%s_WOO%d" % (tn, half))
          SL.release(GT)

          xbf = layer_norm(xres, 1, 0)
          ffn(tn, xres, xbf, 1, csegs)
          if ti + 1 < n_own_tiles:
              nxt_xbf = load_xbf(d_xo, c0 + 512, 512)
          if first:
              ts(HC.t[:, 1, :, :], HC.t[:, 1, :, :], FLAG.t[:, 0:1], None, ALU.mult, None, [HC.r(1), FLAG.r()], [HC.r(1)])
              for l in range(2):
                  st(o_fcs[l], HCS.t[:, l, :, :, :], [HCS.r(l)], osem())
          if last:
              for l in range(2):
                  st(o_fcp[l], HC.t[:, l, :, :], [HC.r(l)], osem())
          xbf = layer_norm(xres, 1, 1)
          SL.release(xbf)
          for c in range(8):
              st(o_y[c, :, c0:c0 + 512], xres[c].f[:, 0:512], [xres[c].r()], slot_sem(xres[c]))
          SL.release(xres)
    except _Stop:
        pass

    P.final_wait("sp")

    allsems = [P.sem[e] for e in P.ENGS] + P.dsems
    for s in allsems:
        s.h = es.enter_context(nc.semaphore(s.name))
    block = es.enter_context(nc.Block())

    @block.tensor
    def _(e):
        P.emit("pe", e)

    @block.scalar
    def _(e):
        P.emit("act", e)

    @block.vector
    def _(e):
        P.emit("dve", e)

    @block.gpsimd
    def _(e):
        P.emit("pool", e)

    @block.sync
    def _(e):
        P.emit("sp", e)

    info = dict(arena=ns, arena_low=SL.low, psum_low=PS.low, nsem=len(allsems),
                nops={e: len(P.ops[e]) for e in P.ENGS})
    es.close()
    return nc, info


def _f(x):
    return np.ascontiguousarray(np.asarray(x, dtype=np.float32))


def _shared_inputs(w_in_e, b_f, rg_conv_w, rg_conv_b, rg_wa, rg_ba, rg_wx, rg_bx, rg_lam, w_out_e, w_in_o, sgu_g,
                   sgu_b, sgu_w, sgu_bias, w_out_o, ln_mix_g, ln_mix_b, ln_ffn_g, ln_ffn_b, ffn_w_up, ffn_conv_w,
                   ffn_conv_b, ffn_w_down):
    m = {}
    W = _f(w_in_e)[0]
    cols = [(0, 512), (512, 1024), (1024, 1536), (1544, 2056), (2056, 2568)]
    m["win0"] = _f(np.stack([W[:, a:b].reshape(8, 128, 512).transpose(1, 0, 2) for a, b in cols]))
    m["wf"] = _f(W[:, 1536:1544].reshape(8, 128, 8).transpose(1, 0, 2))
    Wo = _f(w_out_e)[0]
    m["woa"] = _f(Wo[0:512].reshape(2, 4, 64, 1024).transpose(0, 2, 1, 3))
    m["wor"] = _f(Wo[512:1024].reshape(4, 128, 1024).transpose(1, 0, 2))
    Wu = _f(ffn_w_up)
    m["wup"] = _f(Wu.reshape(2, 8, 128, 2, 11, 256).transpose(0, 4, 2, 1, 3, 5))
    Wd = _f(ffn_w_down)
    m["wdn"] = _f(Wd.reshape(2, 22, 128, 8, 128).transpose(0, 3, 2, 1, 4))
    Wi = _f(w_in_o)[0]
    m["wio"] = _f(Wi.reshape(8, 128, 4, 512).transpose(2, 1, 0, 3))
    Woo = _f(w_out_o)[0]
    m["woo"] = _f(Woo.reshape(8, 128, 2, 512).transpose(2, 1, 0, 3))
    rgw = np.zeros((2, 128, 4, 128), np.float32)
    for wi, w in enumerate((_f(rg_wa)[0], _f(rg_wx)[0])):
        for blk in range(8):
            c, hb = blk // 2, blk % 2
            rgw[wi, hb * 64:(hb + 1) * 64, c, hb * 64:(hb + 1) * 64] = w[blk]
    m["rgw"] = rgw
    sw = _f(sgu_w)[0]
    wst = _f(sw.transpose(2, 0, 1))
    m["wst"] = wst
    m["wst64"] = _f(np.concatenate([wst[0:64, :, 0:64], wst[0:64, :, 0:64]], axis=0))
    m["sgb"] = _f(sgu_bias).reshape(1, 8, 128)
    lnp = np.zeros((128, 2, 2, 2, 8), np.float32)
    for l in range(2):
        for wi, (g, b) in enumerate(((ln_mix_g, ln_mix_b), (ln_ffn_g, ln_ffn_b))):
            lnp[:, l, wi, 0, :] = _f(g)[l].reshape(8, 128).T
            lnp[:, l, wi, 1, :] = _f(b)[l].reshape(8, 128).T
    m["lnp"] = lnp
    fcw = np.zeros((128, 2, NHC, 4), np.float32)
    for l in range(2):
        for j in range(3):
            fcw[:, l, :, j] = _f(ffn_conv_w)[l, j].reshape(NHC, 128).T
        fcw[:, l, :, 3] = _f(ffn_conv_b)[l].reshape(NHC, 128).T
    m["fcw"] = fcw
    rgp = np.zeros((128, 4, 8), np.float32)
    for j in range(4):
        rgp[:, :, j] = _f(rg_conv_w)[0, j].reshape(4, 128).T
    for j, v in enumerate((rg_conv_b, rg_ba, rg_bx, rg_lam)):
        rgp[:, :, 4 + j] = _f(v)[0].reshape(4, 128).T
    m["rgp"] = rgp
    m["bfv"] = _f(b_f)[0].reshape(8, 1)
    m["sgg"] = _f(sgu_g).reshape(1, 1024)
    m["sgbb"] = _f(sgu_b).reshape(1, 1024)
    m["tri"] = _f(np.triu(np.ones((128, 128), np.float32)))
    m["i8"] = _f(np.eye(8, dtype=np.float32))
    return m


_CACHE = {}


def kernel(x_prompt, x_sample, cache_k, cache_v, cache_logf, state_rglru_conv, state_rglru_h, state_ffn_conv,
           **weights):
    if "nc" not in _CACHE:
        _CACHE["nc"] = build_program()
    nc, info = _CACHE["nc"]
    shared = _shared_inputs(**weights)
    xp = _f(x_prompt)
    xs = _f(x_sample)
    ck = _f(cache_k)[0]
    cv = _f(cache_v)[0]
    clf = _f(cache_logf)[0]
    rcs = _f(state_rglru_conv)[0]
    rh0 = _f(state_rglru_h)[0]
    sfc = _f(state_ffn_conv)
    in_maps = []
    for c in range(8):
        b, hh = c // 2, c % 2
        win = np.concatenate([np.zeros((2048, D), np.float32), xp[b]], axis=0)[hh * 2048:hh * 2048 + 4096]
        samp = xs[4 * c:4 * c + 4].reshape(256, D)
        own = np.concatenate([win[1792:2048], samp, win[2048:4096]], axis=0)
        m = dict(shared)
        m["xl"] = _f(win[0:1792].T.reshape(8, 128, 1792))
        m["xo"] = _f(own.T.reshape(8, 128, NOWN))
        kb = ck[4 * c:4 * c + 4, ::-1]
        m["ckT"] = _f(kb.reshape(4, 2048, 4, 128).transpose(0, 2, 3, 1))
        vb = cv[4 * c:4 * c + 4, ::-1]
        m["cvr"] = _f(vb.reshape(4, 16, 128, 4, 128).transpose(0, 3, 2, 1, 4))
        m["clf"] = _f(clf[4 * c:4 * c + 4, ::-1].transpose(0, 2, 1))
        m["rcs"] = _f(rcs[4 * c:4 * c + 4].reshape(4, 3, 4, 128).transpose(3, 2, 0, 1))
        m["rh0"] = _f(rh0[4 * c:4 * c + 4].reshape(4, 4, 128).transpose(2, 1, 0))
        m["sfc"] = _f(sfc[:, 4 * c:4 * c + 4].reshape(2, 4, 2, NHC, 128).transpose(0, 4, 3, 1, 2))
        m["flag"] = np.full((128, 1), float(hh), np.float32)
        m["pen"] = np.full((128, 1), 0.0 if hh else PENV, np.float32)
        in_maps.append(m)
    res = run_bass_kernel_spmd(nc, in_maps, core_ids=list(range(8))).results

    y_p = np.zeros((4, 4096, D), np.float32)
    y_s = np.zeros((32, 64, D), np.float32)
    k_p = np.zeros((1, 4, 4096, 8, 64), np.float32)
    v_p = np.zeros((1, 4, 4096, 8, 64), np.float32)
    lf_p = np.zeros((1, 4, 4096, 8), np.float32)
    k_s = np.zeros((1, 32, 64, 8, 64), np.float32)
    v_s = np.zeros((1, 32, 64, 8, 64), np.float32)
    lf_s = np.zeros((1, 32, 64, 8), np.float32)
    rc_p = np.zeros((1, 4, 3, 512), np.float32)
    rh_p = np.zeros((1, 4, 512), np.float32)
    rc_s = np.zeros((1, 32, 3, 512), np.float32)
    rh_s = np.zeros((1, 32, 512), np.float32)
    fc_p = np.zeros((2, 4, 2, 2 * DFF), np.float32)
    fc_s = np.zeros((2, 32, 2, 2 * DFF), np.float32)
    sg_s = np.zeros((1, 32, 64, D), np.float32)
    for c in range(8):
        r = res[c]
        b, hh = c // 2, c % 2
        sl = slice(hh * 2048, hh * 2048 + 2048)
        yT = r["yT"].reshape(D, NOWN)
        y_p[b, sl] = yT[:, 512:].T
        y_s[4 * c:4 * c + 4] = yT[:, 256:512].T.reshape(4, 64, D)
        kT = r["koT"].reshape(512, NOWN)
        k_p[0, b, sl] = kT[:, 512:].T.reshape(2048, 8, 64)
        k_s[0, 4 * c:4 * c + 4] = kT[:, 256:512].T.reshape(4, 64, 8, 64)
        vo = r["vo"]
        v_p[0, b, sl] = vo[512:].reshape(2048, 8, 64)
        v_s[0, 4 * c:4 * c + 4] = vo[256:512].reshape(4, 64, 8, 64)
        lf = r["lfo"]
        lf_p[0, b, sl] = lf[:, 512:].T
        lf_s[0, 4 * c:4 * c + 4] = lf[:, 256:512].T.reshape(4, 64, 8)
        if hh == 1:
            rc_p[0, b] = r["rco"].transpose(2, 1, 0).reshape(3, 512)
            rh_p[0, b] = r["rho"].T.reshape(512)
            fc_p[:, b] = r["fcpo"].transpose(0, 3, 2, 1).reshape(2, 2, 2 * DFF)
        rc_s[0, 4 * c:4 * c + 4] = r["rcso"].transpose(2, 3, 1, 0).reshape(4, 3, 512)
        rh_s[0, 4 * c:4 * c + 4] = r["rhso"].transpose(2, 1, 0).reshape(4, 512)
        fc_s[:, 4 * c:4 * c + 4] = r["fcso"].transpose(0, 3, 4, 2, 1).reshape(2, 4, 2, 2 * DFF)
        sg_s[0, 4 * c:4 * c + 4] = r["sgvo"].reshape(4, 64, D)
    return (y_p, y_s, k_p, v_p, lf_p, k_s, v_s, lf_s, rc_p, rh_p, rc_s, rh_s, fc_p, fc_s, sg_s)


if __name__ == "__main__":
    import time
    t0 = time.time()
    nc, info = build_program()
    print("built in", time.time() - t0, info)
```

```python
import numpy as np
from contextlib import ExitStack
import concourse.bass as bass
import concourse.mybir as mybir
from concourse.bass_utils import run_bass_kernel_spmd

F32 = mybir.dt.float32
BF16 = mybir.dt.bfloat16
AF = mybir.ActivationFunctionType
ALU = mybir.AluOpType

D = 1024
NCH = 8
DFF = 2816
NHC = 44
ALPHA = 4.0 ** 0.25
EPS = 1e-5
PENV = 30000.0
NLIGHT = 1792
NOWN = 2560
NWSLOT = 4
SAFE_SAME_ENGINE = True
DEBUG_NO_STORES = False
DEBUG_NO_GELU = False
DEBUG_SKIPKT = False
DEBUG_ACTPROBE = False
DEBUG_SKIPKO = False


class Sem:
    def __init__(self, name, dma=False):
        self.name = name
        self.dma = dma
        self.count = 0
        self.h = None


class Reg:
    __slots__ = ("w", "rd")

    def __init__(self):
        self.w = None
        self.rd = {}


class Prog:
    ENGS = ("pe", "act", "dve", "pool", "sp")

    def __init__(self):
        self.sem = {e: Sem("s_" + e) for e in self.ENGS}
        self.ops = {e: [] for e in self.ENGS}
        self.seen = {e: {} for e in self.ENGS}
        self.dsems = []

    def dsem(self, name):
        s = Sem(name, dma=True)
        self.dsems.append(s)
        return s

    def _waits(self, en, reads, writes):
        mysem = self.sem[en]
        need = {}

        def add(tok, skip_same):
            if tok is None:
                return
            s, v = tok
            if s is mysem and (en == "pe" or skip_same or not SAFE_SAME_ENGINE):
                return
            if s.dma:
                v = s.count
            if need.get(s, 0) < v:
                need[s] = v

        for r in reads:
            add(r.w, False)
        for w in writes:
            add(w.w, True)
            for s, v in w.rd.items():
                add((s, v), True)
        out = []
        seen = self.seen[en]
        for s, v in need.items():
            if seen.get(s, 0) < v:
                seen[s] = v
                out.append((s, v))
        return out

    def _commit(self, tok, reads, writes):
        s, v = tok
        for r in reads:
            if r.rd.get(s, 0) < v:
                r.rd[s] = v
        for w in writes:
            w.w = tok
            w.rd = {}

    def op(self, en, fn, reads=(), writes=()):
        waits = self._waits(en, reads, writes)
        s = self.sem[en]
        s.count += 1
        self.ops[en].append((waits, fn, s, 1))
        self._commit((s, s.count), reads, writes)

    def dma(self, qn, fn, reads, writes, sem):
        waits = self._waits(qn, reads, writes)
        sem.count += 16
        self.ops[qn].append((waits, fn, sem, 16))
        self._commit((sem, sem.count), reads, writes)

    def final_wait(self, qn):
        waits = [(s, s.count) for s in self.dsems if s.count > 0]
        waits += [(self.sem[e], self.sem[e].count) for e in self.ENGS if e != qn and self.sem[e].count > 0]
        self.ops[qn].append((waits, None, None, 0))

    def emit(self, en, eh):
        for waits, fn, s, amt in self.ops[en]:
            for ws, v in waits:
                eh.wait_ge(ws.h, v)
            if fn is not None:
                ins = fn(eh)
                ins.then_inc(s.h, amt)


class Buf:
    def __init__(self, t, name=""):
        self.t = t
        self.name = name
        self.regs = {}
        self.sem = None

    def r(self, key=None):
        g = self.regs.get(key)
        if g is None:
            g = self.regs[key] = Reg()
        return g


class Slot(Buf):
    def __init__(self, f, idx):
        super().__init__(f, "slot%d" % idx)
        self.f = f
        self.b = f.bitcast(BF16)
        self.idx = idx


class Pool_:
    def __init__(self, items, name):
        self.free = list(items)
        self.name = name
        self.total = len(items)
        self.low = len(items)

    def alloc(self):
        if not self.free:
            raise RuntimeError("pool %s exhausted" % self.name)
        x = self.free.pop(0)
        self.low = min(self.low, len(self.free))
        return x

    def release(self, *xs):
        for x in xs:
            if isinstance(x, (list, tuple)):
                self.release(*x)
            else:
                assert x not in self.free
                self.free.append(x)


class _Stop(Exception):
    pass


def build_program(n_own_tiles=5, n_light_tiles=4, arena_slots=None, stop_stage=None):
    nc = bass.Bass("TRN2", target_bir_lowering=False)
    P = Prog()

    def din(name, shape):
        return nc.dram_tensor(name, list(shape), F32, kind="ExternalInput").ap()

    def dout(name, shape):
        return nc.dram_tensor(name, list(shape), F32, kind="ExternalOutput").ap()

    d_xl = din("xl", [NCH, 128, NLIGHT])
    d_xo = din("xo", [NCH, 128, NOWN])
    d_ckT = din("ckT", [4, 4, 128, 2048])
    d_cvr = din("cvr", [4, 4, 128, 16, 128])
    d_clf = din("clf", [4, 8, 2048])
    d_rcs = din("rcs", [128, 4, 4, 3])
    d_rh0 = din("rh0", [128, 4, 4])
    d_sfc = din("sfc", [2, 128, NHC, 4, 2])
    d_win0 = din("win0", [5, 128, 8, 512])
    d_wf = din("wf", [128, 8, 8])
    d_woa = din("woa", [2, 64, 4, 1024])
    d_wor = din("wor", [128, 4, 1024])
    d_wup = din("wup", [2, 11, 128, 8, 2, 256])
    d_wdn = din("wdn", [2, 8, 128, 22, 128])
    d_wio = din("wio", [4, 128, 8, 512])
    d_woo = din("woo", [2, 128, 8, 512])
    d_rgw = din("rgw", [2, 128, 4, 128])
    d_wst = din("wst", [128, 8, 128])
    d_wst64 = din("wst64", [128, 8, 64])
    d_sgb = din("sgb", [1, 8, 128])
    d_lnp = din("lnp", [128, 2, 2, 2, 8])
    d_fcw = din("fcw", [128, 2, NHC, 4])
    d_rgp = din("rgp", [128, 4, 8])
    d_bf = din("bfv", [8, 1])
    d_sgg = din("sgg", [1, 1024])
    d_sgbb = din("sgbb", [1, 1024])
    d_tri = din("tri", [128, 128])
    d_i8 = din("i8", [8, 8])
    d_flag = din("flag", [128, 1])
    d_pen = din("pen", [128, 1])

    o_y = dout("yT", [NCH, 128, NOWN])
    o_k = dout("koT", [4, 128, NOWN])
    o_v = dout("vo", [NOWN, 512])
    o_lf = dout("lfo", [8, NOWN])
    o_rc = dout("rco", [128, 4, 3])
    o_rh = dout("rho", [128, 4])
    o_rcs = dout("rcso", [128, 4, 4, 3])
    o_rhs = dout("rhso", [128, 4, 4])
    o_fcp = dout("fcpo", [2, 128, NHC, 2])
    o_fcs = dout("fcso", [2, 128, NHC, 4, 2])
    o_sgv = dout("sgvo", [256, 1024])

    es = ExitStack()

    def sb(name, shape, dt):
        return Buf(es.enter_context(nc.sbuf_tensor(name, list(shape), dt)), name)

    KT = sb("KT", [128, 4, 2048], BF16)
    KT2 = sb("KT2", [128, 4, 2048], BF16)
    VA = sb("VA", [128, 32, 8, 66], BF16)
    CK = sb("CK", [128, 32, 8], F32)
    VAS = [sb("VAS%d" % i, [128, 16, 2, 66], BF16) for i in range(1)]
    VNEW = sb("VNEW", [64, 4, 8, 66], BF16)
    XBH = sb("XBH", [128, 4, 3 + 512], F32)
    HCAR = sb("HCAR", [128, 4], F32)
    CUMC = sb("CUMC", [8, 1], F32)
    HC = sb("HC", [128, 2, NHC, 2], F32)
    HCS = sb("HCS", [128, 2, NHC, 4, 2], F32)
    SFC = sb("SFC", [128, 2, NHC, 4, 2], F32)
    HSL = sb("HSL", [128, 4, 4], F32)
    RH0 = sb("RH0", [128, 4, 4], F32)
    WF = sb("WF", [128, 8, 8], BF16)
    RGW = sb("RGW", [128, 2, 4, 128], BF16)
    WST = sb("WST", [128, 8, 128], BF16)
    WST64 = sb("WST64", [128, 8, 64], BF16)
    SGB1 = sb("SGB1", [1, 8, 128], F32)
    BHI = sb("BHI", [1, 8, 128], BF16)
    BLO = sb("BLO", [1, 8, 128], BF16)
    BTMP = sb("BTMP", [1, 8, 128], F32)
    SGG = sb("SGG", [128, 1024], F32)
    SGBB = sb("SGBB", [128, 1024], F32)
    LNP = sb("LNP", [128, 2, 2, 2, 8], F32)
    FCW = sb("FCW", [128, 2, NHC, 4], F32)
    RGP = sb("RGP", [128, 4, 8], F32)
    RGC = sb("RGC", [128, 4, 4], F32)
    NBF = sb("NBF", [8, 1], F32)
    TRI = sb("TRI", [128, 128], BF16)
    I8 = sb("I8", [8, 8], F32)
    FLAG = sb("FLAG", [128, 1], F32)
    PEN = sb("PEN", [128, 1], F32)
    ONESF = sb("ONESF", [128, 512], F32)
    ONESB = sb("ONESB", [128, 128], BF16)
    ONE1B = sb("ONE1B", [1, 128], BF16)
    CST = sb("CST", [128, 4], F32)
    WSL = [sb("WSL%d" % i, [128, 4096], BF16) for i in range(NWSLOT)]
    for i, w in enumerate(WSL):
        w.sem = P.dsem("wsl%d" % i)

    rem = nc.sbuf_bytes_remaining
    ns = arena_slots if arena_slots is not None else (rem - 1024) // 2048
    ARENA = es.enter_context(nc.sbuf_tensor("ARENA", [128, ns, 512], F32))
    print("arena slots", ns, "sbuf remaining", rem)
    SL = Pool_([Slot(ARENA[:, i, :], i) for i in range(ns)], "arena")
    for s in SL.free:
        s.sem = None
    PSB = []
    for i in range(8):
        t = es.enter_context(nc.psum_tensor("PS%d" % i, [128, 512], F32))
        PSB.append(Buf(t, "ps%d" % i))
    PS = Pool_(PSB, "psum")

    misc_sem = P.dsem("misc_ld")
    misc_sw = P.dsem("misc_sw")
    out_sem = [P.dsem("out%d" % i) for i in range(4)]
    _oc = [0]

    def osem():
        _oc[0] += 1
        return out_sem[_oc[0] % len(out_sem)]

    def slot_sem(s, sw=False):
        if sw:
            if getattr(s, "sem_sw", None) is None:
                s.sem_sw = P.dsem("sw%d" % s.idx)
            return s.sem_sw
        if s.sem is None:
            s.sem = P.dsem("sl%d" % s.idx)
        return s.sem

    def act(out, in_, func, reads, writes, bias=None, scale=None):
        if DEBUG_NO_GELU and func == AF.Gelu_apprx_tanh:
            func = AF.Copy
        kw = {}
        if bias is not None:
            kw["bias"] = bias
        if scale is not None:
            kw["scale"] = scale
        P.op("act", lambda e: e.activation(out=out, in_=in_, func=func, **kw), reads, writes)

    def tt(out, in0, in1, op, reads, writes, en="dve"):
        P.op(en, lambda e: e.tensor_tensor(out=out, in0=in0, in1=in1, op=op), reads, writes)

    def ts(out, in0, s1, s2, op0, op1, reads, writes, en="dve"):
        if s2 is None:
            P.op(en, lambda e: e.tensor_scalar(out=out, in0=in0, scalar1=s1, scalar2=None, op0=op0), reads, writes)
        else:
            P.op(en, lambda e: e.tensor_scalar(out=out, in0=in0, scalar1=s1, scalar2=s2, op0=op0, op1=op1),
                 reads, writes)

    def stt(out, in0, scalar, in1, op0, op1, reads, writes):
        P.op("dve", lambda e: e.scalar_tensor_tensor(out=out, in0=in0, scalar=scalar, in1=in1, op0=op0, op1=op1),
             reads, writes)

    def cp(out, in_, reads, writes, en="dve"):
        P.op(en, lambda e: e.tensor_copy(out=out, in_=in_), reads, writes)

    def mm(out, lhsT, rhs, reads, writes, start=True, stop=True):
        P.op("pe", lambda e: e.matmul(out, lhsT, rhs, start=start, stop=stop), reads, writes)

    def mmg(out, pairs, reads, writes):
        n = len(pairs)

        def fn(e):
            ins = None
            for i, (l, r) in enumerate(pairs):
                ins = e.matmul(out, l, r, start=(i == 0), stop=(i == n - 1))
            return ins

        P.op("pe", fn, reads, writes)

    def ld(q, out, in_, writes, sem, reads=()):
        P.dma(q, lambda e: e.dma_start(out=out, in_=in_), list(reads), list(writes), sem)

    def st(out, in_, reads, sem):
        if DEBUG_NO_STORES:
            return
        P.dma("sp", lambda e: e.dma_start(out=out, in_=in_), list(reads), [], sem)

    def ldm(buf, src, q="sp"):
        ld(q, buf.t[:], src, [buf.r()], misc_sw if q == "pool" else misc_sem)

    ldm(LNP, d_lnp)
    ldm(FCW, d_fcw)
    ldm(RGP, d_rgp)
    ldm(NBF, d_bf)
    ldm(I8, d_i8)
    ldm(FLAG, d_flag)
    ldm(PEN, d_pen)
    for l_ in range(2):
        ld("sp", SFC.t[:, l_, :, :, :], d_sfc[l_], [SFC.r()], misc_sem)
    ldm(RH0, d_rh0)
    ldm(WST, d_wst, q="pool")
    ldm(WST64, d_wst64, q="pool")
    ldm(SGB1, d_sgb)
    ldm(SGG, d_sgg.partition_broadcast(128))
    ldm(SGBB, d_sgbb.partition_broadcast(128))
    ldm(WF, d_wf, q="pool")
    for w_ in range(2):
        ld("pool", RGW.t[:, w_, :, :], d_rgw[w_], [RGW.r()], misc_sw)
    ldm(TRI, d_tri, q="pool")

    P.op("dve", lambda e: e.memset(ONESF.t[:], 1.0), [], [ONESF.r()])
    P.op("dve", lambda e: e.memset(ONESB.t[:], 1.0 / 1024.0), [], [ONESB.r()])
    P.op("dve", lambda e: e.memset(ONE1B.t[:], 1.0), [], [ONE1B.r()])
    P.op("dve", lambda e: e.memset(CST.t[:, 0:1], 1.0), [], [CST.r()])
    P.op("dve", lambda e: e.memset(CST.t[:, 1:2], EPS), [], [CST.r()])
    P.op("dve", lambda e: e.memset(CST.t[:, 2:3], 0.0), [], [CST.r()])
    P.op("dve", lambda e: e.memset(VA.t[:, :, :, 64:66], 1.0), [], [VA.r("ones")])
    for i in range(1):
        P.op("dve", lambda e, i=i: e.memset(VAS[i].t[:, :, :, 64:66], 1.0), [], [VAS[i].r("ones")])
    P.op("dve", lambda e: e.memset(VNEW.t[:, :, :, 64:66], 1.0), [], [VNEW.r("ones")])
    P.op("dve", lambda e: e.memset(XBH.t[:, :, 0:3], 0.0), [], [XBH.r()])
    P.op("dve", lambda e: e.memset(HCAR.t[:], 0.0), [], [HCAR.r()])
    P.op("dve", lambda e: e.memset(CUMC.t[:], 0.0), [], [CUMC.r()])
    P.op("dve", lambda e: e.memset(HC.t[:], 0.0), [], [HC.r(0), HC.r(1)])
    ONE = CST.t[:, 0:1]
    EPSC = CST.t[:, 1:2]
    ts(NBF.t[:], NBF.t[:], -1.0, None, ALU.mult, None, [NBF.r()], [NBF.r()])
    for g in range(8):
        tt(WST.t[:, g, :], WST.t[:, g, :], TRI.t[:, :], ALU.mult, [WST.r(), TRI.r()], [WST.r()])
        tt(WST64.t[0:64, g, :], WST64.t[0:64, g, :], TRI.t[0:64, 0:64], ALU.mult, [WST64.r(), TRI.r()],
           [WST64.r()])
        tt(WST64.t[64:128, g, :], WST64.t[64:128, g, :], TRI.t[64:128, 64:128], ALU.mult, [WST64.r(), TRI.r()],
           [WST64.r()])
    cp(BHI.t[:], SGB1.t[:], [SGB1.r()], [BHI.r()])
    tt(BTMP.t[:], SGB1.t[:], BHI.t[:], ALU.subtract, [SGB1.r(), BHI.r()], [BTMP.r()])
    cp(BLO.t[:], BTMP.t[:], [BTMP.r()], [BLO.r()])
    act(RGC.t[:, :, 2], RGP.t[:, :, 7], AF.Exp, [RGP.r()], [RGC.r()], scale=-1.0)
    act(RGC.t[:, :, 3], RGC.t[:, :, 2], AF.Ln, [RGC.r(), CST.r()], [RGC.r()], bias=ONE)
    ts(RGC.t[:, :, 0], RGC.t[:, :, 3], -8.0, None, ALU.mult, None, [RGC.r()], [RGC.r()])
    ts(RGC.t[:, :, 1], RGC.t[:, :, 3], -16.0, None, ALU.mult, None, [RGC.r()], [RGC.r()])
    ts(RGC.t[:, :, 2], RGP.t[:, :, 5], -1.0, None, ALU.mult, None, [RGP.r(), RGC.r()], [RGC.r()])
    ts(RGC.t[:, :, 3], RGP.t[:, :, 6], -1.0, None, ALU.mult, None, [RGP.r(), RGC.r()], [RGC.r()])

    blocks = []

    def v_k512(t):
        return t[:, 0:4096].rearrange("p (k n) -> p k n", k=8)

    def addblk(name, src, view):
        blocks.append((name, src, view))

    for li in range(n_light_tiles):
        for nm, bi in (("K", 1), ("V", 2), ("XB", 3)):
            addblk("L%d_%s" % (li, nm), d_win0[bi], v_k512)
    for ti in range(n_own_tiles):
        for nm, bi in (("K", 1), ("V", 2), ("Q", 0), ("GB", 4), ("XB", 3)):
            addblk("O%d_%s" % (ti, nm), d_win0[bi], v_k512)
        for i in range(2):
            addblk("O%d_WOA%d" % (ti, i), d_woa[i],
                   lambda t: t[0:64, 0:4096].rearrange("p (h n) -> p h n", h=4))
        addblk("O%d_WOR" % ti, d_wor, lambda t: t[:, 0:4096].rearrange("p (c n) -> p c n", c=4))
        for l in range(2):
            if l == 1:
                for i in range(4):
                    addblk("O%d_WIO%d" % (ti, i), d_wio[i], v_k512)
                for i in range(2):
                    addblk("O%d_WOO%d" % (ti, i), d_woo[i], v_k512)
            for g in range(11):
                addblk("O%d_UP%d_%d" % (ti, l, g), d_wup[l, g],
                       lambda t: t[:, 0:4096].rearrange("p (k u n) -> p k u n", k=8, u=2))
            for o in range(8):
                addblk("O%d_DN%d_%d" % (ti, l, o), d_wdn[l, o],
                       lambda t: t[:, 0:2816].rearrange("p (c n) -> p c n", c=22))

    class WS:
        nload = 0
        done = set()
        cur = {}

    def ws_pump():
        while WS.nload < len(blocks) and (WS.nload < NWSLOT or (WS.nload - NWSLOT) in WS.done):
            i = WS.nload
            name, src, view = blocks[i]
            slot = WSL[i % NWSLOT]
            ld("pool", view(slot.t), src, [slot.r()], slot.sem)
            WS.nload += 1

    def ws_get(name):
        ws_pump()
        for i in range(len(blocks)):
            if blocks[i][0] == name:
                break
        else:
            raise KeyError(name)
        assert i < WS.nload, "weight block %s not prefetched (ring too small)" % name
        slot = WSL[i % NWSLOT]
        WS.cur[name] = i
        return blocks[i][2](slot.t), slot.r()

    def ws_done(name):
        WS.done.add(WS.cur.pop(name))
        ws_pump()

    def xbf_ap(xbf, kc, a=0, b=512):
        return xbf[kc // 2].b[:, (kc % 2) * 512 + a:(kc % 2) * 512 + b]

    def load_xbf(src, c0, W):
        xbf = [SL.alloc() for _ in range(4)]
        for s in range(4):
            dst = xbf[s].b[:, 0:1024].rearrange("p (c t) -> p c t", c=2)[:, :, 0:W]
            ld("pool", dst, src[2 * s:2 * s + 2, :, c0:c0 + W].rearrange("c p t -> p c t"), [xbf[s].r()],
               slot_sem(xbf[s], True))
        return xbf

    def xr(xbf):
        return [s.r() for s in xbf]

    def forget_gate(xbf, W, segs, out_c0, lf_out=True):
        ps = PS.alloc()
        mmg(ps.t[0:8, 0:W], [(WF.t[:, kc, :], xbf_ap(xbf, kc, 0, W)) for kc in range(8)], xr(xbf) + [WF.r()],
            [ps.r()])
        t1 = SL.alloc()
        act(t1.f[0:8, 0:W], ps.t[0:8, 0:W], AF.Exp, [ps.r(), NBF.r()], [t1.r()], bias=NBF.t[:, 0:1], scale=-1.0)
        PS.release(ps)
        t2 = SL.alloc()
        act(t2.f[0:8, 0:W], t1.f[0:8, 0:W], AF.Ln, [t1.r(), CST.r()], [t2.r()], bias=CST.t[0:8, 0:1])
        lf = t1
        ts(lf.f[0:8, 0:W], t2.f[0:8, 0:W], -1.0, None, ALU.mult, None, [t2.r()], [lf.r()])
        if lf_out:
            st(o_lf[:, out_c0:out_c0 + W], lf.f[0:8, 0:W], [lf.r()], slot_sem(lf))
        cum = t2
        for kind, a, b in segs:
            if kind == "samp":
                for bb in range(4):
                    P.op("dve", lambda e, a=a, bb=bb: e.tensor_tensor_scan(
                        out=cum.f[0:8, a + bb * 64:a + bb * 64 + 64], data0=ONESF.t[0:8, 0:64],
                        data1=lf.f[0:8, a + bb * 64:a + bb * 64 + 64], initial=0.0, op0=ALU.mult, op1=ALU.add),
                        [lf.r(), ONESF.r()], [cum.r()])
            else:
                P.op("dve", lambda e, a=a, b=b: e.tensor_tensor_scan(
                    out=cum.f[0:8, a:b], data0=ONESF.t[0:8, 0:b - a], data1=lf.f[0:8, a:b],
                    initial=CUMC.t[0:8, 0:1], op0=ALU.mult, op1=ALU.add),
                    [lf.r(), ONESF.r(), CUMC.r()], [cum.r()])
                cp(CUMC.t[0:8, 0:1], cum.f[0:8, b - 1:b], [cum.r()], [CUMC.r()])
        return lf, cum

    def ck_from_cum(cum, a, b, jt0, add_pen):
        n = (b - a) // 128
        ps = PS.alloc()
        for i in range(n):
            mm(ps.t[:, i * 8:(i + 1) * 8], cum.f[0:8, a + i * 128:a + (i + 1) * 128], I8.t[:, :],
               [cum.r(), I8.r()], [ps.r()])
        dst = CK.t[:, jt0:jt0 + n, :]
        src = ps.t[:, 0:n * 8].rearrange("p (j h) -> p j h", h=8)
        regs = [CK.r(jt0 + i) for i in range(n)]
        if add_pen:
            ts(dst, src, PEN.t[:, 0:1], None, ALU.add, None, [ps.r(), PEN.r()], regs)
        else:
            cp(dst, src, [ps.r()], regs)
        PS.release(ps)

    def rglru_seg(kind, a, b, src_of, gb, hg, xbs=None):
        Wd = b - a
        samp = kind == "samp"

        def v2(ap):
            return ap.rearrange("p (b t) -> p b t", b=4) if samp else ap

        for grp_ in range(2):
            cs_ = (2 * grp_, 2 * grp_ + 1)
            xc, xcb, gr, gi, a2 = {}, {}, {}, {}, {}
            for c in cs_:
                hist, hreg = src_of(c)

                def tap(j):
                    return hist[:, :, j:j + 64] if samp else hist[:, j:j + Wd]

                t = SL.alloc()
                xc[c] = t
                o = v2(t.f[:, 0:Wd])
                act(o, tap(3), AF.Identity, [hreg, RGP.r()], [t.r()], bias=RGP.t[:, c, 4:5], scale=RGP.t[:, c, 3:4])
                for j in range(3):
                    stt(o, tap(j), RGP.t[:, c, j:j + 1], o, ALU.mult, ALU.add, [hreg, RGP.r(), t.r()], [t.r()])
                tb = SL.alloc()
                xcb[c] = tb
                act(tb.b[:, 0:Wd], t.f[:, 0:Wd], AF.Copy, [t.r()], [tb.r()])
            for c in cs_:
                for which, lst, bcol in ((0, gr, 5), (1, gi, 6)):
                    ps = PS.alloc()
                    mm(ps.t[:, 0:Wd], RGW.t[:, which, c, :], xcb[c].b[:, 0:Wd], [RGW.r(), xcb[c].r()], [ps.r()])
                    g = SL.alloc()
                    lst[c] = g
                    act(g.f[:, 0:Wd], ps.t[:, 0:Wd], AF.Exp, [ps.r(), RGC.r()], [g.r()], bias=RGC.t[:, c, 2 + which:3 + which],
                        scale=-1.0)
                    PS.release(ps)
                    ts(g.f[:, 0:Wd], g.f[:, 0:Wd], 1.0, None, ALU.add, None, [g.r()], [g.r()])
                    P.op("dve", lambda e, o_=g.f[:, 0:Wd]: e.reciprocal(out=o_, in_=o_), [g.r()], [g.r()])
                SL.release(xcb[c])
            if kind == 'halo' and grp_ == 0:
                stage(22)
            for c in cs_:
                t = SL.alloc()
                a2[c] = t
                act(t.f[:, 0:Wd], gr[c].f[:, 0:Wd], AF.Exp, [gr[c].r(), RGC.r()], [t.r()], scale=RGC.t[:, c, 1:2])
                act(gr[c].f[:, 0:Wd], gr[c].f[:, 0:Wd], AF.Exp, [gr[c].r(), RGC.r()], [gr[c].r()], scale=RGC.t[:, c, 0:1])
            for c in cs_:
                act(a2[c].f[:, 0:Wd], a2[c].f[:, 0:Wd], AF.Ln, [a2[c].r(), CST.r()], [a2[c].r()], bias=ONE, scale=-1.0)
                act(a2[c].f[:, 0:Wd], a2[c].f[:, 0:Wd], AF.Exp, [a2[c].r()], [a2[c].r()], scale=0.5)
            if kind == 'halo' and grp_ == 0:
                stage(23)
            for c in cs_:
                tt(gi[c].f[:, 0:Wd], gi[c].f[:, 0:Wd], xc[c].f[:, 0:Wd], ALU.mult, [gi[c].r(), xc[c].r()], [gi[c].r()])
                tt(gi[c].f[:, 0:Wd], gi[c].f[:, 0:Wd], a2[c].f[:, 0:Wd], ALU.mult, [gi[c].r(), a2[c].r()], [gi[c].r()])
                h = xc[c]
                if samp:
                    for bb in range(4):
                        P.op("dve", lambda e, o_=h.f[:, bb * 64:bb * 64 + 64], d0=gr[c].f[:, bb * 64:bb * 64 + 64],
                             d1=gi[c].f[:, bb * 64:bb * 64 + 64], i_=RH0.t[:, c, bb:bb + 1]: e.tensor_tensor_scan(
                            out=o_, data0=d0, data1=d1, initial=i_,
                            op0=ALU.mult, op1=ALU.add), [gr[c].r(), gi[c].r(), RH0.r()], [h.r()])
                    cp(HSL.t[:, c, :], h.f[:, 0:256].rearrange("p (b t) -> p b t", b=4)[:, :, 63], [h.r()], [HSL.r()])
                else:
                    P.op("dve", lambda e, o_=h.f[:, 0:Wd], d0=gr[c].f[:, 0:Wd], d1=gi[c].f[:, 0:Wd],
                         i_=HCAR.t[:, c:c + 1]: e.tensor_tensor_scan(
                        out=o_, data0=d0, data1=d1, initial=i_, op0=ALU.mult, op1=ALU.add),
                        [gr[c].r(), gi[c].r(), HCAR.r()], [h.r()])
                    cp(HCAR.t[:, c:c + 1], h.f[:, Wd - 1:Wd], [h.r()], [HCAR.r()])
                if hg is not None:
                    tt(hg[c // 2].b[:, (c % 2) * 512 + a:(c % 2) * 512 + b], h.f[:, 0:Wd], gb[c].f[:, a:b], ALU.mult,
                       [h.r(), gb[c].r()], [hg[c // 2].r()])
                SL.release(xc[c], gr[c], gi[c], a2[c])
            if kind == 'halo' and grp_ == 0:
                stage(24)


    def xbh_src(a):
        def f(c):
            return XBH.t[:, c, a:a + 3 + 512], XBH.r()
        return f

    def attend(q_ap_of, Wq, qblocks, keys, att_dst, att_reg):
        acc = PS.alloc()
        n = len(keys)
        LOOK = 4
        sts = {}

        def qk(i):
            k = keys[i]
            sps = PS.alloc()
            cs = k["cstart"]
            qap, qreg = q_ap_of(cs, Wq)
            mm(sps.t[0:k["nk"], cs:Wq], k["lhsT_k"], qap, [k["k_reg"], qreg], [sps.r()])
            sts[i] = sps

        for i in range(min(LOOK, n)):
            qk(i)
        for i in range(n):
            if i + LOOK < n:
                qk(i + LOOK)
            k = keys[i]
            nk, cs = k["nk"], k["cstart"]
            sps = sts.pop(i)
            pt = SL.alloc()
            for (lo, hi, bias_fn) in qblocks:
                lo2 = max(lo, cs)
                if lo2 >= hi:
                    continue
                bap, breg = bias_fn(k)
                act(pt.b[0:nk, lo2:hi], sps.t[0:nk, lo2:hi], AF.Exp, [sps.r(), breg], [pt.r()], bias=bap, scale=0.125)
            PS.release(sps)
            if k["tri"]:
                tt(pt.b[0:nk, cs:cs + nk], pt.b[0:nk, cs:cs + nk], TRI.t[0:nk, 0:nk], ALU.mult, [pt.r(), TRI.r()],
                   [pt.r()])
            mm(acc.t[0:65, cs:Wq], k["lhsT_v"], pt.b[0:nk, cs:Wq], [k["v_reg"], pt.r()] + k.get("v_extra", []),
               [acc.r()], start=(i == 0), stop=(i == n - 1))
            SL.release(pt)
        rd = SL.alloc()
        P.op("dve", lambda e: e.reciprocal(out=rd.f[64:65, 0:Wq], in_=acc.t[64:65, 0:Wq]), [acc.r()], [rd.r()])
        bc = PS.alloc()
        mm(bc.t[0:64, 0:Wq], ONESF.t[64:65, 0:64], rd.f[64:65, 0:Wq], [ONESF.r(), rd.r()], [bc.r()])
        rb = SL.alloc()
        act(rb.f[0:64, 0:Wq], bc.t[0:64, 0:Wq], AF.Copy, [bc.r()], [rb.r()])
        PS.release(bc)
        tt(att_dst, acc.t[0:64, 0:Wq], rb.f[0:64, 0:Wq], ALU.mult, [acc.r(), rb.r()], [att_reg])
        PS.release(acc)
        SL.release(rd, rb)

    def layer_norm(xres, l, which, W=512):
        psm = PS.alloc()
        psq = PS.alloc()
        for c in range(8):
            t = SL.alloc()
            act(t.b[:, 0:W], xres[c].f[:, 0:W], AF.Copy, [xres[c].r()], [t.r()])
            act(t.b[:, 512:512 + W], xres[c].f[:, 0:W], AF.Square, [xres[c].r()], [t.r()])
            mm(psm.t[:, 0:W], ONESB.t[:, :], t.b[:, 0:W], [ONESB.r(), t.r()], [psm.r()], start=(c == 0), stop=(c == 7))
            mm(psq.t[:, 0:W], ONESB.t[:, :], t.b[:, 512:512 + W], [ONESB.r(), t.r()], [psq.r()], start=(c == 0),
               stop=(c == 7))
            SL.release(t)
        mean = SL.alloc()
        msq = SL.alloc()
        act(mean.f[:, 0:W], psm.t[:, 0:W], AF.Copy, [psm.r()], [mean.r()])
        act(msq.f[:, 0:W], psm.t[:, 0:W], AF.Square, [psm.r()], [msq.r()])
        PS.release(psm)
        tt(msq.f[:, 0:W], psq.t[:, 0:W], msq.f[:, 0:W], ALU.subtract, [psq.r(), msq.r()], [msq.r()])
        PS.release(psq)
        ts(msq.f[:, 0:W], msq.f[:, 0:W], 0.0, EPS, ALU.max, ALU.add, [msq.r()], [msq.r()])
        act(msq.f[:, 0:W], msq.f[:, 0:W], AF.Ln, [msq.r()], [msq.r()])
        act(msq.f[:, 0:W], msq.f[:, 0:W], AF.Exp, [msq.r()], [msq.r()], scale=-0.5)
        xbf = [SL.alloc() for _ in range(4)]
        for c in range(8):
            t = SL.alloc()
            tt(t.f[:, 0:W], xres[c].f[:, 0:W], mean.f[:, 0:W], ALU.subtract, [xres[c].r(), mean.r()], [t.r()])
            tt(t.f[:, 0:W], t.f[:, 0:W], msq.f[:, 0:W], ALU.mult, [t.r(), msq.r()], [t.r()])
            g = LNP.t[:, l, which, 0, c:c + 1]
            bb = LNP.t[:, l, which, 1, c:c + 1]
            act(xres[c].f[:, 0:W], t.f[:, 0:W], AF.Identity, [t.r(), LNP.r()], [xres[c].r()], bias=bb, scale=g)
            act(xbf_ap(xbf, c, 0, W), t.f[:, 0:W], AF.Identity, [t.r(), LNP.r()], [xbf[c // 2].r()], bias=bb, scale=g)
            SL.release(t)
        SL.release(mean, msq)
        return xbf

    def conv_seg(ps, T, l, ch, kind, a, b, carry_in, save_carry):
        wv = lambda j: FCW.t[:, l, ch, j:j + 1]
        act(T.f[:, a:b], ps.t[:, a:b], AF.Identity, [ps.r(), FCW.r()], [T.r()], bias=wv(3), scale=wv(2))
        if kind == "samp":
            p3 = ps.t[:, a:b].rearrange("p (b t) -> p b t", b=4)
            t3 = T.f[:, a:b].rearrange("p (b t) -> p b t", b=4)
            stt(t3[:, :, 1:64], p3[:, :, 0:63], wv(1), t3[:, :, 1:64], ALU.mult, ALU.add, [ps.r(), FCW.r(), T.r()], [T.r()])
            stt(t3[:, :, 2:64], p3[:, :, 0:62], wv(0), t3[:, :, 2:64], ALU.mult, ALU.add, [ps.r(), FCW.r(), T.r()], [T.r()])
            s3 = SFC.t[:, l, ch, :, :]
            stt(t3[:, :, 0:2], s3[:, :, 0:2], wv(0), t3[:, :, 0:2], ALU.mult, ALU.add, [SFC.r(), FCW.r(), T.r()], [T.r()])
            stt(t3[:, :, 0:1], s3[:, :, 1:2], wv(1), t3[:, :, 0:1], ALU.mult, ALU.add, [SFC.r(), FCW.r(), T.r()], [T.r()])
            cp(HCS.t[:, l, ch, :, :], p3[:, :, 62:64], [ps.r()], [HCS.r(l)])
        else:
            stt(T.f[:, a + 1:b], ps.t[:, a:b - 1], wv(1), T.f[:, a + 1:b], ALU.mult, ALU.add, [ps.r(), FCW.r(), T.r()], [T.r()])
            stt(T.f[:, a + 2:b], ps.t[:, a:b - 2], wv(0), T.f[:, a + 2:b], ALU.mult, ALU.add, [ps.r(), FCW.r(), T.r()], [T.r()])
            if carry_in:
                hc = HC.t[:, l, ch, :]
                stt(T.f[:, a:a + 2], hc[:, 0:2], wv(0), T.f[:, a:a + 2], ALU.mult, ALU.add, [HC.r(l), FCW.r(), T.r()], [T.r()])
                stt(T.f[:, a:a + 1], hc[:, 1:2], wv(1), T.f[:, a:a + 1], ALU.mult, ALU.add, [HC.r(l), FCW.r(), T.r()], [T.r()])
            if save_carry:
                cp(HC.t[:, l, ch, :], ps.t[:, b - 2:b], [ps.r()], [HC.r(l)])

    def ffn(tname, xres, xbf, l, segs):
        M = [SL.alloc() for _ in range(11)]
        for g in range(11):
            wv, wreg = ws_get("%s_UP%d_%d" % (tname, l, g))
            for cc in range(2):
                c = g * 2 + cc
                tg = SL.alloc()
                tu = SL.alloc()
                for gu, T in ((0, tg), (1, tu)):
                    ps = PS.alloc()
                    mmg(ps.t[:, 0:512], [(wv[:, kc, gu, cc * 128:(cc + 1) * 128], xbf_ap(xbf, kc)) for kc in range(8)],
                        xr(xbf) + [wreg], [ps.r()])
                    for (kind, a, b, cin, csave) in segs:
                        conv_seg(ps, T, l, gu * 22 + c, kind, a, b, cin, csave)
                    PS.release(ps)
                act(tg.f[:, 0:512], tg.f[:, 0:512], AF.Gelu_apprx_tanh, [tg.r()], [tg.r()])
                tt(M[c // 2].b[:, (c % 2) * 512:(c % 2) * 512 + 512], tg.f[:, 0:512], tu.f[:, 0:512], ALU.mult,
                   [tg.r(), tu.r()], [M[c // 2].r()])
                SL.release(tg, tu)
            ws_done("%s_UP%d_%d" % (tname, l, g))
        SL.release(xbf)
        for o in range(8):
            wv, wreg = ws_get("%s_DN%d_%d" % (tname, l, o))
            ps = PS.alloc()
            mmg(ps.t[:, 0:512], [(wv[:, c, :], M[c // 2].b[:, (c % 2) * 512:(c % 2) * 512 + 512]) for c in range(22)],
                [m.r() for m in M] + [wreg], [ps.r()])
            ws_done("%s_DN%d_%d" % (tname, l, o))
            stt(xres[o].f[:, 0:512], xres[o].f[:, 0:512], ALPHA, ps.t[:, 0:512], ALU.mult, ALU.add,
                [xres[o].r(), ps.r()], [xres[o].r()])
            PS.release(ps)
        SL.release(M)

    light = [(0, 512), (512, 512), (1024, 512), (1536, 256)][:n_light_tiles]
    nxt_xbf = load_xbf(d_xl, 0, 512) if light else None
    for li, (k0, W) in enumerate(light):
        tn = "L%d" % li
        xbf = nxt_xbf
        if li + 1 < len(light):
            nxt_xbf = load_xbf(d_xl, light[li + 1][0], light[li + 1][1])
        jt0 = k0 // 128
        nst = W // 128
        wv, wreg = ws_get(tn + "_K")
        for p in range(4):
            ps = PS.alloc()
            mmg(ps.t[:, 0:W], [(wv[:, kc, p * 128:(p + 1) * 128], xbf_ap(xbf, kc, 0, W)) for kc in range(8)],
                xr(xbf) + [wreg], [ps.r()])
            act(KT.t[:, p, k0:k0 + W], ps.t[:, 0:W], AF.Copy, [ps.r()], [KT.r(jt0 + i) for i in range(nst)])
            PS.release(ps)
        ws_done(tn + "_K")
        wv, wreg = ws_get(tn + "_V")
        for s_ in range(nst):
            ps = PS.alloc()
            mmg(ps.t[:, 0:512], [(xbf_ap(xbf, kc, s_ * 128, s_ * 128 + 128), wv[:, kc, :]) for kc in range(8)],
                xr(xbf) + [wreg], [ps.r()])
            act(VA.t[:, jt0 + s_, :, 0:64], ps.t[:, 0:512].rearrange("p (h d) -> p h d", h=8), AF.Copy, [ps.r()],
                [VA.r(jt0 + s_)])
            PS.release(ps)
        ws_done(tn + "_V")
        lf, cum = forget_gate(xbf, W, [("light", 0, W)], 0, lf_out=False)
        ck_from_cum(cum, 0, W, jt0, add_pen=True)
        SL.release(lf, cum)
        wv, wreg = ws_get(tn + "_XB")
        for c in range(4):
            ps = PS.alloc()
            mmg(ps.t[:, 0:W], [(wv[:, kc, c * 128:(c + 1) * 128], xbf_ap(xbf, kc, 0, W)) for kc in range(8)],
                xr(xbf) + [wreg], [ps.r()])
            act(XBH.t[:, c, 3:3 + W], ps.t[:, 0:W], AF.Copy, [ps.r()], [XBH.r()])
            PS.release(ps)
        ws_done(tn + "_XB")
        SL.release(xbf)
        rglru_seg("light", 0, W, xbh_src(0), None, None)
        cp(XBH.t[:, :, 0:3], XBH.t[:, :, W:W + 3], [XBH.r()], [XBH.r()])

    _cur_tile = [0]

    def stage(k):
        if stop_stage is None:
            return
        if isinstance(stop_stage, tuple):
            if (_cur_tile[0], k) == stop_stage:
                raise _Stop()
        elif k == stop_stage:
            raise _Stop()

    nxt_xbf = load_xbf(d_xo, 0, 512) if n_own_tiles else None
    try:
      for ti in range(n_own_tiles):
          tn = "O%d" % ti
          _cur_tile[0] = ti
          stage(100)
          if DEBUG_ACTPROBE and ti >= 1:
              act(CST.t[:, 3:4], CST.t[:, 2:3], AF.Copy, [CST.r()], [CST.r()])
          c0 = ti * 512
          first = ti == 0
          last = ti == 4
          xbf = nxt_xbf
          if first:
              psegs = [("halo", 0, 256, 1792)]
          else:
              psegs = [("own", 0, 512, 2048 + (ti - 1) * 512)]

          wv, wreg = ws_get(tn + "_K")
          knew = SL.alloc() if first else None
          for p in range(4):
              ps = PS.alloc()
              mmg(ps.t[:, 0:512], [(wv[:, kc, p * 128:(p + 1) * 128], xbf_ap(xbf, kc)) for kc in range(8)],
                  xr(xbf) + [wreg], [ps.r()])
              for (kind, a, b, k0) in psegs:
                  if DEBUG_SKIPKT and ti >= 1:
                      continue
                  if kind == "own":
                      for hf in range(2):
                          cp(KT2.t[:, p, k0 - 2048 + hf * 256:k0 - 2048 + hf * 256 + 256],
                             ps.t[:, a + hf * 256:a + hf * 256 + 256],
                             [ps.r()], [KT2.r(k0 // 128 + hf * 2), KT2.r(k0 // 128 + hf * 2 + 1)])
                  else:
                      act(KT.t[:, p, k0:k0 + (b - a)], ps.t[:, a:b], AF.Copy, [ps.r()],
                          [KT.r(k0 // 128 + i) for i in range((b - a) // 128)])
              if first:
                  act(knew.b[:, p * 256:(p + 1) * 256], ps.t[:, 256:512], AF.Copy, [ps.r()], [knew.r()])
              ko = SL.alloc()
              if not (DEBUG_SKIPKO and ti >= 1):
                  cp(ko.f[:, 0:512], ps.t[:, 0:512], [ps.r()], [ko.r()])
              PS.release(ps)
              st(o_k[p, :, c0:c0 + 512], ko.f[:, 0:512], [ko.r()], slot_sem(ko))
              SL.release(ko)
          ws_done(tn + "_K")
          stage(101)

          wv, wreg = ws_get(tn + "_V")
          for s_ in range(4):
              ps = PS.alloc()
              mmg(ps.t[:, 0:512], [(xbf_ap(xbf, kc, s_ * 128, s_ * 128 + 128), wv[:, kc, :]) for kc in range(8)],
                  xr(xbf) + [wreg], [ps.r()])
              if not (first and s_ >= 2):
                  k0 = psegs[0][3] + s_ * 128
                  if first:
                      act(VA.t[:, k0 // 128, :, 0:64], ps.t[:, 0:512].rearrange("p (h d) -> p h d", h=8), AF.Copy,
                          [ps.r()], [VA.r(k0 // 128)])
                  else:
                      cp(VA.t[:, k0 // 128, :, 0:64], ps.t[:, 0:512].rearrange("p (h d) -> p h d", h=8),
                         [ps.r()], [VA.r(k0 // 128)])
              vo = SL.alloc()
              cp(vo.f[:, 0:512], ps.t[:, 0:512], [ps.r()], [vo.r()])
              PS.release(ps)
              st(o_v[c0 + s_ * 128:c0 + s_ * 128 + 128, :], vo.f[:, 0:512], [vo.r()], slot_sem(vo))
              SL.release(vo)
          if first:
              for bb in range(4):
                  ps = PS.alloc()
                  mmg(ps.t[0:64, 0:512], [(xbf_ap(xbf, kc, 256 + bb * 64, 320 + bb * 64), wv[:, kc, :]) for kc in range(8)],
                      xr(xbf) + [wreg], [ps.r()])
                  act(VNEW.t[:, bb, :, 0:64], ps.t[0:64, 0:512].rearrange("p (h d) -> p h d", h=8), AF.Copy,
                      [ps.r()], [VNEW.r(bb)])
                  PS.release(ps)
          ws_done(tn + "_V")
          stage(102)

          if first:
              fsegs = [("halo", 0, 256), ("samp", 256, 512)]
          else:
              fsegs = [("own", 0, 512)]
          lf, cum = forget_gate(xbf, 512, fsegs, c0)
          SL.release(lf)
          cref = SL.alloc()
          ckn = None
          if first:
              ck_from_cum(cum, 0, 256, 14, add_pen=False)
              qstarts = [0]
          else:
              ck_from_cum(cum, 0, 512, psegs[0][3] // 128, add_pen=False)
              qstarts = [0, 256]
          ps = PS.alloc()
          for qi, q0 in enumerate(qstarts):
              tmp = SL.alloc()
              ts(tmp.f[0:8, 0:128], ONESF.t[0:8, 0:128], cum.f[0:8, q0:q0 + 1], None, ALU.mult, None,
                 [ONESF.r(), cum.r()], [tmp.r()])
              mm(ps.t[:, qi * 8:(qi + 1) * 8], tmp.f[0:8, 0:128], I8.t[:, :], [tmp.r(), I8.r()], [ps.r()])
              SL.release(tmp)
          cp(cref.f[:, 0:8 * len(qstarts)], ps.t[:, 0:8 * len(qstarts)], [ps.r()], [cref.r()])
          PS.release(ps)
          if first:
              ckn = SL.alloc()
              ps = PS.alloc()
              for bb in range(4):
                  mm(ps.t[0:64, bb * 8:(bb + 1) * 8], cum.f[0:8, 256 + bb * 64:320 + bb * 64], I8.t[:, :],
                     [cum.r(), I8.r()], [ps.r()])
              ts(ckn.f[0:64, 0:32], ps.t[0:64, 0:32], -1.0, None, ALU.mult, None, [ps.r()], [ckn.r()])
              PS.release(ps)
          SL.release(cum)

          stage(103)
          wv, wreg = ws_get(tn + "_Q")
          qt = [SL.alloc() for _ in range(2)]
          for p in range(4):
              ps = PS.alloc()
              mmg(ps.t[:, 0:512], [(wv[:, kc, p * 128:(p + 1) * 128], xbf_ap(xbf, kc)) for kc in range(8)],
                  xr(xbf) + [wreg], [ps.r()])
              if first:
                  act(qt[p // 2].b[:, (p % 2) * 512:(p % 2) * 512 + 512], ps.t[:, 0:512], AF.Copy, [ps.r()],
                      [qt[p // 2].r()])
              else:
                  cp(qt[p // 2].b[:, (p % 2) * 512:(p % 2) * 512 + 512], ps.t[:, 0:512], [ps.r()], [qt[p // 2].r()])
              PS.release(ps)
          ws_done(tn + "_Q")

          stage(104)
          wv, wreg = ws_get(tn + "_GB")
          gb = [SL.alloc() for _ in range(4)]
          for c in range(4):
              ps = PS.alloc()
              mmg(ps.t[:, 0:512], [(wv[:, kc, c * 128:(c + 1) * 128], xbf_ap(xbf, kc)) for kc in range(8)],
                  xr(xbf) + [wreg], [ps.r()])
              act(gb[c].f[:, 0:512], ps.t[:, 0:512], AF.Gelu_apprx_tanh, [ps.r()], [gb[c].r()])
              PS.release(ps)
          ws_done(tn + "_GB")

          stage(105)
          wv, wreg = ws_get(tn + "_XB")
          xbs = None
          if first:
              xbs = [SL.alloc() for _ in range(4)]
              for c in range(4):
                  ld("sp", xbs[c].f[:, 0:268].rearrange("p (b t) -> p b t", b=4)[:, :, 0:3], d_rcs[:, c, :, :],
                     [xbs[c].r()], slot_sem(xbs[c]))
          for c in range(4):
              ps = PS.alloc()
              mmg(ps.t[:, 0:512], [(wv[:, kc, c * 128:(c + 1) * 128], xbf_ap(xbf, kc)) for kc in range(8)],
                  xr(xbf) + [wreg], [ps.r()])
              if first:
                  act(XBH.t[:, c, 3:3 + 256], ps.t[:, 0:256], AF.Copy, [ps.r()], [XBH.r()])
                  act(xbs[c].f[:, 0:268].rearrange("p (b t) -> p b t", b=4)[:, :, 3:67],
                      ps.t[:, 256:512].rearrange("p (b t) -> p b t", b=4), AF.Copy, [ps.r()], [xbs[c].r()])
              else:
                  cp(XBH.t[:, c, 3:3 + 512], ps.t[:, 0:512], [ps.r()], [XBH.r()])
              PS.release(ps)
          ws_done(tn + "_XB")
          SL.release(xbf)

          stage(1)
          hg = [SL.alloc() for _ in range(2)]
          if first:
              rglru_seg("halo", 0, 256, xbh_src(0), gb, hg)
              cp(XBH.t[:, :, 0:3], XBH.t[:, :, 256:259], [XBH.r()], [XBH.r()])
              ts(HCAR.t[:, :], HCAR.t[:, :], FLAG.t[:, 0:1], None, ALU.mult, None, [HCAR.r(), FLAG.r()], [HCAR.r()])

              def xbs_src(c):
                  return xbs[c].f[:, 0:268].rearrange("p (b t) -> p b t", b=4), xbs[c].r()
              stage(13)
              rglru_seg("samp", 256, 512, xbs_src, gb, hg)
              stage(16)
              for c in range(4):
                  st(o_rcs[:, c, :, :], xbs[c].f[:, 0:268].rearrange("p (b t) -> p b t", b=4)[:, :, 64:67],
                     [xbs[c].r()], slot_sem(xbs[c]))
              st(o_rhs[:, :, :], HSL.t[:, :, :], [HSL.r()], osem())
              SL.release(xbs)
          else:
              rglru_seg("own", 0, 512, xbh_src(0), gb, hg)
              if last:
                  st(o_rc[:, :, :], XBH.t[:, :, 512:515], [XBH.r()], osem())
                  st(o_rh[:, :], HCAR.t[:, :], [HCAR.r()], osem())
              else:
                  cp(XBH.t[:, :, 0:3], XBH.t[:, :, 512:515], [XBH.r()], [XBH.r()])
          SL.release(gb)

          stage(2)
          att = [SL.alloc() for _ in range(4)]
          for (kind, a, b, k0) in psegs:
              Wq = b - a
              jd0 = k0 // 128
              nj = jd0 + Wq // 128
              nqb = Wq // 256
              for h in range(8):
                  p, r0 = h // 2, (h % 2) * 64
                  bias = SL.alloc()
                  for qi in (nqb - 1,):
                      ts(bias.f[:, qi * 32:qi * 32 + nj], CK.t[:, 0:nj, h], -1.0, cref.f[:, qi * 8 + h:qi * 8 + h + 1],
                         ALU.mult, ALU.add, [CK.r(j) for j in range(nj)] + [cref.r()], [bias.r()])
                  keys = []
                  for j in range(nj):
                      dd = j - jd0
                      ktt = KT if j < 16 else KT2
                      jc = j if j < 16 else j - 16
                      keys.append(dict(lhsT_k=ktt.t[r0:r0 + 64, p, jc * 128:(jc + 1) * 128], k_reg=ktt.r(j),
                                       lhsT_v=VA.t[:, j, h, 0:65], v_reg=VA.r(j), v_extra=[VA.r("ones")], nk=128,
                                       cstart=max(0, dd * 128), tri=dd >= 0, j=j))
                  qbl = []
                  for qi in (nqb - 1,):
                      def bf(k, qi=qi, bias=bias):
                          return bias.f[:, qi * 32 + k["j"]:qi * 32 + k["j"] + 1], bias.r()
                      qbl.append((0, Wq, bf))

                  def qap(lo, hi, p=p, r0=r0, a=a):
                      return qt[p // 2].b[r0:r0 + 64, (p % 2) * 512 + a + lo:(p % 2) * 512 + a + hi], qt[p // 2].r()
                  attend(qap, Wq, qbl, keys, att[h // 2].b[0:64, (h % 2) * 512 + a:(h % 2) * 512 + b], att[h // 2].r())
                  SL.release(bias)
          stage(3)
          if first:
              ts(CK.t[:, 14:16, :], CK.t[:, 14:16, :], PEN.t[:, 0:1], None, ALU.add, None, [CK.r(14), CK.r(15), PEN.r()],
                 [CK.r(14), CK.r(15)])
              for bb in range(4):
                  lfr = [SL.alloc() for _ in range(4)]
                  for i in range(4):
                      ld("sp", lfr[i].f[0:8, 0:512], d_clf[bb, :, i * 512:(i + 1) * 512], [lfr[i].r()], slot_sem(lfr[i]))
                  cks = SL.alloc()
                  ps = PS.alloc()
                  for i in range(4):
                      rr = SL.alloc()
                      init = 0.0 if i == 0 else prev.f[0:8, 511:512]
                      rds = [lfr[i].r(), ONESF.r()] + ([] if i == 0 else [prev.r()])
                      P.op("dve", lambda e, o_=rr.f[0:8, 0:512], d1=lfr[i].f[0:8, 0:512], init=init: e.tensor_tensor_scan(
                          out=o_, data0=ONESF.t[0:8, 0:512], data1=d1, initial=init,
                          op0=ALU.mult, op1=ALU.add), rds, [rr.r()])
                      tt(lfr[i].f[0:8, 0:512], rr.f[0:8, 0:512], lfr[i].f[0:8, 0:512], ALU.subtract, [rr.r(), lfr[i].r()],
                         [lfr[i].r()])
                      for jj in range(4):
                          j = i * 4 + jj
                          mm(ps.t[:, j * 8:(j + 1) * 8], lfr[i].f[0:8, jj * 128:(jj + 1) * 128], I8.t[:, :],
                             [lfr[i].r(), I8.r()], [ps.r()])
                      if i > 0:
                          SL.release(prev)
                      prev = rr
                  SL.release(prev)
                  cp(cks.f[:, 0:128], ps.t[:, 0:128], [ps.r()], [cks.r()])
                  PS.release(ps)
                  SL.release(lfr)
                  for p in range(4):
                      kts = [SL.alloc() for _ in range(2)]
                      for i in range(2):
                          ld("pool", kts[i].b[:, 0:1024], d_ckT[bb, p, :, i * 1024:(i + 1) * 1024], [kts[i].r()],
                             slot_sem(kts[i], True))
                      vst = [SL.alloc() for _ in range(2)]
                      vas = VAS[0]
                      for i in range(2):
                          ld("pool", vst[i].b[:, 0:1024].rearrange("p (j n) -> p j n", j=8),
                             d_cvr[bb, p, :, i * 8:(i + 1) * 8, :], [vst[i].r()], slot_sem(vst[i], True))
                          cp(vas.t[:, i * 8:(i + 1) * 8, :, 0:64],
                             vst[i].b[:, 0:1024].rearrange("p (j h d) -> p j h d", j=8, h=2), [vst[i].r()], [vas.r()],
                             en="dve")
                      SL.release(vst)
                      for hh_ in range(2):
                          h = 2 * p + hh_
                          r0 = hh_ * 64
                          keys = []
                          for j in range(16):
                              keys.append(dict(lhsT_k=kts[j // 8].b[r0:r0 + 64, (j % 8) * 128:(j % 8 + 1) * 128],
                                               k_reg=kts[j // 8].r(), lhsT_v=vas.t[:, j, hh_, 0:65], v_reg=vas.r(),
                                               v_extra=[vas.r("ones")], nk=128, cstart=0, tri=False, j=j))
                          keys.append(dict(lhsT_k=knew.b[r0:r0 + 64, p * 256 + bb * 64:p * 256 + bb * 64 + 64],
                                           k_reg=knew.r(), lhsT_v=VNEW.t[:, bb, h, 0:65], v_reg=VNEW.r(bb),
                                           v_extra=[VNEW.r("ones")], nk=64, cstart=0, tri=True, j=16))

                          def bf(k, h=h, bb=bb, cks=cks):
                              if k["j"] < 16:
                                  return cks.f[:, k["j"] * 8 + h:k["j"] * 8 + h + 1], cks.r()
                              return ckn.f[0:64, bb * 8 + h:bb * 8 + h + 1], ckn.r()

                          def qap(lo, hi, p=p, r0=r0, bb=bb):
                              o = (p % 2) * 512 + 256 + bb * 64
                              return qt[p // 2].b[r0:r0 + 64, o + lo:o + hi], qt[p // 2].r()
                          o = (h % 2) * 512 + 256 + bb * 64
                          attend(qap, 64, [(0, 64, bf)], keys, att[h // 2].b[0:64, o:o + 64], att[h // 2].r())
                      SL.release(kts)
                  SL.release(cks)
              SL.release(knew, ckn)
          SL.release(qt, cref)

          stage(4)
          xres = [SL.alloc() for _ in range(8)]
          for c in range(8):
              ld("sp", xres[c].f[:, 0:512], d_xo[c, :, c0:c0 + 512], [xres[c].r()], slot_sem(xres[c]))
          wa0, ra0 = ws_get(tn + "_WOA0")
          wa1, ra1 = ws_get(tn + "_WOA1")
          wr_, rr_ = ws_get(tn + "_WOR")
          for o in range(8):
              ps = PS.alloc()
              pairs = []
              for h in range(8):
                  wa = wa0 if h < 4 else wa1
                  pairs.append((wa[:, h % 4, o * 128:(o + 1) * 128], att[h // 2].b[0:64, (h % 2) * 512:(h % 2) * 512 + 512]))
              for c in range(4):
                  pairs.append((wr_[:, c, o * 128:(o + 1) * 128], hg[c // 2].b[:, (c % 2) * 512:(c % 2) * 512 + 512]))
              mmg(ps.t[:, 0:512], pairs, [ra0, ra1, rr_] + [x.r() for x in att] + [x.r() for x in hg], [ps.r()])
              stt(xres[o].f[:, 0:512], xres[o].f[:, 0:512], ALPHA, ps.t[:, 0:512], ALU.mult, ALU.add,
                  [xres[o].r(), ps.r()], [xres[o].r()])
              PS.release(ps)
          ws_done(tn + "_WOA0")
          ws_done(tn + "_WOA1")
          ws_done(tn + "_WOR")
          SL.release(att, hg)

          if first:
              csegs = [("halo", 0, 256, False, True), ("samp", 256, 512, False, False)]
          else:
              csegs = [("own", 0, 512, True, True)]

          stage(5)
          xbf = layer_norm(xres, 0, 0)
          stage(6)
          ffn(tn, xres, xbf, 0, csegs)
          stage(7)
          if first:
              ts(HC.t[:, 0, :, :], HC.t[:, 0, :, :], FLAG.t[:, 0:1], None, ALU.mult, None, [HC.r(0), FLAG.r()], [HC.r(0)])
          xbf = layer_norm(xres, 0, 1)

          stage(8)
          U = [SL.alloc() for _ in range(4)]
          for half in range(2):
              wv, wreg = ws_get("%s_WIO%d" % (tn, half))
              for cc in range(4):
                  c = half * 4 + cc
                  ps = PS.alloc()
                  mmg(ps.t[:, 0:512], [(wv[:, kc, cc * 128:(cc + 1) * 128], xbf_ap(xbf, kc)) for kc in range(8)],
                      xr(xbf) + [wreg], [ps.r()])
                  act(U[c // 2].b[:, (c % 2) * 512:(c % 2) * 512 + 512], ps.t[:, 0:512], AF.Gelu_apprx_tanh, [ps.r()],
                      [U[c // 2].r()])
                  PS.release(ps)
              ws_done("%s_WIO%d" % (tn, half))
          VT = [[SL.alloc(), SL.alloc()] for _ in range(4)]
          for vh in range(2):
              wv, wreg = ws_get("%s_WIO%d" % (tn, 2 + vh))
              for s_ in range(4):
                  ps = PS.alloc()
                  mmg(ps.t[:, 0:512], [(xbf_ap(xbf, kc, s_ * 128, s_ * 128 + 128), wv[:, kc, :]) for kc in range(8)],
                      xr(xbf) + [wreg], [ps.r()])
                  act(VT[s_][vh].f[:, 0:512], ps.t[:, 0:512], AF.Gelu_apprx_tanh, [ps.r()], [VT[s_][vh].r()])
                  PS.release(ps)
              ws_done("%s_WIO%d" % (tn, 2 + vh))
          SL.release(xbf)
          VNB = []
          for s_ in range(4):
              stt_ = SL.alloc()
              for vh in range(2):
                  P.op("dve", lambda e, o_=stt_.f[:, vh * 6:vh * 6 + 6], i_=VT[s_][vh].f[:, 0:512]: e.bn_stats(out=o_, in_=i_),
                       [VT[s_][vh].r()], [stt_.r()])
              P.op("dve", lambda e, stt_=stt_: e.bn_aggr(out=stt_.f[:, 16:18], in_=stt_.f[:, 0:12]), [stt_.r()], [stt_.r()])
              ts(stt_.f[:, 17:18], stt_.f[:, 17:18], 0.0, EPS, ALU.max, ALU.add, [stt_.r()], [stt_.r()])
              act(stt_.f[:, 17:18], stt_.f[:, 17:18], AF.Ln, [stt_.r()], [stt_.r()])
              act(stt_.f[:, 17:18], stt_.f[:, 17:18], AF.Exp, [stt_.r()], [stt_.r()], scale=-0.5)
              vb = SL.alloc()
              VNB.append(vb)
              for vh in range(2):
                  v = VT[s_][vh]
                  ts(v.f[:, 0:512], v.f[:, 0:512], stt_.f[:, 16:17], stt_.f[:, 17:18], ALU.subtract, ALU.mult,
                     [v.r(), stt_.r()], [v.r()])
                  tt(v.f[:, 0:512], v.f[:, 0:512], SGG.t[:, vh * 512:(vh + 1) * 512], ALU.mult, [v.r(), SGG.r()], [v.r()])
                  tt(v.f[:, 0:512], v.f[:, 0:512], SGBB.t[:, vh * 512:(vh + 1) * 512], ALU.add, [v.r(), SGBB.r()], [v.r()])
                  act(vb.b[:, vh * 512:(vh + 1) * 512], v.f[:, 0:512], AF.Copy, [v.r()], [vb.r()])
                  if first and s_ >= 2:
                      st(o_sgv[(s_ - 2) * 128:(s_ - 1) * 128, vh * 512:(vh + 1) * 512], v.f[:, 0:512], [v.r()], slot_sem(v))
              SL.release(stt_, VT[s_])
          GT = [SL.alloc() for _ in range(4)]
          for g in range(8):
              ps = PS.alloc()
              for s_ in range(4):
                  if first and s_ >= 2:
                      for b2 in range(2):
                          bb = (s_ - 2) * 2 + b2
                          r0 = b2 * 64
                          o = 256 + bb * 64
                          mm(ps.t[:, o:o + 64], VNB[s_].b[r0:r0 + 64, g * 128:(g + 1) * 128], WST64.t[r0:r0 + 64, g, :],
                             [VNB[s_].r(), WST64.r()], [ps.r()], start=True, stop=False)
                          mm(ps.t[:, o:o + 64], ONE1B.t[0:1, :], BHI.t[0:1, g, 0:64], [ONE1B.r(), BHI.r()], [ps.r()],
                             start=False, stop=False)
                          mm(ps.t[:, o:o + 64], ONE1B.t[0:1, :], BLO.t[0:1, g, 0:64], [ONE1B.r(), BLO.r()], [ps.r()],
                             start=False, stop=True)
                  else:
                      o = s_ * 128
                      mm(ps.t[:, o:o + 128], VNB[s_].b[:, g * 128:(g + 1) * 128], WST.t[:, g, :], [VNB[s_].r(), WST.r()],
                         [ps.r()], start=True, stop=False)
                      mm(ps.t[:, o:o + 128], ONE1B.t[0:1, :], BHI.t[0:1, g, :], [ONE1B.r(), BHI.r()], [ps.r()],
                         start=False, stop=False)
                      mm(ps.t[:, o:o + 128], ONE1B.t[0:1, :], BLO.t[0:1, g, :], [ONE1B.r(), BLO.r()], [ps.r()],
                         start=False, stop=True)
              tt(GT[g // 2].b[:, (g % 2) * 512:(g % 2) * 512 + 512], ps.t[:, 0:512],
                 U[g // 2].b[:, (g % 2) * 512:(g % 2) * 512 + 512], ALU.mult, [ps.r(), U[g // 2].r()], [GT[g // 2].r()])
              PS.release(ps)
          SL.release(VNB, U)
          for half in range(2):
              wv, wreg = ws_get("%s_WOO%d" % (tn, half))
              for oo in range(4):
                  o = half * 4 + oo
                  ps = PS.alloc()
                  mmg(ps.t[:, 0:512], [(wv[:, kc, oo * 128:(oo + 1) * 128], GT[kc // 2].b[:, (kc % 2) * 512:(kc % 2) * 512 + 512])
                                       for kc in range(8)], [x.r() for x in GT] + [wreg], [ps.r()])
                  stt(xres[o].f[:, 0:512], xres[o].f[:, 0:512], ALPHA, ps.t[:, 0:512], ALU.mult, ALU.add,
                      [xres[o].r(), ps.r()], [xres[o].r()])
                  PS.release(ps)
              ws_done("%s_WOO%d" % (tn, half))
          SL.release(GT)

          xbf = layer_norm(xres, 1, 0)
          ffn(tn, xres, xbf, 1, csegs)
          if ti + 1 < n_own_tiles:
              nxt_xbf = load_xbf(d_xo, c0 + 512, 512)
          if first:
              ts(HC.t[:, 1, :, :], HC.t[:, 1, :, :], FLAG.t[:, 0:1], None, ALU.mult, None, [HC.r(1), FLAG.r()], [HC.r(1)])
              for l in range(2):
                  st(o_fcs[l], HCS.t[:, l, :, :, :], [HCS.r(l)], osem())
          if last:
              for l in range(2):
                  st(o_fcp[l], HC.t[:, l, :, :], [HC.r(l)], osem())
          xbf = layer_norm(xres, 1, 1)
          SL.release(xbf)
          for c in range(8):
              st(o_y[c, :, c0:c0 + 512], xres[c].f[:, 0:512], [xres[c].r()], slot_sem(xres[c]))
          SL.release(xres)
    except _Stop:
        pass

    P.final_wait("sp")

    allsems = [P.sem[e] for e in P.ENGS] + P.dsems
    for s in allsems:
        s.h = es.enter_context(nc.semaphore(s.name))
    block = es.enter_context(nc.Block())

    @block.tensor
    def _(e):
        P.emit("pe", e)

    @block.scalar
    def _(e):
        P.emit("act", e)

    @block.vector
    def _(e):
        P.emit("dve", e)

    @block.gpsimd
    def _(e):
        P.emit("pool", e)

    @block.sync
    def _(e):
        P.emit("sp", e)

    info = dict(arena=ns, arena_low=SL.low, psum_low=PS.low, nsem=len(allsems),
                nops={e: len(P.ops[e]) for e in P.ENGS})
    es.close()
    return nc, info


def _f(x):
    return np.ascontiguousarray(np.asarray(x, dtype=np.float32))


def _shared_inputs(w_in_e, b_f, rg_conv_w, rg_conv_b, rg_wa, rg_ba, rg_wx, rg_bx, rg_lam, w_out_e, w_in_o, sgu_g,
                   sgu_b, sgu_w, sgu_bias, w_out_o, ln_mix_g, ln_mix_b, ln_ffn_g, ln_ffn_b, ffn_w_up, ffn_conv_w,
                   ffn_conv_b, ffn_w_down):
    m = {}
    W = _f(w_in_e)[0]
    cols = [(0, 512), (512, 1024), (1024, 1536), (1544, 2056), (2056, 2568)]
    m["win0"] = _f(np.stack([W[:, a:b].reshape(8, 128, 512).transpose(1, 0, 2) for a, b in cols]))
    m["wf"] = _f(W[:, 1536:1544].reshape(8, 128, 8).transpose(1, 0, 2))
    Wo = _f(w_out_e)[0]
    m["woa"] = _f(Wo[0:512].reshape(2, 4, 64, 1024).transpose(0, 2, 1, 3))
    m["wor"] = _f(Wo[512:1024].reshape(4, 128, 1024).transpose(1, 0, 2))
    Wu = _f(ffn_w_up)
    m["wup"] = _f(Wu.reshape(2, 8, 128, 2, 11, 256).transpose(0, 4, 2, 1, 3, 5))
    Wd = _f(ffn_w_down)
    m["wdn"] = _f(Wd.reshape(2, 22, 128, 8, 128).transpose(0, 3, 2, 1, 4))
    Wi = _f(w_in_o)[0]
    m["wio"] = _f(Wi.reshape(8, 128, 4, 512).transpose(2, 1, 0, 3))
    Woo = _f(w_out_o)[0]
    m["woo"] = _f(Woo.reshape(8, 128, 2, 512).transpose(2, 1, 0, 3))
    rgw = np.zeros((2, 128, 4, 128), np.float32)
    for wi, w in enumerate((_f(rg_wa)[0], _f(rg_wx)[0])):
        for blk in range(8):
            c, hb = blk // 2, blk % 2
            rgw[wi, hb * 64:(hb + 1) * 64, c, hb * 64:(hb + 1) * 64] = w[blk]
    m["rgw"] = rgw
    sw = _f(sgu_w)[0]
    wst = _f(sw.transpose(2, 0, 1))
    m["wst"] = wst
    m["wst64"] = _f(np.concatenate([wst[0:64, :, 0:64], wst[0:64, :, 0:64]], axis=0))
    m["sgb"] = _f(sgu_bias).reshape(1, 8, 128)
    lnp = np.zeros((128, 2, 2, 2, 8), np.float32)
    for l in range(2):
        for wi, (g, b) in enumerate(((ln_mix_g, ln_mix_b), (ln_ffn_g, ln_ffn_b))):
            lnp[:, l, wi, 0, :] = _f(g)[l].reshape(8, 128).T
            lnp[:, l, wi, 1, :] = _f(b)[l].reshape(8, 128).T
    m["lnp"] = lnp
    fcw = np.zeros((128, 2, NHC, 4), np.float32)
    for l in range(2):
        for j in range(3):
            fcw[:, l, :, j] = _f(ffn_conv_w)[l, j].reshape(NHC, 128).T
        fcw[:, l, :, 3] = _f(ffn_conv_b)[l].reshape(NHC, 128).T
    m["fcw"] = fcw
    rgp = np.zeros((128, 4, 8), np.float32)
    for j in range(4):
        rgp[:, :, j] = _f(rg_conv_w)[0, j].reshape(4, 128).T
    for j, v in enumerate((rg_conv_b, rg_ba, rg_bx, rg_lam)):
        rgp[:, :, 4 + j] = _f(v)[0].reshape(4, 128).T
    m["rgp"] = rgp
    m["bfv"] = _f(b_f)[0].reshape(8, 1)
    m["sgg"] = _f(sgu_g).reshape(1, 1024)
    m["sgbb"] = _f(sgu_b).reshape(1, 1024)
    m["tri"] = _f(np.triu(np.ones((128, 128), np.float32)))
    m["i8"] = _f(np.eye(8, dtype=np.float32))
    return m


_CACHE = {}


def kernel(x_prompt, x_sample, cache_k, cache_v, cache_logf, state_rglru_conv, state_rglru_h, state_ffn_conv,
           **weights):
    if "nc" not in _CACHE:
        _CACHE["nc"] = build_program()
    nc, info = _CACHE["nc"]
    shared = _shared_inputs(**weights)
    xp = _f(x_prompt)
    xs = _f(x_sample)
    ck = _f(cache_k)[0]
    cv = _f(cache_v)[0]
    clf = _f(cache_logf)[0]
    rcs = _f(state_rglru_conv)[0]
    rh0 = _f(state_rglru_h)[0]
    sfc = _f(state_ffn_conv)
    in_maps = []
    for c in range(8):
        b, hh = c // 2, c % 2
        win = np.concatenate([np.zeros((2048, D), np.float32), xp[b]], axis=0)[hh * 2048:hh * 2048 + 4096]
        samp = xs[4 * c:4 * c + 4].reshape(256, D)
        own = np.concatenate([win[1792:2048], samp, win[2048:4096]], axis=0)
        m = dict(shared)
        m["xl"] = _f(win[0:1792].T.reshape(8, 128, 1792))
        m["xo"] = _f(own.T.reshape(8, 128, NOWN))
        kb = ck[4 * c:4 * c + 4, ::-1]
        m["ckT"] = _f(kb.reshape(4, 2048, 4, 128).transpose(0, 2, 3, 1))
        vb = cv[4 * c:4 * c + 4, ::-1]
        m["cvr"] = _f(vb.reshape(4, 16, 128, 4, 128).transpose(0, 3, 2, 1, 4))
        m["clf"] = _f(clf[4 * c:4 * c + 4, ::-1].transpose(0, 2, 1))
        m["rcs"] = _f(rcs[4 * c:4 * c + 4].reshape(4, 3, 4, 128).transpose(3, 2, 0, 1))
        m["rh0"] = _f(rh0[4 * c:4 * c + 4].reshape(4, 4, 128).transpose(2, 1, 0))
        m["sfc"] = _f(sfc[:, 4 * c:4 * c + 4].reshape(2, 4, 2, NHC, 128).transpose(0, 4, 3, 1, 2))
        m["flag"] = np.full((128, 1), float(hh), np.float32)
        m["pen"] = np.full((128, 1), 0.0 if hh else PENV, np.float32)
        in_maps.append(m)
    res = run_bass_kernel_spmd(nc, in_maps, core_ids=list(range(8))).results

    y_p = np.zeros((4, 4096, D), np.float32)
    y_s = np.zeros((32, 64, D), np.float32)
    k_p = np.zeros((1, 4, 4096, 8, 64), np.float32)
    v_p = np.zeros((1, 4, 4096, 8, 64), np.float32)
    lf_p = np.zeros((1, 4, 4096, 8), np.float32)
    k_s = np.zeros((1, 32, 64, 8, 64), np.float32)
    v_s = np.zeros((1, 32, 64, 8, 64), np.float32)
    lf_s = np.zeros((1, 32, 64, 8), np.float32)
    rc_p = np.zeros((1, 4, 3, 512), np.float32)
    rh_p = np.zeros((1, 4, 512), np.float32)
    rc_s = np.zeros((1, 32, 3, 512), np.float32)
    rh_s = np.zeros((1, 32, 512), np.float32)
    fc_p = np.zeros((2, 4, 2, 2 * DFF), np.float32)
    fc_s = np.zeros((2, 32, 2, 2 * DFF), np.float32)
    sg_s = np.zeros((1, 32, 64, D), np.float32)
    for c in range(8):
        r = res[c]
        b, hh = c // 2, c % 2
        sl = slice(hh * 2048, hh * 2048 + 2048)
        yT = r["yT"].reshape(D, NOWN)
        y_p[b, sl] = yT[:, 512:].T
        y_s[4 * c:4 * c + 4] = yT[:, 256:512].T.reshape(4, 64, D)
        kT = r["koT"].reshape(512, NOWN)
        k_p[0, b, sl] = kT[:, 512:].T.reshape(2048, 8, 64)
        k_s[0, 4 * c:4 * c + 4] = kT[:, 256:512].T.reshape(4, 64, 8, 64)
        vo = r["vo"]
        v_p[0, b, sl] = vo[512:].reshape(2048, 8, 64)
        v_s[0, 4 * c:4 * c + 4] = vo[256:512].reshape(4, 64, 8, 64)
        lf = r["lfo"]
        lf_p[0, b, sl] = lf[:, 512:].T
        lf_s[0, 4 * c:4 * c + 4] = lf[:, 256:512].T.reshape(4, 64, 8)
        if hh == 1:
            rc_p[0, b] = r["rco"].transpose(2, 1, 0).reshape(3, 512)
            rh_p[0, b] = r["rho"].T.reshape(512)
            fc_p[:, b] = r["fcpo"].transpose(0, 3, 2, 1).reshape(2, 2, 2 * DFF)
        rc_s[0, 4 * c:4 * c + 4] = r["rcso"].transpose(2, 3, 1, 0).reshape(4, 3, 512)
        rh_s[0, 4 * c:4 * c + 4] = r["rhso"].transpose(2, 1, 0).reshape(4, 512)
        fc_s[:, 4 * c:4 * c + 4] = r["fcso"].transpose(0, 3, 4, 2, 1).reshape(2, 4, 2, 2 * DFF)
        sg_s[0, 4 * c:4 * c + 4] = r["sgvo"].reshape(4, 64, D)
    return (y_p, y_s, k_p, v_p, lf_p, k_s, v_s, lf_s, rc_p, rh_p, rc_s, rh_s, fc_p, fc_s, sg_s)


if __name__ == "__main__":
    import time
    t0 = time.time()
    nc, info = build_program()
    print("built in", time.time() - t0, info)
```
